# Optimizing a Trainium2 kernel written in Bass

```python
import math
import jax, jax.numpy as jnp
from jax import lax
import numpy as np

D_MODEL = 1024
BATCH = 2
SEQ = 8192
DEPTH = 4

GRID_W = 64
CTX_LEN = 256
EPS = 1e-6

NA_HEADS = 4
NA_HEAD_DIM = 64
NA_WIN_H = 8
NA_WIN_W = 16
MLA_HEADS = 4
MLA_Q_LORA = 256
MLA_KV_LORA = 128
MLA_NOPE = 64
MLA_ROPE = 32
MLA_V = 64
ROPE_BASE = 10000.0
Q_BLOCK = 128
GDN_HEADS = 4
GDN_DK = 64
GDN_DV = 64
GDN_CONV = 4
GDN_CHUNK = 64
S5_GROUPS = 16
S5_GROUP_CH = 16
S5_STATE = 64
D_FF = 4 * D_MODEL

NA_W = NA_HEADS * NA_HEAD_DIM
MLA_W = MLA_HEADS * MLA_V
GDN_W = GDN_HEADS * GDN_DV
S5_W = S5_GROUPS * S5_GROUP_CH
BRANCH_W = 256
N_BRANCH = 4
IN_SPLITS = (3 * NA_W,
             MLA_Q_LORA,
             MLA_KV_LORA + MLA_ROPE,
             2 * GDN_HEADS * GDN_DK + GDN_W,
             GDN_W,
             2 * GDN_HEADS,
             2 * GDN_HEADS,
             S5_W,
             N_BRANCH * D_MODEL)
D_IN = 768 + 256 + 160 + 768 + 256 + 8 + 8 + 256 + 4096

kernel_name = 'hybrid_gated_branch_diffusion_block'


def rms_norm(x, gain):
    xf = x.astype(jnp.float32)
    y = xf * lax.rsqrt(jnp.mean(xf * xf, axis=-1, keepdims=True) + EPS)
    return (y * gain.astype(jnp.float32)).astype(x.dtype)


def l2_normalize(x):
    xf = x.astype(jnp.float32)
    return xf * lax.rsqrt(jnp.sum(xf * xf, axis=-1, keepdims=True) + EPS)


def split_cols(z, sizes):
    out, off = [], 0
    for n in sizes:
        out.append(z[..., off:off + n])
        off += n
    return out


def to_heads(a, n_heads):
    return a.reshape(a.shape[:2] + (n_heads, a.shape[-1] // n_heads))


def axial_rope(x, rows, cols):
    half = x.shape[-1] // 2
    quarter = half // 2
    inv_freq = ROPE_BASE ** (-jnp.arange(quarter, dtype=jnp.float32) / quarter)

    def rotate(xa, pos):
        ang = pos.astype(jnp.float32)[:, None] * inv_freq[None, :]
        cos = jnp.cos(ang)[None, :, None, :].astype(x.dtype)
        sin = jnp.sin(ang)[None, :, None, :].astype(x.dtype)
        x1, x2 = xa[..., :quarter], xa[..., quarter:]
        return jnp.concatenate([x1 * cos - x2 * sin, x2 * cos + x1 * sin], axis=-1)

    return jnp.concatenate([rotate(x[..., :half], rows), rotate(x[..., half:], cols)], axis=-1)


def dense_attention(q, k, v):
    B_, T, H, dq = q.shape
    scale = dq ** -0.5
    nb = T // Q_BLOCK
    qb = jnp.moveaxis(q.reshape(B_, nb, Q_BLOCK, H, dq), 1, 0)

    def attend(qblk):
        s = jnp.einsum('bqhd,bnhd->bhqn', qblk, k).astype(jnp.float32) * scale
        p = jax.nn.softmax(s, axis=-1).astype(v.dtype)
        return jnp.einsum('bhqn,bnhd->bqhd', p, v)

    o = lax.map(attend, qb)
    return jnp.moveaxis(o, 0, 1).reshape(B_, T, H, v.shape[-1])


def neighborhood_attention(q, k, v, k_ctx, v_ctx, rpb):
    B_, S, H, d = q.shape
    rows = S // GRID_W
    kh = min(NA_WIN_H, rows)
    qg = q.reshape(B_, rows, GRID_W, H, d)
    kg = k.reshape(B_, rows, GRID_W, H, d)
    vg = v.reshape(B_, rows, GRID_W, H, d)
    r = jnp.arange(rows)
    row_start = jnp.clip(r - kh // 2, 0, rows - kh)
    row_idx = row_start[:, None] + jnp.arange(kh)[None, :]
    k_band = kg[:, row_idx]
    v_band = vg[:, row_idx]
    col = jnp.arange(GRID_W)
    col_start = jnp.clip(col - NA_WIN_W // 2, 0, GRID_W - NA_WIN_W)
    col_in = (col[None, :] >= col_start[:, None]) & (col[None, :] < col_start[:, None] + NA_WIN_W)
    scale = d ** -0.5
    s_band = jnp.einsum('brqhd,brikhd->bhrqik', qg, k_band).astype(jnp.float32) * scale
    di = row_idx - r[:, None] + NA_WIN_H - 1
    dj = jnp.clip(col[None, :] - col[:, None] + NA_WIN_W - 1, 0, 2 * NA_WIN_W - 2)
    bias = rpb.astype(jnp.float32)[:, di[:, None, :, None], dj[None, :, None, :]]
    s_band = jnp.where(col_in[:, None, :], s_band + bias[None], -jnp.inf)
    s_ctx = jnp.einsum('brqhd,blhd->bhrql', qg, k_ctx).astype(jnp.float32) * scale
    n_band = kh * GRID_W
    scores = jnp.concatenate([s_band.reshape(B_, H, rows, GRID_W, n_band), s_ctx], axis=-1)
    p = jax.nn.softmax(scores, axis=-1).astype(v.dtype)
    p_band = p[..., :n_band].reshape(B_, H, rows, GRID_W, kh, GRID_W)
    o = (jnp.einsum('bhrqik,brikhd->brqhd', p_band, v_band)
         + jnp.einsum('bhrql,blhd->brqhd', p[..., n_band:], v_ctx))
    return o.reshape(B_, S, H, d)


def mla_queries(cq, q_norm, w_uq, rows, cols):
    B_, T, _ = cq.shape
    q = (rms_norm(cq, q_norm) @ w_uq).reshape(B_, T, MLA_HEADS, MLA_NOPE + MLA_ROPE)
    if rows is None:
        return q
    return jnp.concatenate([q[..., :MLA_NOPE], axial_rope(q[..., MLA_NOPE:], rows, cols)], axis=-1)


def mla_keys_values(ckv, kv_norm, w_ukv, rows, cols):
    B_, T, _ = ckv.shape
    c_kv, k_rope = ckv[..., :MLA_KV_LORA], ckv[..., MLA_KV_LORA:]
    kv = (rms_norm(c_kv, kv_norm) @ w_ukv).reshape(B_, T, MLA_HEADS, MLA_NOPE + MLA_V)
    k_rope = k_rope[:, :, None, :]
    if rows is not None:
        k_rope = axial_rope(k_rope, rows, cols)
    k = jnp.concatenate([kv[..., :MLA_NOPE], jnp.broadcast_to(k_rope, (B_, T, MLA_HEADS, MLA_ROPE))], axis=-1)
    return k, kv[..., MLA_NOPE:]


def short_conv(x, w):
    return lax.conv_general_dilated(
        x, w[:, None, :].astype(x.dtype), window_strides=(1,),
        padding=[(GDN_CONV // 2, GDN_CONV - 1 - GDN_CONV // 2)],
        dimension_numbers=('NWC', 'WIO', 'NWC'), feature_group_count=x.shape[-1])


def gdn_prepare(qkv, a, b, conv_w, a_log, dt_bias):
    B_, T, _ = qkv.shape
    qkv = jax.nn.silu(short_conv(qkv, conv_w))
    q, k, v = split_cols(qkv, (GDN_HEADS * GDN_DK, GDN_HEADS * GDN_DK, GDN_W))
    q = l2_normalize(to_heads(q, GDN_HEADS)) * (GDN_DK ** -0.5)
    k = l2_normalize(to_heads(k, GDN_HEADS))
    v = to_heads(v, GDN_HEADS).astype(jnp.float32)
    a = a.reshape(B_, T, 2, GDN_HEADS).astype(jnp.float32)
    g = -jnp.exp(a_log.astype(jnp.float32)) * jax.nn.softplus(a + dt_bias.astype(jnp.float32))
    beta = jax.nn.sigmoid(b.reshape(B_, T, 2, GDN_HEADS).astype(jnp.float32))
    return q, k, v, g, beta


def to_chunks(a):
    B_, T = a.shape[:2]
    a = jnp.moveaxis(a, 2, 1)
    return a.reshape((B_, a.shape[1], T // GDN_CHUNK, GDN_CHUNK) + a.shape[3:])


def gated_delta_chunked(q, k, v, g, beta, s0, with_output):
    B_, T, H, _ = q.shape
    q, k, v, g, beta = [to_chunks(t.astype(jnp.float32)) for t in (q, k, v, g, beta)]
    dv = v.shape[-1]
    gc = jnp.cumsum(g, axis=-1)
    idx = jnp.arange(GDN_CHUNK)
    incl = idx[:, None] >= idx[None, :]
    strict = idx[:, None] > idx[None, :]
    dec_incl = jnp.exp(jnp.where(incl, gc[..., :, None] - gc[..., None, :], -jnp.inf))
    dec_strict = jnp.where(strict, dec_incl, 0.0)
    kk = jnp.einsum('bhnid,bhnjd->bhnij', k, k)
    a_mat = jnp.eye(GDN_CHUNK, dtype=jnp.float32) + beta[..., :, None] * kk * dec_strict
    rhs = jnp.concatenate([v * beta[..., None], k * (beta * jnp.exp(gc))[..., None]], axis=-1)
    sol = lax.linalg.triangular_solve(a_mat, rhs, left_side=True, lower=True)
    u_val, k_cum = sol[..., :dv], sol[..., dv:]
    k_tail = k * jnp.exp(gc[..., -1:] - gc)[..., None]
    g_last = jnp.exp(gc[..., -1])
    xs = (u_val, k_cum, k_tail, g_last)
    if with_output:
        qk = jnp.einsum('bhnid,bhnjd->bhnij', q, k) * dec_incl
        xs = xs + (q * jnp.exp(gc)[..., None], qk)
    xs = tuple(jnp.moveaxis(t, 2, 0) for t in xs)

    def step(S, inp):
        u, kc, kt, gl = inp[:4]
        v_new = u - jnp.einsum('bhcd,bhde->bhce', kc, S)
        S_next = S * gl[..., None, None] + jnp.einsum('bhcd,bhce->bhde', kt, v_new)
        if not with_output:
            return S_next, None
        qd, qk_c = inp[4], inp[5]
        o = jnp.einsum('bhcd,bhde->bhce', qd, S) + jnp.einsum('bhij,bhje->bhie', qk_c, v_new)
        return S_next, o

    s_fin, o = lax.scan(step, s0, xs)
    if not with_output:
        return None, s_fin
    return o.transpose(1, 0, 3, 2, 4).reshape(B_, T, H, dv), s_fin


def gdn_bidirectional(q, k, v, g, beta, s0_fwd, s0_bwd, with_output):
    fl = lambda t: jnp.flip(t, axis=1)
    o_f, s_f = gated_delta_chunked(q, k, v, g[:, :, 0], beta[:, :, 0], s0_fwd, with_output)
    o_b, s_b = gated_delta_chunked(fl(q), fl(k), fl(v), fl(g[:, :, 1]), fl(beta[:, :, 1]), s0_bwd, with_output)
    o = o_f + fl(o_b) if with_output else None
    return o, s_f, s_b


def gdn_output(o, z, norm_w):
    B_, T = z.shape[:2]
    y = rms_norm(o, norm_w) * jax.nn.silu(to_heads(z, GDN_HEADS).astype(jnp.float32))
    return y.reshape(B_, T, GDN_W).astype(z.dtype)


def s5_discretize(a_re, a_im, log_dt, b_re, b_im):
    f32 = jnp.float32
    lam_re = jnp.minimum(a_re.astype(f32), -1e-4)
    lam_im = a_im.astype(f32)
    dt = jnp.exp(log_dt.astype(f32))[:, None]
    mag = jnp.exp(lam_re * dt)
    lb_re = mag * jnp.cos(lam_im * dt)
    lb_im = mag * jnp.sin(lam_im * dt)
    den = lam_re * lam_re + lam_im * lam_im
    f_re = ((lb_re - 1.0) * lam_re + lb_im * lam_im) / den
    f_im = (lb_im * lam_re - (lb_re - 1.0) * lam_im) / den
    b_re = b_re.astype(f32)
    b_im = b_im.astype(f32)
    bb_re = f_re[..., None] * b_re - f_im[..., None] * b_im
    bb_im = f_re[..., None] * b_im + f_im[..., None] * b_re
    return lb_re, lb_im, bb_re, bb_im


def s5_scan(bu_re, bu_im, lb_re, lb_im, h0_re, h0_im):
    bu_re = bu_re.at[:, 0].add(lb_re * h0_re - lb_im * h0_im)
    bu_im = bu_im.at[:, 0].add(lb_re * h0_im + lb_im * h0_re)
    a_re = jnp.broadcast_to(lb_re, bu_re.shape)
    a_im = jnp.broadcast_to(lb_im, bu_im.shape)

    def combine(e1, e2):
        a1r, a1i, b1r, b1i = e1
        a2r, a2i, b2r, b2i = e2
        return (a1r * a2r - a1i * a2i, a1r * a2i + a1i * a2r,
                a2r * b1r - a2i * b1i + b2r, a2r * b1i + a2i * b1r + b2i)

    _, _, x_re, x_im = lax.associative_scan(combine, (a_re, a_im, bu_re, bu_im), axis=1)
    return x_re, x_im


def s5_direction(u, disc, c_re, c_im, h0, reverse, need_y):
    lb_re, lb_im, bb_re, bb_im = disc
    B_, T, _ = u.shape
    ug = u.reshape(B_, T, S5_GROUPS, S5_GROUP_CH).astype(jnp.float32)
    if reverse:
        ug = jnp.flip(ug, axis=1)
    bu_re = jnp.einsum('btgc,gpc->btgp', ug, bb_re)
    bu_im = jnp.einsum('btgc,gpc->btgp', ug, bb_im)
    x_re, x_im = s5_scan(bu_re, bu_im, lb_re, lb_im, h0[0], h0[1])
    final = (x_re[:, -1], x_im[:, -1])
    if not need_y:
        return None, final
    y = (jnp.einsum('btgp,gcp->btgc', x_re, c_re.astype(jnp.float32))
         - jnp.einsum('btgp,gcp->btgc', x_im, c_im.astype(jnp.float32)))
    if reverse:
        y = jnp.flip(y, axis=1)
    return y.reshape(B_, T, S5_W), final


def s5_mixer(u, u_c, need_ctx, a_re, a_im, log_dt, b_re, b_im, c_re, c_im, d_skip, glu_w, glu_b):
    disc = [s5_discretize(a_re[i], a_im[i], log_dt[i], b_re, b_im) for i in range(2)]
    zero = jnp.zeros((u.shape[0], S5_GROUPS, S5_STATE), jnp.float32)
    yc_f, h_f = s5_direction(u_c, disc[0], c_re[0], c_im[0], (zero, zero), False, need_ctx)
    yc_b, h_b = s5_direction(u_c, disc[1], c_re[1], c_im[1], (zero, zero), True, need_ctx)
    y_f, _ = s5_direction(u, disc[0], c_re[0], c_im[0], h_f, False, True)
    y_b, _ = s5_direction(u, disc[1], c_re[1], c_im[1], h_b, True, True)

    def finish(inp, yf, yb):
        y = yf + yb + d_skip.astype(jnp.float32) * inp.astype(jnp.float32)
        y = jax.nn.gelu(y)
        y = y * jax.nn.sigmoid(y @ glu_w.astype(jnp.float32) + glu_b.astype(jnp.float32))
        return y.astype(inp.dtype)

    return finish(u, y_f, y_b), (finish(u_c, yc_f, yc_b) if need_ctx else None)


def merge_branches(ys, gate_logits, w_branch, w_out):
    B_, T = gate_logits.shape[:2]
    gates = jax.nn.sigmoid(gate_logits.astype(jnp.float32)).astype(gate_logits.dtype)
    gates = gates.reshape(B_, T, N_BRANCH, D_MODEL)
    proj = jnp.einsum('btiw,iwd->btid', jnp.stack(ys, axis=2), w_branch)
    return jnp.sum(gates * proj, axis=2) @ w_out


def token_mixer(h, hc, need_ctx, w_in, na_rpb, mla_q_norm, mla_kv_norm, mla_w_uq, mla_w_ukv,
                gdn_conv, gdn_a_log, gdn_dt_bias, gdn_norm, s5_a_re, s5_a_im, s5_log_dt,
                s5_b_re, s5_b_im, s5_c_re, s5_c_im, s5_d, s5_glu_w, s5_glu_b, w_branch, w_out):
    S = h.shape[1]
    t = jnp.arange(S)
    rows, cols = t // GRID_W, t % GRID_W
    (na_qkv, mla_cq, mla_ckv, gdn_qkv, gdn_z, gdn_a, gdn_b, s5_u, gate_logits) = split_cols(h @ w_in, IN_SPLITS)
    (na_qkv_c, mla_cq_c, mla_ckv_c, gdn_qkv_c, gdn_z_c, gdn_a_c, gdn_b_c, s5_u_c, gate_logits_c) = split_cols(hc @ w_in, IN_SPLITS)

    q, k, v = [to_heads(a, NA_HEADS) for a in jnp.split(na_qkv, 3, axis=-1)]
    q_c, k_c, v_c = [to_heads(a, NA_HEADS) for a in jnp.split(na_qkv_c, 3, axis=-1)]
    y_na = neighborhood_attention(q, k, v, k_c, v_c, na_rpb).reshape(h.shape[:2] + (NA_W,))

    k_lat, v_lat = mla_keys_values(mla_ckv, mla_kv_norm, mla_w_ukv, rows, cols)
    k_ctx, v_ctx = mla_keys_values(mla_ckv_c, mla_kv_norm, mla_w_ukv, None, None)
    q_lat = mla_queries(mla_cq, mla_q_norm, mla_w_uq, rows, cols)
    y_mla = dense_attention(q_lat, jnp.concatenate([k_lat, k_ctx], axis=1),
                            jnp.concatenate([v_lat, v_ctx], axis=1)).reshape(h.shape[:2] + (MLA_W,))

    gq, gk, gv, gg, gbeta = gdn_prepare(gdn_qkv, gdn_a, gdn_b, gdn_conv, gdn_a_log, gdn_dt_bias)
    cq, ck, cv, cg, cbeta = gdn_prepare(gdn_qkv_c, gdn_a_c, gdn_b_c, gdn_conv, gdn_a_log, gdn_dt_bias)
    s_zero = jnp.zeros((h.shape[0], GDN_HEADS, GDN_DK, GDN_DV), jnp.float32)
    o_c, s_f, s_b = gdn_bidirectional(cq, ck, cv, cg, cbeta, s_zero, s_zero, need_ctx)
    o_l, _, _ = gdn_bidirectional(gq, gk, gv, gg, gbeta, s_f, s_b, True)
    y_gdn = gdn_output(o_l, gdn_z, gdn_norm)

    y_s5, y_s5_c = s5_mixer(s5_u, s5_u_c, need_ctx, s5_a_re, s5_a_im, s5_log_dt, s5_b_re, s5_b_im,
                            s5_c_re, s5_c_im, s5_d, s5_glu_w, s5_glu_b)

    y = merge_branches([y_na, y_mla, y_gdn, y_s5], gate_logits, w_branch, w_out)
    if not need_ctx:
        return y, None
    L = hc.shape[1]
    y_na_c = dense_attention(q_c, k_c, v_c).reshape(hc.shape[:2] + (NA_W,))
    y_mla_c = dense_attention(mla_queries(mla_cq_c, mla_q_norm, mla_w_uq, None, None), k_ctx, v_ctx).reshape(hc.shape[:2] + (MLA_W,))
    y_gdn_c = gdn_output(o_c, gdn_z_c, gdn_norm)
    y_c = merge_branches([y_na_c, y_mla_c, y_gdn_c, y_s5_c], gate_logits_c, w_branch, w_out)
    return y, y_c


def sq_relu_mlp(h, w1, w2):
    return jnp.square(jax.nn.relu(h @ w1)) @ w2


def setup_inputs(seed: int = 0) -> dict:
    key = jax.random.key(seed)
    ks = jax.random.split(key, 30)
    f32 = jnp.float32

    def nrm(i, shape, scale):
        return scale * jax.random.normal(ks[i], shape, f32)

    def unif(i, shape, lo, hi):
        return jax.random.uniform(ks[i], shape, f32, lo, hi)

    dt = jnp.exp(unif(15, (DEPTH, 2, GDN_HEADS), math.log(1e-3), math.log(1e-1)))
    return {
        'x': nrm(0, (BATCH, SEQ, D_MODEL), 1.0),
        'c': nrm(1, (BATCH, D_MODEL), 1.0),
        'ctx': nrm(2, (BATCH, CTX_LEN, D_MODEL), 1.0),
        'c_ctx': nrm(3, (D_MODEL,), 1.0),
        'ada_w': nrm(4, (DEPTH, D_MODEL, 6 * D_MODEL), 0.5 * D_MODEL ** -0.5),
        'ada_b': nrm(5, (DEPTH, 6 * D_MODEL), 0.02),
        'norm_gains': 1.0 + nrm(6, (DEPTH, 4, D_MODEL), 0.05),
        'w_in': nrm(7, (DEPTH, D_MODEL, D_IN), D_MODEL ** -0.5),
        'na_rpb': nrm(8, (DEPTH, NA_HEADS, 2 * NA_WIN_H - 1, 2 * NA_WIN_W - 1), 0.1),
        'mla_q_norm': 1.0 + nrm(9, (DEPTH, MLA_Q_LORA), 0.05),
        'mla_kv_norm': 1.0 + nrm(10, (DEPTH, MLA_KV_LORA), 0.05),
        'mla_w_uq': nrm(11, (DEPTH, MLA_Q_LORA, MLA_HEADS * (MLA_NOPE + MLA_ROPE)), MLA_Q_LORA ** -0.5),
        'mla_w_ukv': nrm(12, (DEPTH, MLA_KV_LORA, MLA_HEADS * (MLA_NOPE + MLA_V)), MLA_KV_LORA ** -0.5),
        'gdn_conv': nrm(13, (DEPTH, GDN_CONV, 2 * GDN_HEADS * GDN_DK + GDN_W), GDN_CONV ** -0.5),
        'gdn_a_log': jnp.log(unif(14, (DEPTH, 2, GDN_HEADS), 1.0, 16.0)),
        'gdn_dt_bias': dt + jnp.log(-jnp.expm1(-dt)),
        'gdn_norm': 1.0 + nrm(16, (DEPTH, GDN_DV), 0.05),
        's5_a_re': -0.5 + nrm(17, (DEPTH, 2, S5_GROUPS, S5_STATE), 0.01),
        's5_a_im': jnp.broadcast_to(math.pi * jnp.arange(S5_STATE, dtype=f32), (DEPTH, 2, S5_GROUPS, S5_STATE)),
        's5_log_dt': unif(18, (DEPTH, 2, S5_GROUPS), math.log(1e-3), math.log(1e-1)),
        's5_b_re': nrm(19, (DEPTH, S5_GROUPS, S5_STATE, S5_GROUP_CH), (2 * S5_GROUP_CH) ** -0.5),
        's5_b_im': nrm(20, (DEPTH, S5_GROUPS, S5_STATE, S5_GROUP_CH), (2 * S5_GROUP_CH) ** -0.5),
        's5_c_re': nrm(21, (DEPTH, 2, S5_GROUPS, S5_GROUP_CH, S5_STATE), S5_STATE ** -0.5),
        's5_c_im': nrm(22, (DEPTH, 2, S5_GROUPS, S5_GROUP_CH, S5_STATE), S5_STATE ** -0.5),
        's5_d': nrm(23, (DEPTH, S5_W), 1.0),
        's5_glu_w': nrm(24, (DEPTH, S5_W, S5_W), S5_W ** -0.5),
        's5_glu_b': nrm(25, (DEPTH, S5_W), 0.02),
        'w_branch': nrm(26, (DEPTH, N_BRANCH, BRANCH_W, D_MODEL), BRANCH_W ** -0.5),
        'w_out': nrm(27, (DEPTH, D_MODEL, D_MODEL), D_MODEL ** -0.5),
        'mlp_w1': nrm(28, (DEPTH, D_MODEL, D_FF), D_MODEL ** -0.5),
        'mlp_w2': nrm(29, (DEPTH, D_FF, D_MODEL), D_FF ** -0.5),
    }


def reference(x, c, ctx, c_ctx, ada_w, ada_b, norm_gains, w_in, na_rpb, mla_q_norm, mla_kv_norm,
              mla_w_uq, mla_w_ukv, gdn_conv, gdn_a_log, gdn_dt_bias, gdn_norm, s5_a_re, s5_a_im,
              s5_log_dt, s5_b_re, s5_b_im, s5_c_re, s5_c_im, s5_d, s5_glu_w, s5_glu_b, w_branch,
              w_out, mlp_w1, mlp_w2):
    xc = ctx
    for l in range(DEPTH):
        need_ctx = l < DEPTH - 1
        mod = jax.nn.silu(c) @ ada_w[l] + ada_b[l]
        mod_c = jax.nn.silu(c_ctx) @ ada_w[l] + ada_b[l]
        sh1, sc1, g1, sh2, sc2, g2 = jnp.split(mod[:, None, :], 6, axis=-1)
        sh1c, sc1c, g1c, sh2c, sc2c, g2c = jnp.split(mod_c[None, None, :], 6, axis=-1)

        h = rms_norm(x, norm_gains[l, 0]) * (1.0 + sc1) + sh1
        hc = rms_norm(xc, norm_gains[l, 0]) * (1.0 + sc1c) + sh1c
        y, y_c = token_mixer(h, hc, need_ctx, w_in[l], na_rpb[l], mla_q_norm[l], mla_kv_norm[l],
                             mla_w_uq[l], mla_w_ukv[l], gdn_conv[l], gdn_a_log[l], gdn_dt_bias[l],
                             gdn_norm[l], s5_a_re[l], s5_a_im[l], s5_log_dt[l], s5_b_re[l], s5_b_im[l],
                             s5_c_re[l], s5_c_im[l], s5_d[l], s5_glu_w[l], s5_glu_b[l], w_branch[l], w_out[l])
        x = x + g1 * rms_norm(y, norm_gains[l, 1])
        h = rms_norm(x, norm_gains[l, 2]) * (1.0 + sc2) + sh2
        x = x + g2 * rms_norm(sq_relu_mlp(h, mlp_w1[l], mlp_w2[l]), norm_gains[l, 3])
        if need_ctx:
            xc = xc + g1c * rms_norm(y_c, norm_gains[l, 1])
            hc = rms_norm(xc, norm_gains[l, 2]) * (1.0 + sc2c) + sh2c
            xc = xc + g2c * rms_norm(sq_relu_mlp(hc, mlp_w1[l], mlp_w2[l]), norm_gains[l, 3])
    return x
```

```python
import numpy as np
import concourse.bass as bass
import concourse.mybir as mybir
from concourse.bass_utils import run_bass_kernel_spmd

F32 = mybir.dt.float32
BF16 = mybir.dt.bfloat16
ALU = mybir.AluOpType
AF = mybir.ActivationFunctionType
AX = mybir.AxisListType

D = 1024
DEPTH = 4
SEQ = 8192
CTX = 256
NCORE = 8
EPS = 1e-6
D_IN = 6576
NMIX = 2480
NT = 2112


class Dep:
    __slots__ = ("w", "r", "name")

    def __init__(self, name=""):
        self.w = None
        self.r = {}
        self.name = name


class FW:
    NDMA = 6

    def __init__(self):
        self.nc = bass.Bass("TRN2", target_bir_lowering=False)
        nc = self.nc
        self.eng = dict(pe=nc.tensor, dve=nc.vector, act=nc.scalar, pool=nc.gpsimd, sp=nc.sync)
        self.sem = {e: nc.alloc_semaphore(f"sem_{e}") for e in ("pe", "dve", "act", "pool")}
        self.cnt = {e: 0 for e in self.sem}
        self.dsem = {q: [nc.alloc_semaphore(f"dsem_{q}{i}") for i in range(self.NDMA)] for q in ("sp", "act", "pool")}
        self.dcnt = {q: [0] * self.NDMA for q in self.dsem}
        self.drr = {q: 0 for q in self.dsem}
        self.waited = {e: {} for e in self.eng}
        self.n_ins = 0
        self._psum = []
        self._ps_i = 0
        self._uid = 0

    def dram(self, name, shape, dt=F32, kind="ExternalInput"):
        return self.nc.dram_tensor(name, list(shape), dt, kind=kind).ap()

    def sb(self, name, shape, dt=F32):
        return self.nc.alloc_sbuf_tensor("s_" + name, list(shape), dt), Dep(name)

    def psum_banks(self, n=8):
        for i in range(n):
            t = self.nc.alloc_psum_tensor(f"ps{i}", [128, 512], F32)
            self._psum.append((t, Dep(f"ps{i}")))

    def ps(self):
        r = self._psum[self._ps_i % len(self._psum)]
        self._ps_i += 1
        return r

    def _wait(self, e, ev):
        key, sem, val, src = ev
        if e == "pe" and src == "pe":
            return
        if self.waited[e].get(key, 0) >= val:
            return
        self.eng[e].wait_ge(sem, val)
        self.waited[e][key] = val
        self.n_ins += 1

    def _pre(self, e, reads, writes):
        for d in reads:
            if d.w is not None:
                self._wait(e, d.w)
        for d in writes:
            if d.w is not None:
                self._wait(e, d.w)
            for ev in d.r.values():
                self._wait(e, ev)

    def _post(self, ev, reads, writes):
        for d in writes:
            d.w = ev
            d.r = {}
        for d in reads:
            if d not in writes:
                d.r[ev[0]] = ev

    def op(self, e, fn, reads=(), writes=()):
        reads = list(reads); writes = list(writes)
        self._pre(e, reads, writes)
        ins = fn(self.eng[e])
        self.cnt[e] += 1
        ins.then_inc(self.sem[e], 1)
        ev = (e, self.sem[e], self.cnt[e], e)
        self._post(ev, reads, writes)
        self.n_ins += 1
        return ins

    def dma(self, q, out, in_, reads=(), writes=(), **kw):
        reads = list(reads); writes = list(writes)
        self._pre(q, reads, writes)
        i = self.drr[q] % self.NDMA
        self.drr[q] += 1
        ins = self.eng[q].dma_start(out=out, in_=in_, **kw)
        self.dcnt[q][i] += 16
        ins.then_inc(self.dsem[q][i], 16)
        ev = (f"d_{q}{i}", self.dsem[q][i], self.dcnt[q][i], "dma")
        self._post(ev, reads, writes)
        self.n_ins += 1
        return ins

    def finish(self):
        for q in self.dsem:
            for i in range(self.NDMA):
                if self.dcnt[q][i] > 0:
                    self.eng["sp"].wait_ge(self.dsem[q][i], self.dcnt[q][i])
        return self.nc

    def mm(self, ps_ap, lhsT, rhs, start, stop, reads, ps_dep):
        return self.op("pe", lambda e: e.matmul(ps_ap, lhsT, rhs, start=start, stop=stop),
                       reads=reads, writes=[ps_dep])


def build_M():
    fw = FW(); nc = fw.nc
    NCOL = 3072
    cT = fw.dram("cT", [128, 8, 4])
    aw = fw.dram("aw", [1024, NCOL])
    ab = fw.dram("ab", [128, NCOL // 128])
    out = fw.dram("modT", [128, NCOL // 128, 4], kind="ExternalOutput")
    fw.psum_banks(4)
    c_sb, c_d = fw.sb("c_sb", [128, 8, 4])
    s_sb, s_d = fw.sb("s_sb", [128, 8, 4])
    b_sb, b_d = fw.sb("b_sb", [128, NCOL // 128])
    o_sb, o_d = fw.sb("o_sb", [128, NCOL // 128, 4])
    W, _ = fw.sb("W", [128, 8, NCOL])
    Wd = [Dep() for _ in range(8)]
    fw.dma("sp", c_sb[:], cT, writes=[c_d])
    fw.dma("sp", b_sb[:], ab, writes=[b_d])
    for kc in range(8):
        fw.dma("sp" if kc % 2 else "pool", W[:, kc, :], aw[kc * 128:(kc + 1) * 128, :], writes=[Wd[kc]])
    fw.op("act", lambda e: e.activation(s_sb[:], c_sb[:], AF.Silu), reads=[c_d], writes=[s_d])
    for m in range(NCOL // 128):
        pt, pd = fw.ps()
        for kc in range(8):
            fw.mm(pt[:, 0:4], W[:, kc, m * 128:(m + 1) * 128], s_sb[:, kc, :], kc == 0, kc == 7,
                  [Wd[kc], s_d], pd)
        fw.op("dve", lambda e: e.tensor_scalar(o_sb[:, m, :], pt[:, 0:4], b_sb[:, m:m + 1], None, ALU.add),
              reads=[pd, b_d], writes=[o_d])
    fw.dma("sp", out, o_sb[:], reads=[o_d])
    return fw.finish()


def _tiles():
    return [(0, 512, 0), (512, 512, 0), (1024, 512, 0), (1536, 512, 0), (2048, 64, 1)]


def _norm_mod(fw, x_t, x_d, n, ones_bf, ones_d, avec, bvec, v_d, h_t, h_d, sq_t, sq_d, rb_t, rb_d, tmp_t, tmp_d):
    fw.op("act", lambda e: e.activation(sq_t[:, :, 0:n], x_t[:, :, 0:n], AF.Square), reads=[x_d], writes=[sq_d])
    pt, pd = fw.ps()
    for kc in range(8):
        fw.mm(pt[:, 0:n], ones_bf[:], sq_t[:, kc, 0:n], kc == 0, kc == 7, [ones_d, sq_d], pd)
    fw.op("act", lambda e: e.activation(rb_t[:, 0:n], pt[:, 0:n], AF.Sqrt, bias=float(D * EPS)),
          reads=[pd], writes=[rb_d])
    fw.op("dve", lambda e: e.reciprocal(rb_t[:, 0:n], rb_t[:, 0:n]), reads=[rb_d], writes=[rb_d])
    for kc in range(8):
        fw.op("dve", lambda e: e.tensor_tensor(tmp_t[:, kc, 0:n], x_t[:, kc, 0:n], rb_t[:, 0:n], ALU.mult),
              reads=[x_d, rb_d], writes=[tmp_d[kc]])
        fw.op("act", lambda e: e.activation(h_t[:, kc, 0:n], tmp_t[:, kc, 0:n], AF.Identity,
                                            bias=bvec[:, kc:kc + 1], scale=avec[:, kc:kc + 1]),
              reads=[tmp_d[kc], v_d], writes=[h_d[kc]])


def _prep_ab(fw, vec_t, v_d, gi, sci, shi, a_t, a_d):
    fw.op("dve", lambda e: e.tensor_scalar(a_t[:, :, 0], vec_t[:, :, sci], 1.0, 32.0, ALU.add, ALU.mult),
          reads=[v_d], writes=[a_d])
    fw.op("dve", lambda e: e.tensor_tensor(a_t[:, :, 0], a_t[:, :, 0], vec_t[:, :, gi], ALU.mult),
          reads=[v_d, a_d], writes=[a_d])
    fw.op("dve", lambda e: e.tensor_copy(a_t[:, :, 1], vec_t[:, :, shi]), reads=[v_d, a_d], writes=[a_d])


def build_A():
    fw = FW(); nc = fw.nc
    xT = fw.dram("xT", [128, 8, NT])
    vecs = fw.dram("vecs", [128, 8, 8])
    w_in = fw.dram("w_in", [D, D_IN])
    zT = fw.dram("zT", [NMIX, NT], kind="ExternalOutput")
    gT = fw.dram("gT", [4096, NT], kind="ExternalOutput")
    fw.psum_banks(8)
    W, _ = fw.sb("W", [128, 8, D_IN], BF16)
    Wd = [Dep() for _ in range(8)]
    vec_t, v_d = fw.sb("vec", [128, 8, 8])
    ab_l, ab_ld = fw.sb("ab_l", [128, 8, 2])
    ab_c, ab_cd = fw.sb("ab_c", [128, 8, 2])
    ones_bf, ones_d = fw.sb("ones", [128, 128], BF16)
    xs = [fw.sb(f"x{i}", [128, 8, 512]) for i in range(2)]
    sq_t, sq_d = fw.sb("sq", [128, 8, 512], BF16)
    rb_t, rb_d = fw.sb("rb", [128, 512])
    tmp_t, _ = fw.sb("tmp", [128, 8, 512]); tmp_d = [Dep() for _ in range(8)]
    h_t, _ = fw.sb("h", [128, 8, 512], BF16); h_d = [Dep() for _ in range(8)]
    osb = [fw.sb(f"o{i}", [128, 512]) for i in range(4)]

    fw.dma("sp", vec_t[:], vecs, writes=[v_d])
    tiles = _tiles()
    fw.dma("sp", xs[0][0][:, :, 0:512], xT[:, :, 0:512], writes=[xs[0][1]])
    w_in_v = w_in.rearrange("(kc p) c -> p kc c", p=128)
    Wd = []
    cb = 0
    while cb < D_IN:
        ce = min(cb + 512, D_IN)
        dd = Dep(); Wd.append(dd)
        fw.dma("pool", W[:, :, cb:ce], w_in_v[:, :, cb:ce], writes=[dd])
        cb = ce
    fw.op("dve", lambda e: e.memset(ones_bf[:], 1.0), writes=[ones_d])
    _prep_ab(fw, vec_t, v_d, 0, 1, 2, ab_l, ab_ld)
    _prep_ab(fw, vec_t, v_d, 0, 3, 4, ab_c, ab_cd)

    chunks = []
    c0 = 0
    while c0 < NMIX:
        m = min(128, NMIX - c0); chunks.append((c0, m, 0)); c0 += m
    for g in range(32):
        chunks.append((NMIX + g * 128, 128, 1))
    oi = 0
    for ti, (t0, n, isctx) in enumerate(tiles):
        x_t, x_d = xs[ti % 2]
        if ti + 1 < len(tiles):
            t1, n1, _ = tiles[ti + 1]
            nx_t, nx_d = xs[(ti + 1) % 2]
            fw.dma("sp", nx_t[:, :, 0:n1], xT[:, :, t1:t1 + n1], writes=[nx_d])
        ab_t, ab_d = (ab_c, ab_cd) if isctx else (ab_l, ab_ld)
        _norm_mod(fw, x_t, x_d, n, ones_bf, ones_d, ab_t[:, :, 0], ab_t[:, :, 1], ab_d, h_t, h_d,
                  sq_t, sq_d, rb_t, rb_d, tmp_t, tmp_d)
        for (c0, m, isg) in chunks:
            pt, pd = fw.ps()
            for kc in range(8):
                fw.mm(pt[0:m, 0:n], W[:, kc, c0:c0 + m], h_t[:, kc, 0:n], kc == 0, kc == 7,
                      [Wd[bb] for bb in range(c0 // 512, (c0 + m - 1) // 512 + 1)] + [h_d[kc]], pd)
            o_t, o_d = osb[oi % 4]; oi += 1
            if isg:
                fw.op("act", lambda e: e.activation(o_t[0:m, 0:n], pt[0:m, 0:n], AF.Sigmoid), reads=[pd], writes=[o_d])
                fw.dma("sp", gT[c0 - NMIX:c0 - NMIX + m, t0:t0 + n], o_t[0:m, 0:n], reads=[o_d])
            else:
                fw.op("dve", lambda e: e.tensor_copy(o_t[0:m, 0:n], pt[0:m, 0:n]), reads=[pd], writes=[o_d])
                fw.dma("sp", zT[c0:c0 + m, t0:t0 + n], o_t[0:m, 0:n], reads=[o_d])
    return fw.finish()


def _ctiles():
    return [(i * 256, 256, 0) for i in range(8)] + [(2048, 64, 1)]


def _rms_resid(fw, y_t, y_d, x_t, x_d, n, ones_bf, ones_d, cvec, c_d, sq_t, sq_d, rb_t, rb_d, tmp_t, tmp_d):
    fw.op("act", lambda e: e.activation(sq_t[:, :, 0:n], y_t[:, :, 0:n], AF.Square), reads=[y_d], writes=[sq_d])
    pt, pd = fw.ps()
    for kc in range(8):
        fw.mm(pt[:, 0:n], ones_bf[:], sq_t[:, kc, 0:n], kc == 0, kc == 7, [ones_d, sq_d], pd)
    fw.op("act", lambda e: e.activation(rb_t[:, 0:n], pt[:, 0:n], AF.Sqrt, bias=float(D * EPS)),
          reads=[pd], writes=[rb_d])
    fw.op("dve", lambda e: e.reciprocal(rb_t[:, 0:n], rb_t[:, 0:n]), reads=[rb_d], writes=[rb_d])
    for kc in range(8):
        fw.op("dve", lambda e: e.tensor_tensor(tmp_t[:, kc, 0:n], y_t[:, kc, 0:n], rb_t[:, 0:n], ALU.mult),
              reads=[y_d, rb_d], writes=[tmp_d[kc]])
        fw.op("dve", lambda e: e.scalar_tensor_tensor(x_t[:, kc, 0:n], tmp_t[:, kc, 0:n], cvec[:, kc:kc + 1],
                                                       x_t[:, kc, 0:n], ALU.mult, ALU.add),
              reads=[tmp_d[kc], c_d, x_d], writes=[x_d])


def build_C1():
    fw = FW(); nc = fw.nc
    TN = 256
    xT = fw.dram("xT", [128, 8, NT])
    gT = fw.dram("gT", [4096, NT])
    names = ["yna", "ymla", "gof", "gob", "gz", "s5f", "s5b", "s5u"]
    yin = {k: fw.dram(k, [256, NT]) for k in names}
    vecs = fw.dram("vecs", [128, 8, 4])
    v2 = fw.dram("v2", [128, 2, 4])
    w_branch = fw.dram("w_branch", [4, 256, D])
    w_out = fw.dram("w_out", [D, D])
    glu_w = fw.dram("glu_w", [256, 256])
    x1T = fw.dram("x1T", [128, 8, NT], kind="ExternalOutput")
    fw.psum_banks(8)
    WB, WB_d = fw.sb("WB", [128, 8, D], BF16)
    WO, WO_d = fw.sb("WO", [128, 8, D], BF16)
    WG, WG_d = fw.sb("WG", [128, 2, 256], BF16)
    vec_t, v_d = fw.sb("vec", [128, 8, 4])
    v2_t, v2_d = fw.sb("v2", [128, 2, 4])
    c_l, c_ld = fw.sb("c_l", [128, 8]); c_c, c_cd = fw.sb("c_c", [128, 8])
    gn8, gn8_d = fw.sb("gn8", [128, 2])
    ones_bf, ones_d = fw.sb("ones", [128, 128], BF16)
    bd_bf, bd_d = fw.sb("bd", [128, 128], BF16)
    x_t, x_d = fw.sb("x", [128, 8, TN])
    yb = {k: fw.sb("b_" + k, [128, 2, TN], BF16 if k in ("yna", "ymla") else F32) for k in names}
    t1, t1_d = fw.sb("t1", [128, 2, TN]); t2, t2_d = fw.sb("t2", [128, 2, TN]); t3, t3_d = fw.sb("t3", [128, 2, TN])
    sqs, sqs_d = fw.sb("sqs", [128, 2, TN], BF16)
    ygdn, ygdn_d = fw.sb("ygdn", [128, 2, TN], BF16)
    yg, yg_d = fw.sb("yg", [128, 2, TN]); ygb, ygb_d = fw.sb("ygb", [128, 2, TN], BF16)
    ys5, ys5_d = fw.sb("ys5", [128, 2, TN], BF16)
    gbuf = [fw.sb(f"g{i}", [128, TN]) for i in range(8)]
    acc, acc_d = fw.sb("acc", [128, TN]); tmpm, tmpm_d = fw.sb("tmpm", [128, TN])
    mrg, _ = fw.sb("mrg", [128, 8, TN], BF16); mrg_d = [Dep() for _ in range(8)]
    y_t, y_d = fw.sb("y", [128, 8, TN])
    sq_t, sq_d = fw.sb("sq", [128, 8, TN], BF16)
    rb_t, rb_d = fw.sb("rb", [128, TN])
    tmp_t, _ = fw.sb("tmp", [128, 8, TN]); tmp_d = [Dep() for _ in range(8)]

    fw.dma("sp", vec_t[:], vecs, writes=[v_d])
    fw.dma("sp", v2_t[:], v2, writes=[v2_d])
    for i in range(4):
        for kc in range(2):
            fw.dma("pool", WB[:, i * 2 + kc, :], w_branch[i, kc * 128:(kc + 1) * 128, :], writes=[WB_d])
    for kc in range(8):
        fw.dma("pool", WO[:, kc, :], w_out[kc * 128:(kc + 1) * 128, :], writes=[WO_d])
    for kc in range(2):
        fw.dma("pool", WG[:, kc, :], glu_w[kc * 128:(kc + 1) * 128, :], writes=[WG_d])
    fw.op("dve", lambda e: e.memset(ones_bf[:], 1.0), writes=[ones_d])
    fw.op("dve", lambda e: e.memset(bd_bf[:], 0.0), writes=[bd_d])
    fw.op("dve", lambda e: e.memset(bd_bf[0:64, 0:64], 1.0), writes=[bd_d])
    fw.op("dve", lambda e: e.memset(bd_bf[64:128, 64:128], 1.0), writes=[bd_d])
    for (cv, cd, gi) in ((c_l, c_ld, 1), (c_c, c_cd, 2)):
        fw.op("dve", lambda e: e.scalar_tensor_tensor(cv[:], vec_t[:, :, gi], 32.0, vec_t[:, :, 0], ALU.mult, ALU.mult),
              reads=[v_d], writes=[cd])
    fw.op("dve", lambda e: e.tensor_scalar(gn8[:], v2_t[:, :, 0], 8.0, None, ALU.mult), reads=[v2_d], writes=[gn8_d])
    C_TANH = 0.7978845608028654
    for (t0, n, isctx) in _ctiles():
        fw.dma("sp", x_t[:, :, 0:n], xT[:, :, t0:t0 + n], writes=[x_d])
        for k in names:
            bt, bdp = yb[k]
            src = yin[k].rearrange("(c p) t -> p c t", p=128)[:, :, t0:t0 + n]
            fw.dma("pool" if k in ("yna", "ymla") else "sp", bt[:, :, 0:n], src, writes=[bdp])
        fw.op("dve", lambda e: e.tensor_tensor(t1[:, :, 0:n], yb["gof"][0][:, :, 0:n], yb["gob"][0][:, :, 0:n], ALU.add),
              reads=[yb["gof"][1], yb["gob"][1]], writes=[t1_d])
        fw.op("act", lambda e: e.activation(sqs[:, :, 0:n], t1[:, :, 0:n], AF.Square), reads=[t1_d], writes=[sqs_d])
        fw.op("act", lambda e: e.activation(t3[:, :, 0:n], yb["gz"][0][:, :, 0:n], AF.Silu), reads=[yb["gz"][1]], writes=[t3_d])
        for c in range(2):
            pt, pd = fw.ps()
            fw.mm(pt[:, 0:n], bd_bf[:], sqs[:, c, 0:n], True, True, [bd_d, sqs_d], pd)
            fw.op("act", lambda e: e.activation(t2[:, c, 0:n], pt[:, 0:n], AF.Sqrt, bias=float(64 * EPS)),
                  reads=[pd], writes=[t2_d])
        fw.op("dve", lambda e: e.reciprocal(t2[:, :, 0:n], t2[:, :, 0:n]), reads=[t2_d], writes=[t2_d])
        fw.op("dve", lambda e: e.tensor_tensor(t1[:, :, 0:n], t1[:, :, 0:n], t2[:, :, 0:n], ALU.mult),
              reads=[t1_d, t2_d], writes=[t1_d])
        for c in range(2):
            fw.op("dve", lambda e: e.scalar_tensor_tensor(ygdn[:, c, 0:n], t1[:, c, 0:n], gn8[:, c:c + 1], t3[:, c, 0:n],
                                                           ALU.mult, ALU.mult),
                  reads=[t1_d, gn8_d, t3_d], writes=[ygdn_d])
        for c in range(2):
            fw.op("dve", lambda e: e.scalar_tensor_tensor(t1[:, c, 0:n], yb["s5u"][0][:, c, 0:n], v2_t[:, c, 1:2],
                                                           yb["s5f"][0][:, c, 0:n], ALU.mult, ALU.add),
                  reads=[yb["s5u"][1], yb["s5f"][1], v2_d], writes=[t1_d])
        fw.op("dve", lambda e: e.tensor_tensor(t1[:, :, 0:n], t1[:, :, 0:n], yb["s5b"][0][:, :, 0:n], ALU.add),
              reads=[t1_d, yb["s5b"][1]], writes=[t1_d])
        fw.op("dve", lambda e: e.tensor_tensor(t2[:, :, 0:n], t1[:, :, 0:n], t1[:, :, 0:n], ALU.mult), reads=[t1_d], writes=[t2_d])
        fw.op("dve", lambda e: e.tensor_scalar(t2[:, :, 0:n], t2[:, :, 0:n], 0.044715, 1.0, ALU.mult, ALU.add),
              reads=[t2_d], writes=[t2_d])
        fw.op("dve", lambda e: e.tensor_tensor(t2[:, :, 0:n], t2[:, :, 0:n], t1[:, :, 0:n], ALU.mult),
              reads=[t1_d, t2_d], writes=[t2_d])
        fw.op("act", lambda e: e.activation(t3[:, :, 0:n], t2[:, :, 0:n], AF.Tanh, scale=C_TANH), reads=[t2_d], writes=[t3_d])
        fw.op("dve", lambda e: e.tensor_scalar(t3[:, :, 0:n], t3[:, :, 0:n], 1.0, 0.5, ALU.add, ALU.mult),
              reads=[t3_d], writes=[t3_d])
        fw.op("dve", lambda e: e.tensor_tensor(yg[:, :, 0:n], t3[:, :, 0:n], t1[:, :, 0:n], ALU.mult),
              reads=[t3_d, t1_d], writes=[yg_d])
        fw.op("act", lambda e: e.activation(ygb[:, :, 0:n], yg[:, :, 0:n], AF.Copy), reads=[yg_d], writes=[ygb_d])
        for m in range(2):
            pt, pd = fw.ps()
            for kc in range(2):
                fw.mm(pt[:, 0:n], WG[:, kc, m * 128:(m + 1) * 128], ygb[:, kc, 0:n], kc == 0, kc == 1, [WG_d, ygb_d], pd)
            fw.op("act", lambda e: e.activation(t2[:, m, 0:n], pt[:, 0:n], AF.Sigmoid, bias=v2_t[:, m, 2:3]),
                  reads=[pd, v2_d], writes=[t2_d])
        fw.op("dve", lambda e: e.tensor_tensor(ys5[:, :, 0:n], yg[:, :, 0:n], t2[:, :, 0:n], ALU.mult),
              reads=[yg_d, t2_d], writes=[ys5_d])
        br = [(yb["yna"][0], yb["yna"][1]), (yb["ymla"][0], yb["ymla"][1]), (ygdn, ygdn_d), (ys5, ys5_d)]
        gi = 0
        for m in range(8):
            for i in range(4):
                g_t, g_d = gbuf[gi % 8]; gi += 1
                fw.dma("sp" if (gi % 2) else "pool", g_t[:, 0:n], gT[i * 1024 + m * 128:i * 1024 + (m + 1) * 128, t0:t0 + n], writes=[g_d])
                pt, pd = fw.ps()
                for kc in range(2):
                    fw.mm(pt[:, 0:n], WB[:, i * 2 + kc, m * 128:(m + 1) * 128], br[i][0][:, kc, 0:n], kc == 0, kc == 1,
                          [WB_d, br[i][1]], pd)
                if i == 0:
                    fw.op("dve", lambda e: e.tensor_tensor(acc[:, 0:n], pt[:, 0:n], g_t[:, 0:n], ALU.mult),
                          reads=[pd, g_d], writes=[acc_d])
                else:
                    fw.op("dve", lambda e: e.tensor_tensor(tmpm[:, 0:n], pt[:, 0:n], g_t[:, 0:n], ALU.mult),
                          reads=[pd, g_d], writes=[tmpm_d])
                    if i < 3:
                        fw.op("dve", lambda e: e.tensor_tensor(acc[:, 0:n], acc[:, 0:n], tmpm[:, 0:n], ALU.add),
                              reads=[acc_d, tmpm_d], writes=[acc_d])
                    else:
                        fw.op("dve", lambda e: e.tensor_tensor(mrg[:, m, 0:n], acc[:, 0:n], tmpm[:, 0:n], ALU.add),
                              reads=[acc_d, tmpm_d], writes=[mrg_d[m]])
        for m in range(8):
            pt, pd = fw.ps()
            for kc in range(8):
                fw.mm(pt[:, 0:n], WO[:, kc, m * 128:(m + 1) * 128], mrg[:, kc, 0:n], kc == 0, kc == 7, [WO_d, mrg_d[kc]], pd)
            fw.op("act", lambda e: e.activation(y_t[:, m, 0:n], pt[:, 0:n], AF.Copy), reads=[pd], writes=[y_d])
        cv, cd = (c_c, c_cd) if isctx else (c_l, c_ld)
        _rms_resid(fw, y_t, y_d, x_t, x_d, n, ones_bf, ones_d, cv, cd, sq_t, sq_d, rb_t, rb_d, tmp_t, tmp_d)
        fw.dma("sp", x1T[:, :, t0:t0 + n], x_t[:, :, 0:n], reads=[x_d])
    return fw.finish()


def build_C2():
    fw = FW(); nc = fw.nc
    TN = 256
    xT = fw.dram("xT", [128, 8, NT])
    vecs = fw.dram("vecs", [128, 8, 8])
    w1 = fw.dram("w1", [D, 4096])
    w2 = fw.dram("w2", [4096, D])
    x2T = fw.dram("x2T", [128, 8, NT], kind="ExternalOutput")
    fw.psum_banks(8)
    W1, _ = fw.sb("W1", [128, 8, 4096], BF16); W1d = [Dep() for _ in range(8)]
    W2, _ = fw.sb("W2", [128, 32, D], BF16); W2d = [Dep() for _ in range(32)]
    vec_t, v_d = fw.sb("vec", [128, 8, 8])
    ab_l, ab_ld = fw.sb("ab_l", [128, 8, 2]); ab_c, ab_cd = fw.sb("ab_c", [128, 8, 2])
    c_l, c_ld = fw.sb("c_l", [128, 8]); c_c, c_cd = fw.sb("c_c", [128, 8])
    ones_bf, ones_d = fw.sb("ones", [128, 128], BF16)
    x_t, x_d = fw.sb("x", [128, 8, TN])
    sq_t, sq_d = fw.sb("sq", [128, 8, TN], BF16)
    rb_t, rb_d = fw.sb("rb", [128, TN])
    tmp_t, _ = fw.sb("tmp", [128, 8, TN]); tmp_d = [Dep() for _ in range(8)]
    h_t, _ = fw.sb("h", [128, 8, TN], BF16); h_d = [Dep() for _ in range(8)]
    hid, _ = fw.sb("hid", [128, 32, TN], BF16); hid_d = [Dep() for _ in range(32)]
    rbuf = [fw.sb(f"r{i}", [128, TN]) for i in range(2)]
    o_t, o_d = fw.sb("o", [128, 8, TN])

    fw.dma("sp", vec_t[:], vecs, writes=[v_d])
    w1_v = w1.rearrange("(kc p) c -> p kc c", p=128); w2_v = w2.rearrange("(kc p) c -> p kc c", p=128)
    W1d = [Dep() for _ in range(8)]; W2d = [Dep() for _ in range(8)]
    for bb in range(8):
        fw.dma("pool", W1[:, :, bb * 512:(bb + 1) * 512], w1_v[:, :, bb * 512:(bb + 1) * 512], writes=[W1d[bb]])
    for bb in range(8):
        fw.dma("pool", W2[:, :, bb * 128:(bb + 1) * 128], w2_v[:, :, bb * 128:(bb + 1) * 128], writes=[W2d[bb]])
    fw.op("dve", lambda e: e.memset(ones_bf[:], 1.0), writes=[ones_d])
    _prep_ab(fw, vec_t, v_d, 0, 1, 2, ab_l, ab_ld)
    _prep_ab(fw, vec_t, v_d, 0, 3, 4, ab_c, ab_cd)
    for (cv, cd, gi) in ((c_l, c_ld, 6), (c_c, c_cd, 7)):
        fw.op("dve", lambda e: e.scalar_tensor_tensor(cv[:], vec_t[:, :, gi], 32.0, vec_t[:, :, 5], ALU.mult, ALU.mult),
              reads=[v_d], writes=[cd])
    for (t0, n, isctx) in _ctiles():
        fw.dma("sp", x_t[:, :, 0:n], xT[:, :, t0:t0 + n], writes=[x_d])
        ab_t, ab_d = (ab_c, ab_cd) if isctx else (ab_l, ab_ld)
        _norm_mod(fw, x_t, x_d, n, ones_bf, ones_d, ab_t[:, :, 0], ab_t[:, :, 1], ab_d, h_t, h_d,
                  sq_t, sq_d, rb_t, rb_d, tmp_t, tmp_d)
        for m in range(32):
            pt, pd = fw.ps()
            for kc in range(8):
                fw.mm(pt[:, 0:n], W1[:, kc, m * 128:(m + 1) * 128], h_t[:, kc, 0:n], kc == 0, kc == 7, [W1d[m // 4], h_d[kc]], pd)
            r_t, r_d = rbuf[m % 2]
            fw.op("act", lambda e: e.activation(r_t[:, 0:n], pt[:, 0:n], AF.Relu), reads=[pd], writes=[r_d])
            fw.op("dve", lambda e: e.tensor_tensor(hid[:, m, 0:n], r_t[:, 0:n], r_t[:, 0:n], ALU.mult),
                  reads=[r_d], writes=[hid_d[m]])
        for m in range(8):
            pt, pd = fw.ps()
            for kc in range(32):
                fw.mm(pt[:, 0:n], W2[:, kc, m * 128:(m + 1) * 128], hid[:, kc, 0:n], kc == 0, kc == 31, [W2d[m], hid_d[kc]], pd)
            fw.op("act", lambda e: e.activation(o_t[:, m, 0:n], pt[:, 0:n], AF.Copy), reads=[pd], writes=[o_d])
        cv, cd = (c_c, c_cd) if isctx else (c_l, c_ld)
        _rms_resid(fw, o_t, o_d, x_t, x_d, n, ones_bf, ones_d, cv, cd, sq_t, sq_d, rb_t, rb_d, tmp_t, tmp_d)
        fw.dma("sp", x2T[:, :, t0:t0 + n], x_t[:, :, 0:n], reads=[x_d])
    return fw.finish()


TS = CTX + SEQ
NEG = -30000.0


def _attend(fw, QT, q_d, q0, nq, kd, KT, k_d, blocks, VA, v_d, scale, ident, id_d, ones_f, on_d,
            po_pair, pbufs, osb, osb_d, rec, rec_d, yo, yo_d, out_ap, cnt):
    po, po_d = po_pair
    nb = len(blocks)
    LA = 2
    inflight = {}
    for step in range(nb + LA):
        if step < nb:
            off, bias_ap, b_d = blocks[step]
            pt, pd = fw.ps()
            fw.mm(pt[:, 0:nq], KT[0:kd, off:off + 128], QT[0:kd, q0:q0 + nq], True, bias_ap is None, [k_d, q_d], pd)
            if bias_ap is not None:
                fw.mm(pt[:, 0:nq], ident[:], bias_ap, False, True, [id_d, b_d], pd)
            inflight[step] = (pt, pd)
        bi = step - LA
        if bi >= 0:
            off = blocks[bi][0]
            pt, pd = inflight.pop(bi)
            p_t, p_d = pbufs[cnt[0] % len(pbufs)]; cnt[0] += 1
            fw.op("act", lambda e: e.activation(p_t[:, 0:nq], pt[:, 0:nq], AF.Exp, scale=float(scale)), reads=[pd], writes=[p_d])
            fw.mm(po[0:65, 0:nq], VA[:, off // 128, 0:65], p_t[:, 0:nq], bi == 0, bi == nb - 1, [v_d, p_d], po_d)
    fw.op("act", lambda e: e.activation(osb[0:65, 0:nq], po[0:65, 0:nq], AF.Copy), reads=[po_d], writes=[osb_d])
    pt, pd = fw.ps()
    fw.mm(pt[0:64, 0:nq], ones_f[64:65, 0:64], osb[64:65, 0:nq], True, True, [on_d, osb_d], pd)
    fw.op("dve", lambda e: e.reciprocal(rec[0:64, 0:nq], pt[0:64, 0:nq]), reads=[pd], writes=[rec_d])
    fw.op("dve", lambda e: e.tensor_tensor(yo[0:64, 0:nq], osb[0:64, 0:nq], rec[0:64, 0:nq], ALU.mult),
          reads=[osb_d, rec_d], writes=[yo_d])
    fw.dma("sp", out_ap, yo[0:64, 0:nq], reads=[yo_d])


def build_BA():
    fw = FW(); nc = fw.nc
    na_q = fw.dram("na_q", [64, TS]); na_k = fw.dram("na_k", [64, TS]); na_v = fw.dram("na_v", [TS, 64])
    nbias = fw.dram("nbias", [24, 128, 512])
    cq = fw.dram("cq", [256, TS]); ckv = fw.dram("ckv", [128, TS])
    kr = fw.dram("kr", [96, TS]); krs = fw.dram("krs", [96, TS])
    cosF = fw.dram("cosF", [96, TS]); sinF = fw.dram("sinF", [96, TS])
    wuq = fw.dram("wuq", [256, 96]); wuqs = fw.dram("wuqs", [256, 96])
    wuk = fw.dram("wuk", [128, 64]); wuv = fw.dram("wuv", [128, 64])
    nrm = fw.dram("nrm", [128, 4])
    identd = fw.dram("ident", [128, 128])
    ynaT = fw.dram("ynaT", [64, TS], kind="ExternalOutput")
    ymlaT = fw.dram("ymlaT", [64, TS], kind="ExternalOutput")
    fw.psum_banks(6)
    po_pairs = [(nc.alloc_psum_tensor(f"po{i}", [128, 512], F32), Dep()) for i in range(2)]

    nqT, nq_d = fw.sb("nqT", [64, TS], BF16); nkT, nk_d = fw.sb("nkT", [64, TS], BF16)
    nVA, nv_d = fw.sb("nVA", [128, 66, 65], BF16)
    nB, nB_d = fw.sb("nB", [128, 24, 512], BF16)
    mQT, _ = fw.sb("mQT", [96, TS], BF16); mKT, _ = fw.sb("mKT", [96, TS], BF16)
    mVA, _ = fw.sb("mVA", [128, 66, 65], BF16)
    ident, id_d = fw.sb("identb", [128, 128], BF16)
    ones_f, on_d = fw.sb("ones_f", [128, 64])
    ones_bf, ones_d = fw.sb("ones", [128, 128], BF16)
    Wq, wq_d = fw.sb("Wq", [128, 2, 96], BF16); Wqs, wqs_d = fw.sb("Wqs", [128, 2, 96], BF16)
    Wk, wk_d = fw.sb("Wk", [128, 64], BF16); Wv, wv_d = fw.sb("Wv", [128, 64], BF16)
    nrm_t, nrm_d = fw.sb("nrm", [128, 4]); nrm2, nrm2_d = fw.sb("nrm2", [128, 4])
    stg, stg_d = fw.sb("stg", [64, 2112])
    cq_t, cq_d = fw.sb("cq", [128, 2, 512]); ck_t, ck_d = fw.sb("ck", [128, 512])
    kr_t, kr_d = fw.sb("kr", [96, 512]); krs_t, krs_d = fw.sb("krs", [96, 512])
    cs_t, cs_d = fw.sb("cs", [96, 512]); sn_t, sn_d = fw.sb("sn", [96, 512])
    sq_t, sq_d = fw.sb("sq", [128, 2, 512], BF16); sqk, sqk_d = fw.sb("sqk", [128, 512], BF16)
    rq, rq_d = fw.sb("rq", [128, 512]); rk, rk_d = fw.sb("rk", [128, 512])
    tq, tq_d = fw.sb("tq", [128, 2, 512]); tk, tk_d = fw.sb("tk", [128, 512])
    cqn, cqn_d = fw.sb("cqn", [128, 2, 512], BF16); ckn, ckn_d = fw.sb("ckn", [128, 512], BF16)
    ta, ta_d = fw.sb("ta", [96, 512]); tb, tb_d = fw.sb("tb", [96, 512])
    pbufs = [fw.sb(f"p{i}", [128, 512], BF16) for i in range(3)]
    osb, osb_d = fw.sb("osb", [128, 512]); rec, rec_d = fw.sb("rec", [64, 512]); yo, yo_d = fw.sb("yo", [64, 512])

    fw.dma("pool", ident[:], identd, writes=[id_d])
    fw.op("dve", lambda e: e.memset(ones_f[:], 1.0), writes=[on_d])
    fw.op("dve", lambda e: e.memset(ones_bf[:], 1.0), writes=[ones_d])
    fw.dma("sp", nrm_t[:], nrm, writes=[nrm_d])
    fw.op("dve", lambda e: e.tensor_scalar(nrm2[:, 0:2], nrm_t[:, 0:2], 16.0, None, ALU.mult), reads=[nrm_d], writes=[nrm2_d])
    fw.op("dve", lambda e: e.tensor_scalar(nrm2[:, 2:3], nrm_t[:, 2:3], float(np.sqrt(128.0)), None, ALU.mult),
          reads=[nrm_d, nrm2_d], writes=[nrm2_d])
    for kc in range(2):
        fw.dma("pool", Wq[:, kc, :], wuq[kc * 128:(kc + 1) * 128, :], writes=[wq_d])
        fw.dma("pool", Wqs[:, kc, :], wuqs[kc * 128:(kc + 1) * 128, :], writes=[wqs_d])
    fw.dma("pool", Wk[:], wuk, writes=[wk_d]); fw.dma("pool", Wv[:], wuv, writes=[wv_d])
    fw.dma("pool", nkT[:], na_k, writes=[nk_d])
    fw.dma("pool", nVA[:, :, 0:64], na_v.rearrange("(j p) d -> p j d", p=128), writes=[nv_d])
    fw.op("dve", lambda e: e.memset(nVA[:, :, 64:65], 1.0), reads=[], writes=[nv_d])
    mv_d = Dep()
    fw.op("dve", lambda e: e.memset(mVA[:, :, 64:65], 1.0), writes=[mv_d])
    for j in range(6):
        fw.dma("pool", nB[:, j * 4:(j + 1) * 4, :], nbias[j * 4:(j + 1) * 4].rearrange("s p q -> p s q"), writes=[nB_d])
    for j in range(4):
        fw.dma("sp", stg[:], na_q[:, j * 2112:(j + 1) * 2112], writes=[stg_d])
        fw.op("act", lambda e: e.activation(nqT[:, j * 2112:(j + 1) * 2112], stg[:], AF.Copy, scale=0.125),
              reads=[stg_d], writes=[nq_d])
    mq_d = Dep(); mk_d = Dep()
    tiles = [(0, 256)] + [(256 + 512 * i, 512) for i in range(16)]
    for (t0, n) in tiles:
        fw.dma("sp", cq_t[:, :, 0:n], cq.rearrange("(c p) t -> p c t", p=128)[:, :, t0:t0 + n], writes=[cq_d])
        fw.dma("sp", ck_t[:, 0:n], ckv[:, t0:t0 + n], writes=[ck_d])
        fw.dma("sp", kr_t[:, 0:n], kr[:, t0:t0 + n], writes=[kr_d])
        fw.dma("sp", krs_t[:, 0:n], krs[:, t0:t0 + n], writes=[krs_d])
        fw.dma("sp", cs_t[:, 0:n], cosF[:, t0:t0 + n], writes=[cs_d])
        fw.dma("sp", sn_t[:, 0:n], sinF[:, t0:t0 + n], writes=[sn_d])
        fw.op("act", lambda e: e.activation(sq_t[:, :, 0:n], cq_t[:, :, 0:n], AF.Square), reads=[cq_d], writes=[sq_d])
        pt, pd = fw.ps()
        for kc in range(2):
            fw.mm(pt[:, 0:n], ones_bf[:], sq_t[:, kc, 0:n], kc == 0, kc == 1, [ones_d, sq_d], pd)
        fw.op("act", lambda e: e.activation(rq[:, 0:n], pt[:, 0:n], AF.Sqrt, bias=float(256 * EPS)), reads=[pd], writes=[rq_d])
        fw.op("dve", lambda e: e.reciprocal(rq[:, 0:n], rq[:, 0:n]), reads=[rq_d], writes=[rq_d])
        for kc in range(2):
            fw.op("dve", lambda e: e.tensor_tensor(tq[:, kc, 0:n], cq_t[:, kc, 0:n], rq[:, 0:n], ALU.mult),
                  reads=[cq_d, rq_d], writes=[tq_d])
            fw.op("act", lambda e: e.activation(cqn[:, kc, 0:n], tq[:, kc, 0:n], AF.Copy, scale=nrm2[:, kc:kc + 1]),
                  reads=[tq_d, nrm2_d], writes=[cqn_d])
        p1, p1d = fw.ps(); p2, p2d = fw.ps()
        for kc in range(2):
            fw.mm(p1[0:96, 0:n], Wq[:, kc, :], cqn[:, kc, 0:n], kc == 0, kc == 1, [wq_d, cqn_d], p1d)
        for kc in range(2):
            fw.mm(p2[0:96, 0:n], Wqs[:, kc, :], cqn[:, kc, 0:n], kc == 0, kc == 1, [wqs_d, cqn_d], p2d)
        fw.op("dve", lambda e: e.tensor_tensor(ta[:, 0:n], p1[0:96, 0:n], cs_t[:, 0:n], ALU.mult), reads=[p1d, cs_d], writes=[ta_d])
        fw.op("dve", lambda e: e.tensor_tensor(tb[:, 0:n], p2[0:96, 0:n], sn_t[:, 0:n], ALU.mult), reads=[p2d, sn_d], writes=[tb_d])
        fw.op("dve", lambda e: e.tensor_tensor(mQT[:, t0:t0 + n], ta[:, 0:n], tb[:, 0:n], ALU.add), reads=[ta_d, tb_d], writes=[mq_d])
        fw.op("act", lambda e: e.activation(sqk[:, 0:n], ck_t[:, 0:n], AF.Square), reads=[ck_d], writes=[sqk_d])
        pt, pd = fw.ps()
        fw.mm(pt[:, 0:n], ones_bf[:], sqk[:, 0:n], True, True, [ones_d, sqk_d], pd)
        fw.op("act", lambda e: e.activation(rk[:, 0:n], pt[:, 0:n], AF.Sqrt, bias=float(128 * EPS)), reads=[pd], writes=[rk_d])
        fw.op("dve", lambda e: e.reciprocal(rk[:, 0:n], rk[:, 0:n]), reads=[rk_d], writes=[rk_d])
        fw.op("dve", lambda e: e.tensor_tensor(tk[:, 0:n], ck_t[:, 0:n], rk[:, 0:n], ALU.mult), reads=[ck_d, rk_d], writes=[tk_d])
        fw.op("act", lambda e: e.activation(ckn[:, 0:n], tk[:, 0:n], AF.Copy, scale=nrm2[:, 2:3]), reads=[tk_d, nrm2_d], writes=[ckn_d])
        pt, pd = fw.ps()
        fw.mm(pt[0:64, 0:n], Wk[:], ckn[:, 0:n], True, True, [wk_d, ckn_d], pd)
        fw.op("act", lambda e: e.activation(mKT[0:64, t0:t0 + n], pt[0:64, 0:n], AF.Copy), reads=[pd], writes=[mk_d])
        fw.op("dve", lambda e: e.tensor_tensor(ta[64:96, 0:n], kr_t[64:96, 0:n], cs_t[64:96, 0:n], ALU.mult),
              reads=[kr_d, cs_d, ta_d], writes=[ta_d])
        fw.op("dve", lambda e: e.tensor_tensor(tb[64:96, 0:n], krs_t[64:96, 0:n], sn_t[64:96, 0:n], ALU.mult),
              reads=[krs_d, sn_d, tb_d], writes=[tb_d])
        fw.op("dve", lambda e: e.tensor_tensor(mKT[64:96, t0:t0 + n], ta[64:96, 0:n], tb[64:96, 0:n], ALU.add),
              reads=[ta_d, tb_d], writes=[mk_d])
        for j in range(n // 128):
            pt, pd = fw.ps()
            fw.mm(pt[:, 0:64], ckn[:, j * 128:(j + 1) * 128], Wv[:], True, True, [ckn_d, wv_d], pd)
            fw.op("dve", lambda e: e.tensor_copy(mVA[:, t0 // 128 + j, 0:64], pt[:, 0:64]), reads=[pd], writes=[mv_d])
    cnt = [0]; ai = 0
    common = dict(ident=ident, id_d=id_d, ones_f=ones_f, on_d=on_d, pbufs=pbufs, osb=osb, osb_d=osb_d,
                  rec=rec, rec_d=rec_d, yo=yo, yo_d=yo_d, cnt=cnt)
    ctxb = [(0, None, None), (128, None, None)]
    _attend(fw, nqT, nq_d, 0, 256, 64, nkT, nk_d, ctxb, nVA, nv_d, 1.0, po_pair=po_pairs[0], out_ap=ynaT[:, 0:256], **common)
    for i in range(16):
        cls = 0 if i == 0 else (2 if i == 15 else 1)
        kr0 = 0 if i == 0 else (112 if i == 15 else 8 * i - 4)
        blocks = list(ctxb) + [(256 + kr0 * 64 + 128 * j, nB[:, cls * 8 + j, :], nB_d) for j in range(8)]
        q0 = 256 + 512 * i
        _attend(fw, nqT, nq_d, q0, 512, 64, nkT, nk_d, blocks, nVA, nv_d, 1.0, po_pair=po_pairs[(i + 1) % 2],
                out_ap=ynaT[:, q0:q0 + 512], **common)
    msc = 96.0 ** -0.5
    _attend(fw, mQT, mq_d, 0, 256, 96, mKT, mk_d, ctxb, mVA, mv_d, msc, po_pair=po_pairs[0], out_ap=ymlaT[:, 0:256], **common)
    allb = [(128 * j, None, None) for j in range(66)]
    for i in range(16):
        q0 = 256 + 512 * i
        _attend(fw, mQT, mq_d, q0, 512, 96, mKT, mk_d, allb, mVA, mv_d, msc, po_pair=po_pairs[(i + 1) % 2],
                out_ap=ymlaT[:, q0:q0 + 512], **common)
    return fw.finish()


S5L = 32
S5NC = TS // S5L
_KCOLS = np.array(list(range(0, 33)) + list(range(31, -1, -1)) + [32 * (2 ** j) for j in range(9)], np.float32)


def _bc(ap2, axis, count):
    P, n = ap2.shape
    shape = [P, n, count] if axis == 2 else [P, count, n]
    return ap2.unsqueeze(axis).broadcast_to(shape)


def build_BS():
    fw = FW(); nc = fw.nc
    PI = float(np.pi)
    spar = fw.dram("spar", [64, 8, 4])
    bpar = fw.dram("bpar", [64, 4, 2, 16])
    cpar = fw.dram("cpar", [64, 8, 2, 16])
    kvd = fw.dram("kv", [64, 74])
    maskd = fw.dram("mask", [4, 128, 512])
    i64d = fw.dram("i64", [64, 64])
    ud = fw.dram("u", [8, 512, S5NC])
    yd = fw.dram("y", [8, 512, S5NC], kind="ExternalOutput")
    fw.psum_banks(8)
    sp_t, sp_d = fw.sb("sp", [64, 8, 4]); bp_t, bp_d = fw.sb("bp", [64, 4, 2, 16]); cp_t, cp_d = fw.sb("cp", [64, 8, 2, 16])
    kv, kv_d = fw.sb("kv", [64, 74]); msk, msk_d = fw.sb("msk", [128, 4, 512]); i64, i64_d = fw.sb("i64", [64, 64])
    sc, sc_d = fw.sb("sc", [64, 8, 16])
    ang2, ang_d = fw.sb("ang2", [64, 2, 74]); r1, r1_d = fw.sb("r1", [64, 2, 74]); r2, r2_d = fw.sb("r2", [64, 2, 74])
    qi, qi_d = fw.sb("qi", [64, 2, 74], mybir.dt.int32)
    sc2, sn_d = fw.sb("sc2", [64, 2, 74]); cs_d = Dep(); sn = sc2[:, 0, :]; cs = sc2[:, 1, :]
    mP, mP_d = fw.sb("mP", [64, 74]); mN, mN_d = fw.sb("mN", [64, 32])
    A, A_d = fw.sb("A", [64, 74]); Bt, Bt_d = fw.sb("Bt", [64, 74]); Btn, Btn_d = fw.sb("Btn", [64, 74])
    aN, aN_d = fw.sb("aN", [64, 32]); bN, bN_d = fw.sb("bN", [64, 32])
    bb, bb_d = fw.sb("bb", [64, 2, 16])
    T1, T1_d = fw.sb("T1", [64, 512]); T2, T2_d = fw.sb("T2", [64, 512])
    Lk_re, Lk_re_d = fw.sb("Lk_re", [64, 512]); Lk_im, Lk_im_d = fw.sb("Lk_im", [64, 512])
    Lq_re, Lq_re_d = fw.sb("Lq_re", [64, 512]); Lq_in, Lq_in_d = fw.sb("Lq_in", [64, 512])
    Lg_re, Lg_re_d = fw.sb("Lg_re", [64, 512]); Lg_im, Lg_im_d = fw.sb("Lg_im", [64, 512])
    Mo_re, Mo_re_d = fw.sb("Mo_re", [64, 512], BF16); Mo_in, Mo_in_d = fw.sb("Mo_in", [64, 512], BF16)
    MI, MI_d = fw.sb("MI", [128, 4, 512], BF16); MG, MG_d = fw.sb("MG", [128, 4, 128], BF16)
    U, U_d = fw.sb("U", [128, 4, S5NC], BF16)
    X = [[fw.sb(f"X{i}{j}", [64, S5NC]) for j in range(2)] for i in range(2)]
    Xp, Xp_d = fw.sb("Xp", [64, 2, S5NC], BF16)
    Ysb, Ysb_d = fw.sb("Ysb", [128, 4, S5NC])

    for (t, d, src) in ((sp_t, sp_d, spar), (bp_t, bp_d, bpar), (cp_t, cp_d, cpar), (kv, kv_d, kvd), (i64, i64_d, i64d)):
        fw.dma("sp", t[:], src, writes=[d])
    fw.dma("sp", msk[:], maskd.rearrange("k p q -> p k q"), writes=[msk_d])
    V = lambda e, fn, r, w: fw.op(e, fn, reads=r, writes=w)
    S = lambda c: sc[:, :, c]
    V("dve", lambda e: e.tensor_scalar(S(0), sp_t[:, :, 0], -1e-4, None, ALU.min), [sp_d], [sc_d])
    V("act", lambda e: e.activation(S(1), sp_t[:, :, 2], AF.Exp), [sp_d, sc_d], [sc_d])
    V("dve", lambda e: e.tensor_tensor(S(2), S(0), S(1), ALU.mult), [sc_d], [sc_d])
    V("dve", lambda e: e.tensor_tensor(S(3), sp_t[:, :, 1], S(1), ALU.mult), [sp_d, sc_d], [sc_d])
    V("dve", lambda e: e.tensor_tensor(S(4), S(0), S(0), ALU.mult), [sc_d], [sc_d])
    V("dve", lambda e: e.tensor_tensor(S(10), sp_t[:, :, 1], sp_t[:, :, 1], ALU.mult), [sp_d, sc_d], [sc_d])
    V("dve", lambda e: e.tensor_tensor(S(4), S(4), S(10), ALU.add), [sc_d], [sc_d])
    V("dve", lambda e: e.reciprocal(S(4), S(4)), [sc_d], [sc_d])
    V("dve", lambda e: e.tensor_scalar(S(5), S(2), -1.0, None, ALU.mult), [sc_d], [sc_d])

    for gd in range(8):
        gl = gd % 4
        th = sc[:, gd, 3:4]; lrdt = sc[:, gd, 2:3]; nlrdt = sc[:, gd, 5:6]
        V("dve", lambda e: e.tensor_scalar(ang2[:, 0, :], kv[:], th, None, ALU.mult), [kv_d, sc_d], [ang_d])
        V("dve", lambda e: e.tensor_scalar(ang2[:, 1, :], ang2[:, 0, :], 0.5 * PI, None, ALU.add), [ang_d], [ang_d])
        V("dve", lambda e: e.tensor_scalar(r1[:], ang2[:], 1.0 / (2 * PI), None, ALU.mult), [ang_d], [r1_d])
        V("dve", lambda e: e.tensor_copy(qi[:], r1[:]), [r1_d], [qi_d])
        V("dve", lambda e: e.tensor_copy(r1[:], qi[:]), [qi_d], [r1_d])
        V("dve", lambda e: e.scalar_tensor_tensor(r2[:], r1[:], -2 * PI, ang2[:], ALU.mult, ALU.add), [r1_d, ang_d], [r2_d])
        V("dve", lambda e: e.tensor_scalar(r1[:], r2[:], PI, None, ALU.is_gt), [r2_d], [r1_d])
        V("dve", lambda e: e.scalar_tensor_tensor(r2[:], r1[:], -2 * PI, r2[:], ALU.mult, ALU.add), [r1_d, r2_d], [r2_d])
        V("dve", lambda e: e.tensor_scalar(r1[:], r2[:], -PI, None, ALU.is_lt), [r2_d], [r1_d])
        V("dve", lambda e: e.scalar_tensor_tensor(r2[:], r1[:], 2 * PI, r2[:], ALU.mult, ALU.add), [r1_d, r2_d], [r2_d])
        V("act", lambda e: e.activation(sc2[:], r2[:], AF.Sin), [r2_d], [sn_d, cs_d])
        V("act", lambda e: e.activation(mP[:], kv[:], AF.Exp, scale=lrdt), [kv_d, sc_d], [mP_d])
        V("act", lambda e: e.activation(mN[:], kv[:, 0:32], AF.Exp, scale=nlrdt), [kv_d, sc_d], [mN_d])
        V("dve", lambda e: e.tensor_tensor(A[:], mP[:], cs, ALU.mult), [mP_d, cs_d], [A_d])
        V("dve", lambda e: e.tensor_tensor(Bt[:], mP[:], sn, ALU.mult), [mP_d, sn_d], [Bt_d])
        V("dve", lambda e: e.tensor_scalar(Btn[:], Bt[:], -1.0, None, ALU.mult), [Bt_d], [Btn_d])
        V("dve", lambda e: e.tensor_tensor(aN[:], mN[:], sc2[:, 1, 0:32], ALU.mult), [mN_d, cs_d], [aN_d])
        V("dve", lambda e: e.scalar_tensor_tensor(bN[:], mN[:], -1.0, sc2[:, 0, 0:32], ALU.mult, ALU.mult), [mN_d, sn_d], [bN_d])
        lr = sc[:, gd, 0:1]; li = sp_t[:, gd, 1:2]; rden = sc[:, gd, 4:5]
        c6 = sc[:, gd, 6:7]; c7 = sc[:, gd, 7:8]; c8 = sc[:, gd, 8:9]; c9 = sc[:, gd, 9:10]
        c10 = sc[:, gd, 10:11]; c11 = sc[:, gd, 11:12]; c12 = sc[:, gd, 12:13]
        V("dve", lambda e: e.tensor_scalar(c6, A[:, 1:2], -1.0, None, ALU.add), [A_d, sc_d], [sc_d])
        V("dve", lambda e: e.tensor_copy(c7, Bt[:, 1:2]), [Bt_d, sc_d], [sc_d])
        V("dve", lambda e: e.tensor_tensor(c10, c6, lr, ALU.mult), [sc_d], [sc_d])
        V("dve", lambda e: e.tensor_tensor(c11, c7, li, ALU.mult), [sc_d, sp_d], [sc_d])
        V("dve", lambda e: e.tensor_tensor(c10, c10, c11, ALU.add), [sc_d], [sc_d])
        V("dve", lambda e: e.tensor_tensor(c8, c10, rden, ALU.mult), [sc_d], [sc_d])
        V("dve", lambda e: e.tensor_tensor(c10, c7, lr, ALU.mult), [sc_d], [sc_d])
        V("dve", lambda e: e.tensor_tensor(c11, c6, li, ALU.mult), [sc_d, sp_d], [sc_d])
        V("dve", lambda e: e.tensor_tensor(c10, c10, c11, ALU.subtract), [sc_d], [sc_d])
        V("dve", lambda e: e.tensor_tensor(c9, c10, rden, ALU.mult), [sc_d], [sc_d])
        V("dve", lambda e: e.tensor_scalar(c12, c9, -1.0, None, ALU.mult), [sc_d], [sc_d])
        bre = bp_t[:, gl, 0, :]; bim = bp_t[:, gl, 1, :]
        V("dve", lambda e: e.tensor_scalar(bb[:, 0, :], bre, c8, None, ALU.mult), [bp_d, sc_d], [bb_d])
        V("dve", lambda e: e.scalar_tensor_tensor(bb[:, 0, :], bim, c12, bb[:, 0, :], ALU.mult, ALU.add), [bp_d, sc_d, bb_d], [bb_d])
        V("dve", lambda e: e.tensor_scalar(bb[:, 1, :], bim, c8, None, ALU.mult), [bp_d, sc_d, bb_d], [bb_d])
        V("dve", lambda e: e.scalar_tensor_tensor(bb[:, 1, :], bre, c9, bb[:, 1, :], ALU.mult, ALU.add), [bp_d, sc_d, bb_d], [bb_d])
        cre = cp_t[:, gd, 0, :]; cim = cp_t[:, gd, 1, :]

        def outer(dst, dst_d, a1, a1d, v1, v1d, a2, a2d, v2, v2d, sub, negate=False):
            d3 = dst[:].rearrange("p (s c) -> p s c", c=16)
            t1 = T1[:].rearrange("p (s c) -> p s c", c=16); t2 = T2[:].rearrange("p (s c) -> p s c", c=16)
            V("dve", lambda e: e.tensor_tensor(t1, _bc(a1, 2, 16), _bc(v1, 1, 32), ALU.mult), [a1d, v1d], [T1_d])
            V("pool", lambda e: e.tensor_tensor(t2, _bc(a2, 2, 16), _bc(v2, 1, 32), ALU.mult), [a2d, v2d], [T2_d])
            if negate:
                V("dve", lambda e: e.scalar_tensor_tensor(d3, t1, -1.0, t2, ALU.mult, ALU.subtract), [T1_d, T2_d], [dst_d])
            else:
                V("dve", lambda e: e.tensor_tensor(d3, t1, t2, ALU.subtract if sub else ALU.add), [T1_d, T2_d], [dst_d])

        outer(Lk_re, Lk_re_d, aN[:], aN_d, bb[:, 0, :], bb_d, bN[:], bN_d, bb[:, 1, :], bb_d, True)
        outer(Lk_im, Lk_im_d, aN[:], aN_d, bb[:, 1, :], bb_d, bN[:], bN_d, bb[:, 0, :], bb_d, False)
        outer(Lq_re, Lq_re_d, A[:, 0:32], A_d, cre, cp_d, Bt[:, 0:32], Bt_d, cim, cp_d, True)
        outer(Lq_in, Lq_in_d, A[:, 0:32], A_d, cim, cp_d, Bt[:, 0:32], Bt_d, cre, cp_d, False, negate=True)
        outer(Lg_re, Lg_re_d, A[:, 33:65], A_d, bb[:, 0, :], bb_d, Bt[:, 33:65], Bt_d, bb[:, 1, :], bb_d, True)
        outer(Lg_im, Lg_im_d, A[:, 33:65], A_d, bb[:, 1, :], bb_d, Bt[:, 33:65], Bt_d, bb[:, 0, :], bb_d, False)
        outer(Mo_re, Mo_re_d, A[:, 1:33], A_d, cre, cp_d, Bt[:, 1:33], Bt_d, cim, cp_d, True)
        outer(Mo_in, Mo_in_d, A[:, 1:33], A_d, cim, cp_d, Bt[:, 1:33], Bt_d, cre, cp_d, False, negate=True)
        for ki in range(4):
            pt, pd = fw.ps()
            fw.mm(pt[:, :], Lk_re[:, ki * 128:(ki + 1) * 128], Lq_re[:], True, False, [Lk_re_d, Lq_re_d], pd)
            fw.mm(pt[:, :], Lk_im[:, ki * 128:(ki + 1) * 128], Lq_in[:], False, True, [Lk_im_d, Lq_in_d], pd)
            V("dve", lambda e: e.tensor_tensor(MI[:, ki, :], pt[:, :], msk[:, ki, :], ALU.mult), [pd, msk_d], [MI_d])
            pt, pd = fw.ps()
            fw.mm(pt[:, 0:64], Lg_re[:, ki * 128:(ki + 1) * 128], i64[:], True, True, [Lg_re_d, i64_d], pd)
            fw.mm(pt[:, 64:128], Lg_im[:, ki * 128:(ki + 1) * 128], i64[:], True, True, [Lg_im_d, i64_d], pd)
            V("act", lambda e: e.activation(MG[:, ki, :], pt[:, 0:128], AF.Copy), [pd], [MG_d])
        fw.dma("pool", U[:], ud[gd].rearrange("(k p) n -> p k n", p=128), writes=[U_d])
        (x0r, x0r_d), (x0i, x0i_d) = X[0]
        for ri, (xt, xd) in enumerate(X[0]):
            pt, pd = fw.ps()
            for ki in range(4):
                fw.mm(pt[0:64, 0:S5NC], MG[:, ki, ri * 64:(ri + 1) * 64], U[:, ki, :], ki == 0, ki == 3, [MG_d, U_d], pd)
            V("act", lambda e: e.activation(xt[:], pt[0:64, 0:S5NC], AF.Copy), [pd], [xd])
        cur = 0
        for j in range(9):
            sft = 2 ** j
            if sft >= S5NC:
                break
            lre = A[:, 65 + j:66 + j]; lim = Bt[:, 65 + j:66 + j]; limn = Btn[:, 65 + j:66 + j]
            (or_, or_d), (oi, oi_d) = X[cur]
            (nr, nr_d), (ni, ni_d) = X[1 - cur]
            V("dve", lambda e: e.tensor_copy(nr[:, 0:sft], or_[:, 0:sft]), [or_d], [nr_d])
            V("pool", lambda e: e.tensor_copy(ni[:, 0:sft], oi[:, 0:sft]), [oi_d], [ni_d])
            V("dve", lambda e: e.scalar_tensor_tensor(nr[:, sft:], or_[:, 0:S5NC - sft], lre, or_[:, sft:], ALU.mult, ALU.add),
              [or_d, A_d, nr_d], [nr_d])
            V("dve", lambda e: e.scalar_tensor_tensor(nr[:, sft:], oi[:, 0:S5NC - sft], limn, nr[:, sft:], ALU.mult, ALU.add),
              [oi_d, Btn_d, nr_d], [nr_d])
            V("dve", lambda e: e.scalar_tensor_tensor(ni[:, sft:], oi[:, 0:S5NC - sft], lre, oi[:, sft:], ALU.mult, ALU.add),
              [oi_d, A_d, ni_d], [ni_d])
            V("dve", lambda e: e.scalar_tensor_tensor(ni[:, sft:], or_[:, 0:S5NC - sft], lim, ni[:, sft:], ALU.mult, ALU.add),
              [or_d, Bt_d, ni_d], [ni_d])
            cur = 1 - cur
        (fr, fr_d), (fi, fi_d) = X[cur]
        V("dve", lambda e: e.memset(Xp[:, :, 0:1], 0.0), [], [Xp_d])
        V("dve", lambda e: e.tensor_copy(Xp[:, 0, 1:], fr[:, 0:S5NC - 1]), [fr_d, Xp_d], [Xp_d])
        V("dve", lambda e: e.tensor_copy(Xp[:, 1, 1:], fi[:, 0:S5NC - 1]), [fi_d, Xp_d], [Xp_d])
        for mo in range(4):
            pt, pd = fw.ps()
            for ki in range(mo + 1):
                fw.mm(pt[:, 0:S5NC], MI[:, ki, mo * 128:(mo + 1) * 128], U[:, ki, :], ki == 0, False, [MI_d, U_d], pd)
            fw.mm(pt[:, 0:S5NC], Mo_re[:, mo * 128:(mo + 1) * 128], Xp[:, 0, :], False, False, [Mo_re_d, Xp_d], pd)
            fw.mm(pt[:, 0:S5NC], Mo_in[:, mo * 128:(mo + 1) * 128], Xp[:, 1, :], False, True, [Mo_in_d, Xp_d], pd)
            V("act", lambda e: e.activation(Ysb[:, mo, :], pt[:, 0:S5NC], AF.Copy), [pd], [Ysb_d])
        fw.dma("sp", yd[gd].rearrange("(k p) n -> p k n", p=128), Ysb[:], reads=[Ysb_d])
    return fw.finish()


def _s5_mask():
    i = np.arange(512)
    s_ = i // 16
    return (s_[:, None] <= s_[None, :]).astype(np.float32).reshape(4, 128, 512)


def prep_BS(zs, l, inputs):
    maps = []
    mask = _s5_mask()
    kvt = np.ascontiguousarray(np.broadcast_to(_KCOLS[None, :], (64, 74)))
    for core in range(NCORE):
        b, h = core // 4, core % 4
        u = zs[b][:, 2224:2480]
        spar = np.zeros((64, 8, 4), np.float32); cpar = np.zeros((64, 8, 2, 16), np.float32)
        bpar = np.zeros((64, 4, 2, 16), np.float32)
        U = np.zeros((8, 512, S5NC), np.float32)
        for gl in range(4):
            g = 4 * h + gl
            bpar[:, gl, 0] = inputs['s5_b_re'][l][g]; bpar[:, gl, 1] = inputs['s5_b_im'][l][g]
            ug = u[:, g * 16:(g + 1) * 16]
            for d in range(2):
                gd = d * 4 + gl
                spar[:, gd, 0] = inputs['s5_a_re'][l][d, g]; spar[:, gd, 1] = inputs['s5_a_im'][l][d, g]
                spar[:, gd, 2] = inputs['s5_log_dt'][l][d, g]
                cpar[:, gd, 0] = inputs['s5_c_re'][l][d, g].T; cpar[:, gd, 1] = inputs['s5_c_im'][l][d, g].T
                useq = ug if d == 0 else np.concatenate([ug[:CTX][::-1], ug[CTX:][::-1]], 0)
                U[gd] = useq.reshape(S5NC, 512).T
        maps.append({"spar": spar, "bpar": bpar, "cpar": cpar, "kv": kvt, "mask": mask,
                     "i64": np.eye(64, dtype=np.float32), "u": U})
    return maps


def post_BS(res):
    out = []
    for b in range(2):
        yf = np.zeros((TS, 256), np.float32); yb = np.zeros((TS, 256), np.float32)
        for h in range(4):
            y = res[b * 4 + h]["y"]
            for gl in range(4):
                g = 4 * h + gl
                yf[:, g * 16:(g + 1) * 16] = y[gl].T.reshape(TS, 16)
                r = y[4 + gl].T.reshape(TS, 16)
                yb[:, g * 16:(g + 1) * 16] = np.concatenate([r[:CTX][::-1], r[CTX:][::-1]], 0)
        out.append((yf, yb))
    return out


GPAD = TS + 8
GNC = TS // 64


def _tokp(c):
    return 2 + 64 * c if c < 4 else 262 + 64 * (c - 4)


def build_BG():
    fw = FW(); nc = fw.nc
    rawd = fw.dram("raw", [2, 3, 64, GPAD])
    convd = fw.dram("convw", [64, 3, 4])
    abd = fw.dram("ab", [2, 2, 64, GNC])
    gpd = fw.dram("gpar", [64, 2, 2])
    trid = fw.dram("tri", [64, 64]); mud = fw.dram("maskU", [64, 64]); smd = fw.dram("smask", [64, 64]); i64d = fw.dram("i64", [64, 64])
    od = fw.dram("o", [2, TS, 64], kind="ExternalOutput")
    fw.psum_banks(6)
    po_pairs = [(nc.alloc_psum_tensor(f"po{i}", [128, 512], F32), Dep()) for i in range(2)]
    V = lambda e, fn, r, w: fw.op(e, fn, reads=r, writes=w)
    DD = (0, 1)

    cw, cw_d = fw.sb("cw", [64, 3, 4]); gp, gp_d = fw.sb("gp", [64, 2, 2]); gp2, gp2_d = fw.sb("gp2", [64, 2, 2])
    tri, tri_d = fw.sb("tri", [64, 64]); mU, mU_d = fw.sb("mU", [64, 64]); sm, sm_d = fw.sb("sm", [64, 64]); i64, i64_d = fw.sb("i64", [64, 64])
    ones, ones_d = fw.sb("ones", [64, 64])
    W = 512

    def g3(name):
        t, d = fw.sb(name, [64, W])
        return t, d, (lambda n=8, t=t: t[:, 0:n * 64].rearrange("p (c i) -> p c i", i=64))

    def per_dir(maker):
        return [maker(d) for d in DD]

    col = lambda nm: per_dir(lambda d: fw.sb(f"{nm}{d}", [64, GNC]))
    a_t = col("a_t"); b_t = col("b_t"); g_t = col("g_t"); ng_t = col("ng_t"); bet = col("bet"); nbet = col("nbet")
    gc = col("gc"); egc = col("egc"); etl = col("etl"); gl = col("gl")
    rawt = per_dir(lambda d: [fw.sb(f"rawt{d}{q}", [64, W + 4]) for q in range(3)])
    fq = per_dir(lambda d: [fw.sb(f"fq{d}{q}", [64, W]) for q in range(3)])
    sq = per_dir(lambda d: fw.sb(f"sq{d}", [64, W])); rn = per_dir(lambda d: fw.sb(f"rn{d}", [64, W]))
    G = lambda nm: per_dir(lambda d: g3(f"{nm}{d}"))
    Ktm = G("Ktm"); Vtm = G("Vtm"); Ke = G("Ke"); Kt = G("Kt"); gB = G("gB"); ngB = G("ngB")
    tE = G("tE"); DT = G("DT"); DTs = G("DTs"); eB = G("eB"); qd = G("qd"); QK = G("QK")
    P = per_dir(lambda d: [g3(f"P{d}{i}") for i in range(2)]); PT = per_dir(lambda d: [g3(f"PT{d}{i}") for i in range(2)])
    R = per_dir(lambda d: [g3(f"R{d}{i}") for i in range(2)])
    ub = G("ub"); kcT = G("kcT"); Osb = G("Osb")
    vn = per_dir(lambda d: [fw.sb(f"vn{d}{i}", [64, 64]) for i in range(2)])
    Sb = per_dir(lambda d: [fw.sb(f"S{d}{i}", [64, 64]) for i in range(2)])

    for (t, d, src) in ((cw, cw_d, convd), (gp, gp_d, gpd), (tri, tri_d, trid), (mU, mU_d, mud), (sm, sm_d, smd), (i64, i64_d, i64d)):
        fw.dma("sp", t[:], src, writes=[d])
    V("dve", lambda e: e.memset(ones[:], 1.0), [], [ones_d])
    V("act", lambda e: e.activation(gp2[:, :, 0], gp[:, :, 0], AF.Exp), [gp_d], [gp2_d])
    V("dve", lambda e: e.tensor_scalar(gp2[:, :, 0], gp2[:, :, 0], -1.0, None, ALU.mult), [gp2_d], [gp2_d])
    V("dve", lambda e: e.tensor_copy(gp2[:, :, 1], gp[:, :, 1]), [gp_d, gp2_d], [gp2_d])

    for d in DD:
        fw.dma("sp", a_t[d][0][:], abd[d, 0], writes=[a_t[d][1]]); fw.dma("sp", b_t[d][0][:], abd[d, 1], writes=[b_t[d][1]])
        V("act", lambda e: e.activation(g_t[d][0][:], a_t[d][0][:], AF.Exp, bias=gp2[:, d, 1:2]), [a_t[d][1], gp2_d], [g_t[d][1]])
        V("act", lambda e: e.activation(g_t[d][0][:], g_t[d][0][:], AF.Ln, bias=1.0), [g_t[d][1]], [g_t[d][1]])
        V("dve", lambda e: e.tensor_scalar(g_t[d][0][:], g_t[d][0][:], gp2[:, d, 0:1], None, ALU.mult), [g_t[d][1], gp2_d], [g_t[d][1]])
        V("dve", lambda e: e.tensor_scalar(ng_t[d][0][:], g_t[d][0][:], -1.0, None, ALU.mult), [g_t[d][1]], [ng_t[d][1]])
        V("act", lambda e: e.activation(bet[d][0][:], b_t[d][0][:], AF.Sigmoid), [b_t[d][1]], [bet[d][1]])
        V("dve", lambda e: e.tensor_scalar(nbet[d][0][:], bet[d][0][:], -1.0, None, ALU.mult), [bet[d][1]], [nbet[d][1]])
        pt, pd = fw.ps()
        fw.mm(pt[0:64, 0:GNC], tri[:], g_t[d][0][:], True, True, [tri_d, g_t[d][1]], pd)
        V("dve", lambda e: e.tensor_copy(gc[d][0][:], pt[0:64, 0:GNC]), [pd], [gc[d][1]])
        V("act", lambda e: e.activation(egc[d][0][:], pt[0:64, 0:GNC], AF.Exp), [pd], [egc[d][1]])
        pt, pd = fw.ps()
        fw.mm(pt[0:64, 0:GNC], ones[:], g_t[d][0][:], True, True, [ones_d, g_t[d][1]], pd)
        V("act", lambda e: e.activation(gl[d][0][:], pt[0:64, 0:GNC], AF.Exp), [pd], [gl[d][1]])
        V("dve", lambda e: e.tensor_tensor(etl[d][0][:], pt[0:64, 0:GNC], gc[d][0][:], ALU.subtract), [pd, gc[d][1]], [etl[d][1]])
        V("act", lambda e: e.activation(etl[d][0][:], etl[d][0][:], AF.Exp), [etl[d][1]], [etl[d][1]])
        V("dve", lambda e: e.memset(Sb[d][0][0][:], 0.0), [], [Sb[d][0][1]])

    s_cur = [0, 0]
    groups = [(0, 4)] + [(4 + 8 * i, 8) for i in range(16)]
    for gi, (c0, ncg) in enumerate(groups):
        p0 = _tokp(c0); w = ncg * 64
        cs = slice(c0, c0 + ncg)
        bc2 = lambda ap: ap.unsqueeze(2).broadcast_to([64, ncg, 64])
        bc1 = lambda ap: ap.unsqueeze(1).broadcast_to([64, ncg, 64])
        CH = [slice(c * 64, (c + 1) * 64) for c in range(ncg)]
        for d in DD:
            offs = [j - 2 for j in range(4)] if d == 0 else [2 - j for j in range(4)]
            for qi in range(3):
                r_t, r_d = rawt[d][qi]; f_t, f_d = fq[d][qi]
                fw.dma("sp", r_t[:, 0:w + 4], rawd[d, qi, :, p0 - 2:p0 + w + 2], writes=[r_d])
                if qi != 2:
                    V("dve", lambda e: e.tensor_scalar(f_t[:, 0:w], r_t[:, 2 + offs[0]:2 + offs[0] + w], cw[:, qi, 0:1], None, ALU.mult),
                      [r_d, cw_d], [f_d])
                    for j in range(1, 4):
                        V("dve", lambda e: e.scalar_tensor_tensor(f_t[:, 0:w], r_t[:, 2 + offs[j]:2 + offs[j] + w], cw[:, qi, j:j + 1],
                                                                  f_t[:, 0:w], ALU.mult, ALU.add), [r_d, cw_d, f_d], [f_d])
                else:
                    tmp_t, tmp_d = sq[d]
                    V("pool", lambda e: e.tensor_scalar(f_t[:, 0:w], r_t[:, 2 + offs[0]:2 + offs[0] + w], cw[:, qi, 0:1], None, ALU.mult),
                      [r_d, cw_d], [f_d])
                    for j in range(1, 4):
                        V("pool", lambda e: e.tensor_scalar(tmp_t[:, 0:w], r_t[:, 2 + offs[j]:2 + offs[j] + w], cw[:, qi, j:j + 1], None, ALU.mult),
                          [r_d, cw_d], [tmp_d])
                        V("pool", lambda e: e.tensor_tensor(f_t[:, 0:w], f_t[:, 0:w], tmp_t[:, 0:w], ALU.add), [tmp_d, f_d], [f_d])
                V("act", lambda e: e.activation(f_t[:, 0:w], f_t[:, 0:w], AF.Silu), [f_d], [f_d])
        for d in DD:
            for qi in range(2):
                f_t, f_d = fq[d][qi]; sq_t, sq_d = sq[d]; rn_t, rn_d = rn[d]
                V("act", lambda e: e.activation(sq_t[:, 0:w], f_t[:, 0:w], AF.Square), [f_d], [sq_d])
                pt, pd = fw.ps()
                fw.mm(pt[0:64, 0:w], ones[:], sq_t[:, 0:w], True, True, [ones_d, sq_d], pd)
                V("act", lambda e: e.activation(rn_t[:, 0:w], pt[0:64, 0:w], AF.Sqrt, bias=float(EPS)), [pd], [rn_d])
                V("dve", lambda e: e.reciprocal(rn_t[:, 0:w], rn_t[:, 0:w]), [rn_d], [rn_d])
                if qi == 0:
                    V("dve", lambda e: e.scalar_tensor_tensor(f_t[:, 0:w], f_t[:, 0:w], 0.125, rn_t[:, 0:w], ALU.mult, ALU.mult),
                      [f_d, rn_d], [f_d])
                else:
                    V("dve", lambda e: e.tensor_tensor(f_t[:, 0:w], f_t[:, 0:w], rn_t[:, 0:w], ALU.mult), [f_d, rn_d], [f_d])
        for d in DD:
            (qf, qf_d), (kf, kf_d), (vf, vf_d) = fq[d]
            for (src, src_d, (dst, dst_d, _)) in ((kf, kf_d, Ktm[d]), (vf, vf_d, Vtm[d])):
                pt, pd = fw.ps()
                for sl in CH:
                    fw.op("pe", lambda e: e.transpose(pt[0:64, sl], src[:, sl], i64[:]), reads=[src_d, i64_d], writes=[pd])
                V("act", lambda e: e.activation(dst[:, 0:w], pt[0:64, 0:w], AF.Copy), [pd], [dst_d])
        for d in DD:
            V("dve", lambda e: e.tensor_tensor(Ke[d][2](ncg), Ktm[d][2](ncg), bc2(egc[d][0][:, cs]), ALU.mult), [Ktm[d][1], egc[d][1]], [Ke[d][1]])
            V("pool", lambda e: e.tensor_tensor(Kt[d][2](ncg), Ktm[d][2](ncg), bc2(etl[d][0][:, cs]), ALU.mult), [Ktm[d][1], etl[d][1]], [Kt[d][1]])
            V("dve", lambda e: e.tensor_copy(gB[d][2](ncg), bc2(g_t[d][0][:, cs])), [g_t[d][1]], [gB[d][1]])
            V("pool", lambda e: e.tensor_copy(ngB[d][2](ncg), bc2(ng_t[d][0][:, cs])), [ng_t[d][1]], [ngB[d][1]])
        for d in DD:
            (qf, qf_d), (kf, kf_d), (vf, vf_d) = fq[d]
            pE, pEd = fw.ps(); pG, pGd = fw.ps()
            gB_t, gB_d, _ = gB[d]; ngB_t, ngB_d, _ = ngB[d]
            for sl in CH:
                fw.mm(pE[0:64, sl], gB_t[:, sl], tri[:], True, False, [gB_d, tri_d], pEd)
                fw.mm(pE[0:64, sl], tri[:], ngB_t[:, sl], False, True, [ngB_d, tri_d], pEd)
                fw.mm(pG[0:64, sl], gB_t[:, sl], tri[:], True, True, [gB_d, tri_d], pGd)
            pE3 = pE[0:64, 0:w].rearrange("p (c i) -> p c i", i=64)
            V("dve", lambda e: e.scalar_tensor_tensor(tE[d][2](ncg), pE3, 0.0, bc1(mU[:]), ALU.min, ALU.add), [pEd, mU_d], [tE[d][1]])
            V("act", lambda e: e.activation(DT[d][0][:, 0:w], tE[d][0][:, 0:w], AF.Exp), [tE[d][1]], [DT[d][1]])
            V("pool", lambda e: e.tensor_tensor(DTs[d][2](ncg), DT[d][2](ncg), bc1(sm[:]), ALU.mult), [DT[d][1], sm_d], [DTs[d][1]])
            V("act", lambda e: e.activation(eB[d][0][:, 0:w], pG[0:64, 0:w], AF.Exp), [pGd], [eB[d][1]])
            V("dve", lambda e: e.tensor_tensor(qd[d][0][:, 0:w], qf[:, 0:w], eB[d][0][:, 0:w], ALU.mult), [qf_d, eB[d][1]], [qd[d][1]])
        for d in DD:
            (qf, qf_d), (kf, kf_d), (vf, vf_d) = fq[d]
            pK, pKd = fw.ps(); pQ, pQd = fw.ps()
            for sl in CH:
                fw.mm(pK[0:64, sl], kf[:, sl], kf[:, sl], True, True, [kf_d], pKd)
                fw.mm(pQ[0:64, sl], kf[:, sl], qf[:, sl], True, True, [kf_d, qf_d], pQd)
            X_t, X_d, X3 = P[d][0]
            V("dve", lambda e: e.tensor_tensor(X_t[:, 0:w], pK[0:64, 0:w], DTs[d][0][:, 0:w], ALU.mult), [pKd, DTs[d][1]], [X_d])
            V("dve", lambda e: e.tensor_tensor(X3(ncg), X3(ncg), bc2(bet[d][0][:, cs]), ALU.mult), [X_d, bet[d][1]], [X_d])
            V("dve", lambda e: e.tensor_tensor(QK[d][0][:, 0:w], pQ[0:64, 0:w], DT[d][0][:, 0:w], ALU.mult), [pQd, DT[d][1]], [QK[d][1]])
        for d in DD:
            X_t, X_d, X3 = P[d][0]; XT_t, XT_d, _ = PT[d][0]; R_t, R_d, R3 = R[d][0]
            pt, pd = fw.ps()
            for sl in CH:
                fw.op("pe", lambda e: e.transpose(pt[0:64, sl], X_t[:, sl], i64[:]), reads=[X_d, i64_d], writes=[pd])
            V("act", lambda e: e.activation(XT_t[:, 0:w], pt[0:64, 0:w], AF.Copy), [pd], [XT_d])
            V("dve", lambda e: e.scalar_tensor_tensor(R3(ncg), X3(ncg), -1.0, bc1(i64[:]), ALU.mult, ALU.add), [X_d, i64_d], [R_d])
        cp = 0; cr = 0
        for k in range(1, 6):
            for d in DD:
                Pc, Pc_d, _ = P[d][cp]; PTc, PTc_d, _ = PT[d][cp]
                Pn, Pn_d, _ = P[d][1 - cp]; PTn, PTn_d, _ = PT[d][1 - cp]
                pa, pad_ = fw.ps()
                for sl in CH:
                    fw.mm(pa[0:64, sl], Pc[:, sl], PTc[:, sl], True, True, [Pc_d, PTc_d], pad_)
                V("act", lambda e: e.activation(PTn[:, 0:w], pa[0:64, 0:w], AF.Copy), [pad_], [PTn_d])
                if k < 5:
                    pb, pbd = fw.ps()
                    for sl in CH:
                        fw.mm(pb[0:64, sl], PTc[:, sl], Pc[:, sl], True, True, [Pc_d, PTc_d], pbd)
                    V("act", lambda e: e.activation(Pn[:, 0:w], pb[0:64, 0:w], AF.Copy), [pbd], [Pn_d])
            for d in DD:
                PTn, PTn_d, _ = PT[d][1 - cp]
                Rc, Rc_d, _ = R[d][cr]; Rn, Rn_d, _ = R[d][1 - cr]
                pc_, pcd = fw.ps()
                for sl in CH:
                    fw.mm(pc_[0:64, sl], PTn[:, sl], Rc[:, sl], True, True, [PTn_d, Rc_d], pcd)
                V("dve", lambda e: e.tensor_tensor(Rn[:, 0:w], Rc[:, 0:w], pc_[0:64, 0:w], ALU.add), [Rc_d, pcd], [Rn_d])
            cp = 1 - cp; cr = 1 - cr
        for d in DD:
            ZT, ZT_d, _ = R[d][cr]
            pU, pUd = fw.ps(); pC, pCd = fw.ps()
            for sl in CH:
                fw.mm(pU[0:64, sl], ZT[:, sl], Vtm[d][0][:, sl], True, True, [ZT_d, Vtm[d][1]], pUd)
                fw.mm(pC[0:64, sl], Ke[d][0][:, sl], ZT[:, sl], True, True, [ZT_d, Ke[d][1]], pCd)
            pU3 = pU[0:64, 0:w].rearrange("p (c i) -> p c i", i=64)
            V("dve", lambda e: e.tensor_tensor(ub[d][2](ncg), pU3, bc2(bet[d][0][:, cs]), ALU.mult), [pUd, bet[d][1]], [ub[d][1]])
            V("act", lambda e: e.activation(kcT[d][0][:, 0:w], pC[0:64, 0:w], AF.Copy), [pCd], [kcT[d][1]])
        for c in range(ncg):
            sl = CH[c]; cc = c0 + c
            pas = []; pbs = []
            for d in DD:
                S_t, S_d = Sb[d][s_cur[d]]
                pa, pad_ = fw.ps(); pas.append((pa, pad_))
                fw.mm(pa[0:64, 0:64], kcT[d][0][:, sl], S_t[:], True, True, [kcT[d][1], S_d], pad_)
            for d in DD:
                vn_t, vn_d = vn[d][c % 2]; pa, pad_ = pas[d]
                V("dve", lambda e: e.scalar_tensor_tensor(vn_t[:], pa[0:64, 0:64], nbet[d][0][:, cc:cc + 1], ub[d][0][:, sl], ALU.mult, ALU.add),
                  [pad_, nbet[d][1], ub[d][1]], [vn_d])
            for d in DD:
                S_t, S_d = Sb[d][s_cur[d]]; vn_t, vn_d = vn[d][c % 2]; po, po_d = po_pairs[d]
                pb, pbd = fw.ps(); pbs.append((pb, pbd))
                fw.mm(pb[0:64, 0:64], Kt[d][0][:, sl], vn_t[:], True, True, [Kt[d][1], vn_d], pbd)
                fw.mm(po[0:64, sl], qd[d][0][:, sl], S_t[:], True, False, [qd[d][1], S_d], po_d)
                fw.mm(po[0:64, sl], QK[d][0][:, sl], vn_t[:], False, True, [QK[d][1], vn_d], po_d)
            for d in DD:
                S_t, S_d = Sb[d][s_cur[d]]; Sn_t, Sn_d = Sb[d][1 - s_cur[d]]; pb, pbd = pbs[d]
                V("dve", lambda e: e.scalar_tensor_tensor(Sn_t[:], S_t[:], gl[d][0][:, cc:cc + 1], pb[0:64, 0:64], ALU.mult, ALU.add),
                  [S_d, gl[d][1], pbd], [Sn_d])
                s_cur[d] = 1 - s_cur[d]
        for d in DD:
            po, po_d = po_pairs[d]
            V("act", lambda e: e.activation(Osb[d][0][:, 0:w], po[0:64, 0:w], AF.Copy), [po_d], [Osb[d][1]])
            fw.dma("sp", od[d, c0 * 64:(c0 + ncg) * 64, :].rearrange("(c i) e -> i c e", i=64), Osb[d][2](ncg), reads=[Osb[d][1]])
    return fw.finish()


def prep_BG(zs, l, inputs):
    maps = []
    ii = np.arange(64)
    tri = (ii[:, None] <= ii[None, :]).astype(np.float32)
    maskU = np.where(ii[None, :] >= ii[:, None], 0.0, NEG).astype(np.float32)
    smask = (ii[None, :] > ii[:, None]).astype(np.float32)
    for core in range(NCORE):
        b, h = core // 4, core % 4
        z = zs[b]
        raw = np.zeros((2, 3, 64, GPAD), np.float32)
        ab = np.zeros((2, 2, 64, GNC), np.float32)
        convw = np.zeros((64, 3, 4), np.float32)
        gpar = np.zeros((64, 2, 2), np.float32)
        for qi in range(3):
            cols = slice(1184 + qi * 256 + h * 64, 1184 + qi * 256 + (h + 1) * 64)
            x = z[:, cols]
            convw[:, qi, :] = inputs['gdn_conv'][l][:, qi * 256 + h * 64:qi * 256 + (h + 1) * 64].T
            for d in range(2):
                xc, xl = x[:CTX], x[CTX:]
                if d == 1:
                    xc, xl = xc[::-1], xl[::-1]
                raw[d, qi, :, 2:258] = xc.T; raw[d, qi, :, 262:262 + SEQ] = xl.T
        for d in range(2):
            for wi, c0 in enumerate((2208, 2216)):
                v = z[:, c0 + d * 4 + h]
                if d == 1:
                    v = np.concatenate([v[:CTX][::-1], v[CTX:][::-1]])
                ab[d, wi] = v.reshape(GNC, 64).T
            gpar[:, d, 0] = inputs['gdn_a_log'][l][d, h]; gpar[:, d, 1] = inputs['gdn_dt_bias'][l][d, h]
        maps.append({"raw": raw, "convw": convw, "ab": ab, "gpar": gpar, "tri": tri, "maskU": maskU, "smask": smask,
                     "i64": np.eye(64, dtype=np.float32)})
    return maps


def post_BG(res):
    out = []
    for b in range(2):
        of = np.zeros((TS, 256), np.float32); ob = np.zeros((TS, 256), np.float32)
        for h in range(4):
            o = res[b * 4 + h]["o"]
            of[:, h * 64:(h + 1) * 64] = o[0]
            ob[:, h * 64:(h + 1) * 64] = np.concatenate([o[1][:CTX][::-1], o[1][CTX:][::-1]], 0)
        out.append((of, ob))
    return out


_NC_CACHE = {}


def _get(name, builder):
    if name not in _NC_CACHE:
        _NC_CACHE[name] = builder()
    return _NC_CACHE[name]


def _run(name, builder, in_maps):
    nc = _get(name, builder)
    res = run_bass_kernel_spmd(nc, in_maps, core_ids=list(range(NCORE)))
    return res.results


def fm(v):
    v = np.asarray(v, np.float32)
    lead = v.shape[:-1]
    a = v.reshape(lead + (v.shape[-1] // 128, 128))
    a = np.moveaxis(a, -1, 0)
    a = np.moveaxis(a, -1, 1)
    return np.ascontiguousarray(a)


_ROPE_PERM = np.array(list(range(8, 16)) + list(range(0, 8)) + list(range(24, 32)) + list(range(16, 24)))


def _rope_tables():
    t = np.arange(SEQ)
    rows = (t // 64).astype(np.float32); cols = (t % 64).astype(np.float32)
    inv = (10000.0 ** (-np.arange(8, dtype=np.float32) / 8)).astype(np.float32)
    ar = rows[None, :] * inv[:, None]; ac = cols[None, :] * inv[:, None]
    cosF = np.ones((96, TS), np.float32); sinF = np.zeros((96, TS), np.float32)
    for base, a in ((64, ar), (80, ac)):
        cosF[base:base + 8, CTX:] = np.cos(a); cosF[base + 8:base + 16, CTX:] = np.cos(a)
        sinF[base:base + 8, CTX:] = -np.sin(a); sinF[base + 8:base + 16, CTX:] = np.sin(a)
    return cosF, sinF


def _na_bias(rpb_h):
    out = np.full((24, 128, 512), NEG, np.float32)
    kk = np.arange(128); kr_l = kk // 64; kc = kk % 64
    qq = np.arange(512); qr_l = qq // 64; qc = qq % 64
    for cls, (r0, kr0) in enumerate(((0, 0), (8, 4), (120, 112))):
        r = r0 + qr_l
        rs = np.clip(r - 4, 0, 120)
        cs = np.clip(qc - 8, 0, 48)
        for j in range(8):
            kr = kr0 + 2 * j + kr_l
            okr = (kr[:, None] >= rs[None, :]) & (kr[:, None] < rs[None, :] + 8)
            okc = (kc[:, None] >= cs[None, :]) & (kc[:, None] < cs[None, :] + 16)
            di = np.clip(kr[:, None] - r[None, :] + 7, 0, 14)
            dj = np.clip(kc[:, None] - qc[None, :] + 15, 0, 30)
            vals = rpb_h[di, dj]
            out[cls * 8 + j] = np.where(okr & okc, vals, np.float32(NEG))
    return out


def prep_BA(zs, l, inputs, consts):
    cosF, sinF = consts
    maps = []
    T = lambda a: np.ascontiguousarray(a.T)
    for core in range(NCORE):
        b, h = core // 4, core % 4
        z = zs[b]
        krope = z[:, 1152:1184]
        kr = np.zeros((96, TS), np.float32); krs = np.zeros((96, TS), np.float32)
        kr[64:96] = krope.T; krs[64:96] = krope[:, _ROPE_PERM].T
        wuq = np.ascontiguousarray(inputs['mla_w_uq'][l][:, h * 96:(h + 1) * 96])
        wuqs = wuq.copy(); wuqs[:, 64:96] = wuq[:, 64 + _ROPE_PERM]
        wkv = inputs['mla_w_ukv'][l]
        nrm = np.zeros((128, 4), np.float32)
        nrm[:, 0] = inputs['mla_q_norm'][l][0:128]; nrm[:, 1] = inputs['mla_q_norm'][l][128:256]
        nrm[:, 2] = inputs['mla_kv_norm'][l]
        maps.append({
            "na_q": T(z[:, h * 64:(h + 1) * 64]), "na_k": T(z[:, 256 + h * 64:256 + (h + 1) * 64]),
            "na_v": np.ascontiguousarray(z[:, 512 + h * 64:512 + (h + 1) * 64]),
            "nbias": _na_bias(inputs['na_rpb'][l][h]),
            "cq": T(z[:, 768:1024]), "ckv": T(z[:, 1024:1152]), "kr": kr, "krs": krs, "cosF": cosF, "sinF": sinF,
            "wuq": wuq, "wuqs": wuqs,
            "wuk": np.ascontiguousarray(wkv[:, h * 128:h * 128 + 64]),
            "wuv": np.ascontiguousarray(wkv[:, h * 128 + 64:(h + 1) * 128]),
            "nrm": nrm, "ident": np.eye(128, dtype=np.float32)})
    return maps


def run_M(c, c_ctx, ada_w, ada_b):
    cs = np.stack([c[0], c[1], c_ctx, c_ctx], axis=-1)
    cT = np.ascontiguousarray(cs.reshape(8, 128, 4).transpose(1, 0, 2))
    maps = []
    for core in range(NCORE):
        l, half = core // 2, core % 2
        sl = slice(half * 3072, (half + 1) * 3072)
        maps.append({"cT": cT, "aw": np.ascontiguousarray(ada_w[l][:, sl]),
                     "ab": np.ascontiguousarray(ada_b[l][sl].reshape(24, 128).T)})
    res = _run("M", build_M, maps)
    mod = np.zeros((DEPTH, 6144, 4), np.float32)
    for core in range(NCORE):
        l, half = core // 2, core % 2
        o = res[core]["modT"]
        mod[l, half * 3072:(half + 1) * 3072] = o.transpose(1, 0, 2).reshape(3072, 4)
    return mod[..., :3]


def _tok_shard_T(per_batch):
    out = []
    for core in range(NCORE):
        b, q = core // 4, core % 4
        a = per_batch[b]
        blk = np.concatenate([a[CTX + q * 2048:CTX + (q + 1) * 2048], a[q * 64:(q + 1) * 64]], 0)
        out.append(np.ascontiguousarray(blk.T))
    return out


def _fmv(v):
    F, n = v.shape
    return np.ascontiguousarray(v.reshape(F // 128, 128, n).transpose(1, 0, 2))


def kernel(**inputs):
    import time
    inputs = {k: np.asarray(v, np.float32) for k, v in inputs.items()}
    T0 = time.time()
    mod = run_M(inputs['c'], inputs['c_ctx'], inputs['ada_w'], inputs['ada_b'])
    consts = _rope_tables()
    xT = []
    for core in range(NCORE):
        b, q = core // 4, core % 4
        blk = np.concatenate([inputs['x'][b, q * 2048:(q + 1) * 2048], inputs['ctx'][b, q * 64:(q + 1) * 64]], 0)
        xT.append(np.ascontiguousarray(blk.T.reshape(8, 128, NT).transpose(1, 0, 2)))
    for l in range(DEPTH):
        ng = inputs['norm_gains'][l]; m = mod[l]
        seg = lambda i, col: m[i * 1024:(i + 1) * 1024, col]
        maps = []
        for core in range(NCORE):
            b = core // 4
            vec = np.zeros((1024, 8), np.float32)
            vec[:, 0] = ng[0]; vec[:, 1] = seg(1, b); vec[:, 2] = seg(0, b); vec[:, 3] = seg(1, 2); vec[:, 4] = seg(0, 2)
            maps.append({"xT": xT[core], "vecs": _fmv(vec), "w_in": inputs['w_in'][l]})
        rA = _run("A", build_A, maps)
        zs = []
        for b in range(2):
            lat = np.concatenate([rA[b * 4 + q]["zT"][:, :2048].T for q in range(4)], 0)
            cx = np.concatenate([rA[b * 4 + q]["zT"][:, 2048:].T for q in range(4)], 0)
            zs.append(np.concatenate([cx, lat], 0))
        rBA = _run("BA", build_BA, prep_BA(zs, l, inputs, consts))
        rBS = _run("BS", build_BS, prep_BS(zs, l, inputs))
        rBG = _run("BG", build_BG, prep_BG(zs, l, inputs))
        yna = [np.concatenate([rBA[b * 4 + h]["ynaT"].T for h in range(4)], 1) for b in range(2)]
        ymla = [np.concatenate([rBA[b * 4 + h]["ymlaT"].T for h in range(4)], 1) for b in range(2)]
        s5 = post_BS(rBS); gd = post_BG(rBG)
        sh = {"yna": _tok_shard_T(yna), "ymla": _tok_shard_T(ymla),
              "gof": _tok_shard_T([g[0] for g in gd]), "gob": _tok_shard_T([g[1] for g in gd]),
              "gz": _tok_shard_T([z[:, 1952:2208] for z in zs]),
              "s5f": _tok_shard_T([y[0] for y in s5]), "s5b": _tok_shard_T([y[1] for y in s5]),
              "s5u": _tok_shard_T([z[:, 2224:2480] for z in zs])}
        maps = []
        for core in range(NCORE):
            b = core // 4
            vec = np.zeros((1024, 4), np.float32); vec[:, 0] = ng[1]; vec[:, 1] = seg(2, b); vec[:, 2] = seg(2, 2)
            v2 = np.zeros((256, 4), np.float32)
            v2[:, 0] = np.tile(inputs['gdn_norm'][l], 4); v2[:, 1] = inputs['s5_d'][l]; v2[:, 2] = inputs['s5_glu_b'][l]
            d = {"xT": xT[core], "gT": rA[core]["gT"], "vecs": _fmv(vec), "v2": _fmv(v2),
                 "w_branch": inputs['w_branch'][l], "w_out": inputs['w_out'][l], "glu_w": inputs['s5_glu_w'][l]}
            for k in sh:
                d[k] = sh[k][core]
            maps.append(d)
        rC1 = _run("C1", build_C1, maps)
        maps = []
        for core in range(NCORE):
            b = core // 4
            vec = np.zeros((1024, 8), np.float32)
            vec[:, 0] = ng[2]; vec[:, 1] = seg(4, b); vec[:, 2] = seg(3, b); vec[:, 3] = seg(4, 2); vec[:, 4] = seg(3, 2)
            vec[:, 5] = ng[3]; vec[:, 6] = seg(5, b); vec[:, 7] = seg(5, 2)
            maps.append({"xT": rC1[core]["x1T"], "vecs": _fmv(vec), "w1": inputs['mlp_w1'][l], "w2": inputs['mlp_w2'][l]})
        rC2 = _run("C2", build_C2, maps)
        xT = [rC2[core]["x2T"] for core in range(NCORE)]
        print(f"[kernel] layer {l} done at {time.time() - T0:.0f}s", flush=True)
    out = np.zeros((2, SEQ, D), np.float32)
    for core in range(NCORE):
        b, q = core // 4, core % 4
        out[b, q * 2048:(q + 1) * 2048] = xT[core].transpose(2, 1, 0).reshape(NT, D)[:2048]
    return out
```

```python
import numpy as np
import concourse.bass as bass
import concourse.mybir as mybir
from concourse.bass_utils import run_bass_kernel_spmd

F32 = mybir.dt.float32
BF16 = mybir.dt.bfloat16
ALU = mybir.AluOpType
AF = mybir.ActivationFunctionType
AX = mybir.AxisListType

D = 1024
DEPTH = 4
SEQ = 8192
CTX = 256
NCORE = 8
EPS = 1e-6
D_IN = 6576
NMIX = 2480
NT = 2112


class Dep:
    __slots__ = ("w", "r", "name")

    def __init__(self, name=""):
        self.w = None
        self.r = {}
        self.name = name


class FW:
    NDMA = 8

    def __init__(self):
        self.nc = bass.Bass("TRN2", target_bir_lowering=False)
        nc = self.nc
        self.eng = dict(pe=nc.tensor, dve=nc.vector, act=nc.scalar, pool=nc.gpsimd, sp=nc.sync)
        self.sem = {e: nc.alloc_semaphore(f"sem_{e}") for e in ("pe", "dve", "act", "pool")}
        self.cnt = {e: 0 for e in self.sem}
        self.dsem = {q: [nc.alloc_semaphore(f"dsem_{q}{i}") for i in range(self.NDMA)] for q in ("sp", "act", "pool")}
        self.dcnt = {q: [0] * self.NDMA for q in self.dsem}
        self.drr = {q: 0 for q in self.dsem}
        self.waited = {e: {} for e in self.eng}
        self.n_ins = 0
        self._psum = []
        self._ps_i = 0
        self._uid = 0

    def dram(self, name, shape, dt=F32, kind="ExternalInput"):
        return self.nc.dram_tensor(name, list(shape), dt, kind=kind).ap()

    def sb(self, name, shape, dt=F32):
        return self.nc.alloc_sbuf_tensor("s_" + name, list(shape), dt), Dep(name)

    def psum_banks(self, n=8):
        for i in range(n):
            t = self.nc.alloc_psum_tensor(f"ps{i}", [128, 512], F32)
            self._psum.append((t, Dep(f"ps{i}")))

    def ps(self):
        r = self._psum[self._ps_i % len(self._psum)]
        self._ps_i += 1
        return r

    def _wait(self, e, ev):
        key, sem, val, src = ev
        if e == "pe" and src == "pe":
            return
        if self.waited[e].get(key, 0) >= val:
            return
        self.eng[e].wait_ge(sem, val)
        self.waited[e][key] = val
        self.n_ins += 1

    def _pre(self, e, reads, writes):
        for d in reads:
            if d.w is not None:
                self._wait(e, d.w)
        for d in writes:
            if d.w is not None:
                self._wait(e, d.w)
            for ev in d.r.values():
                self._wait(e, ev)

    def _post(self, ev, reads, writes):
        for d in writes:
            d.w = ev
            d.r = {}
        for d in reads:
            if d not in writes:
                d.r[ev[0]] = ev

    def op(self, e, fn, reads=(), writes=()):
        reads = list(reads); writes = list(writes)
        self._pre(e, reads, writes)
        ins = fn(self.eng[e])
        self.cnt[e] += 1
        ins.then_inc(self.sem[e], 1)
        ev = (e, self.sem[e], self.cnt[e], e)
        self._post(ev, reads, writes)
        self.n_ins += 1
        return ins

    def dma(self, q, out, in_, reads=(), writes=(), **kw):
        reads = list(reads); writes = list(writes)
        self._pre(q, reads, writes)
        i = self.drr[q] % self.NDMA
        self.drr[q] += 1
        if self.dcnt[q][i] > 0:
            self._wait(q, (f"d_{q}{i}", self.dsem[q][i], self.dcnt[q][i], "dma"))
        ins = self.eng[q].dma_start(out=out, in_=in_, **kw)
        self.dcnt[q][i] += 16
        ins.then_inc(self.dsem[q][i], 16)
        ev = (f"d_{q}{i}", self.dsem[q][i], self.dcnt[q][i], "dma")
        self._post(ev, reads, writes)
        self.n_ins += 1
        return ins

    def finish(self):
        for q in self.dsem:
            for i in range(self.NDMA):
                if self.dcnt[q][i] > 0:
                    self.eng["sp"].wait_ge(self.dsem[q][i], self.dcnt[q][i])
        return self.nc

    def mm(self, ps_ap, lhsT, rhs, start, stop, reads, ps_dep):
        return self.op("pe", lambda e: e.matmul(ps_ap, lhsT, rhs, start=start, stop=stop),
                       reads=reads, writes=[ps_dep])


def build_M():
    fw = FW(); nc = fw.nc
    NCOL = 3072
    cT = fw.dram("cT", [128, 8, 4])
    aw = fw.dram("aw", [1024, NCOL])
    ab = fw.dram("ab", [128, NCOL // 128])
    out = fw.dram("modT", [128, NCOL // 128, 4], kind="ExternalOutput")
    fw.psum_banks(4)
    c_sb, c_d = fw.sb("c_sb", [128, 8, 4])
    s_sb, s_d = fw.sb("s_sb", [128, 8, 4])
    b_sb, b_d = fw.sb("b_sb", [128, NCOL // 128])
    o_sb, o_d = fw.sb("o_sb", [128, NCOL // 128, 4])
    W, _ = fw.sb("W", [128, 8, NCOL])
    Wd = [Dep() for _ in range(8)]
    fw.dma("sp", c_sb[:], cT, writes=[c_d])
    fw.dma("sp", b_sb[:], ab, writes=[b_d])
    for kc in range(8):
        fw.dma("sp" if kc % 2 else "pool", W[:, kc, :], aw[kc * 128:(kc + 1) * 128, :], writes=[Wd[kc]])
    fw.op("act", lambda e: e.activation(s_sb[:], c_sb[:], AF.Silu), reads=[c_d], writes=[s_d])
    for m in range(NCOL // 128):
        pt, pd = fw.ps()
        for kc in range(8):
            fw.mm(pt[:, 0:4], W[:, kc, m * 128:(m + 1) * 128], s_sb[:, kc, :], kc == 0, kc == 7,
                  [Wd[kc], s_d], pd)
        fw.op("dve", lambda e: e.tensor_scalar(o_sb[:, m, :], pt[:, 0:4], b_sb[:, m:m + 1], None, ALU.add),
              reads=[pd, b_d], writes=[o_d])
    fw.dma("sp", out, o_sb[:], reads=[o_d])
    return fw.finish()


def _tiles():
    return [(0, 512, 0), (512, 512, 0), (1024, 512, 0), (1536, 512, 0), (2048, 64, 1)]


def _norm_mod(fw, x_t, x_d, n, ones_bf, ones_d, avec, bvec, v_d, h_t, h_d, sq_t, sq_d, rb_t, rb_d, tmp_t, tmp_d):
    fw.op("act", lambda e: e.activation(sq_t[:, :, 0:n], x_t[:, :, 0:n], AF.Square), reads=[x_d], writes=[sq_d])
    pt, pd = fw.ps()
    for kc in range(8):
        fw.mm(pt[:, 0:n], ones_bf[:], sq_t[:, kc, 0:n], kc == 0, kc == 7, [ones_d, sq_d], pd)
    fw.op("act", lambda e: e.activation(rb_t[:, 0:n], pt[:, 0:n], AF.Sqrt, bias=float(D * EPS)),
          reads=[pd], writes=[rb_d])
    fw.op("dve", lambda e: e.reciprocal(rb_t[:, 0:n], rb_t[:, 0:n]), reads=[rb_d], writes=[rb_d])
    for kc in range(8):
        fw.op("dve", lambda e: e.tensor_tensor(tmp_t[:, kc, 0:n], x_t[:, kc, 0:n], rb_t[:, 0:n], ALU.mult),
              reads=[x_d, rb_d], writes=[tmp_d[kc]])
        fw.op("act", lambda e: e.activation(h_t[:, kc, 0:n], tmp_t[:, kc, 0:n], AF.Identity,
                                            bias=bvec[:, kc:kc + 1], scale=avec[:, kc:kc + 1]),
              reads=[tmp_d[kc], v_d], writes=[h_d[kc]])


def _prep_ab(fw, vec_t, v_d, gi, sci, shi, a_t, a_d):
    fw.op("dve", lambda e: e.tensor_scalar(a_t[:, :, 0], vec_t[:, :, sci], 1.0, 32.0, ALU.add, ALU.mult),
          reads=[v_d], writes=[a_d])
    fw.op("dve", lambda e: e.tensor_tensor(a_t[:, :, 0], a_t[:, :, 0], vec_t[:, :, gi], ALU.mult),
          reads=[v_d, a_d], writes=[a_d])
    fw.op("dve", lambda e: e.tensor_copy(a_t[:, :, 1], vec_t[:, :, shi]), reads=[v_d, a_d], writes=[a_d])


def build_A():
    fw = FW(); nc = fw.nc
    xT = fw.dram("xT", [128, 8, NT])
    vecs = fw.dram("vecs", [128, 8, 8])
    w_in = fw.dram("w_in", [D, D_IN])
    zT = fw.dram("zT", [NMIX, NT], kind="ExternalOutput")
    gT = fw.dram("gT", [4096, NT], kind="ExternalOutput")
    fw.psum_banks(8)
    W, _ = fw.sb("W", [128, 8, D_IN], BF16)
    Wd = [Dep() for _ in range(8)]
    vec_t, v_d = fw.sb("vec", [128, 8, 8])
    ab_l, ab_ld = fw.sb("ab_l", [128, 8, 2])
    ab_c, ab_cd = fw.sb("ab_c", [128, 8, 2])
    ones_bf, ones_d = fw.sb("ones", [128, 128], BF16)
    xs = [fw.sb(f"x{i}", [128, 8, 512]) for i in range(2)]
    sq_t, sq_d = fw.sb("sq", [128, 8, 512], BF16)
    rb_t, rb_d = fw.sb("rb", [128, 512])
    tmp_t, _ = fw.sb("tmp", [128, 8, 512]); tmp_d = [Dep() for _ in range(8)]
    h_t, _ = fw.sb("h", [128, 8, 512], BF16); h_d = [Dep() for _ in range(8)]
    osb = [fw.sb(f"o{i}", [128, 512]) for i in range(4)]

    fw.dma("sp", vec_t[:], vecs, writes=[v_d])
    tiles = _tiles()
    fw.dma("sp", xs[0][0][:, :, 0:512], xT[:, :, 0:512], writes=[xs[0][1]])
    w_in_v = w_in.rearrange("(kc p) c -> p kc c", p=128)
    Wd = []
    cb = 0
    while cb < D_IN:
        ce = min(cb + 512, D_IN)
        dd = Dep(); Wd.append(dd)
        fw.dma("pool", W[:, :, cb:ce], w_in_v[:, :, cb:ce], writes=[dd])
        cb = ce
    fw.op("dve", lambda e: e.memset(ones_bf[:], 1.0), writes=[ones_d])
    _prep_ab(fw, vec_t, v_d, 0, 1, 2, ab_l, ab_ld)
    _prep_ab(fw, vec_t, v_d, 0, 3, 4, ab_c, ab_cd)

    chunks = []
    c0 = 0
    while c0 < NMIX:
        m = min(128, NMIX - c0); chunks.append((c0, m, 0)); c0 += m
    for g in range(32):
        chunks.append((NMIX + g * 128, 128, 1))
    oi = 0
    for ti, (t0, n, isctx) in enumerate(tiles):
        x_t, x_d = xs[ti % 2]
        if ti + 1 < len(tiles):
            t1, n1, _ = tiles[ti + 1]
            nx_t, nx_d = xs[(ti + 1) % 2]
            fw.dma("sp", nx_t[:, :, 0:n1], xT[:, :, t1:t1 + n1], writes=[nx_d])
        ab_t, ab_d = (ab_c, ab_cd) if isctx else (ab_l, ab_ld)
        _norm_mod(fw, x_t, x_d, n, ones_bf, ones_d, ab_t[:, :, 0], ab_t[:, :, 1], ab_d, h_t, h_d,
                  sq_t, sq_d, rb_t, rb_d, tmp_t, tmp_d)
        for (c0, m, isg) in chunks:
            pt, pd = fw.ps()
            for kc in range(8):
                fw.mm(pt[0:m, 0:n], W[:, kc, c0:c0 + m], h_t[:, kc, 0:n], kc == 0, kc == 7,
                      [Wd[bb] for bb in range(c0 // 512, (c0 + m - 1) // 512 + 1)] + [h_d[kc]], pd)
            o_t, o_d = osb[oi % 4]; oi += 1
            if isg:
                fw.op("act", lambda e: e.activation(o_t[0:m, 0:n], pt[0:m, 0:n], AF.Sigmoid), reads=[pd], writes=[o_d])
                fw.dma("sp", gT[c0 - NMIX:c0 - NMIX + m, t0:t0 + n], o_t[0:m, 0:n], reads=[o_d])
            else:
                fw.op("dve", lambda e: e.tensor_copy(o_t[0:m, 0:n], pt[0:m, 0:n]), reads=[pd], writes=[o_d])
                fw.dma("sp", zT[c0:c0 + m, t0:t0 + n], o_t[0:m, 0:n], reads=[o_d])
    return fw.finish()


def _ctiles():
    return [(i * 256, 256, 0) for i in range(8)] + [(2048, 64, 1)]


def _rms_resid(fw, y_t, y_d, x_t, x_d, n, ones_bf, ones_d, cvec, c_d, sq_t, sq_d, rb_t, rb_d, tmp_t, tmp_d):
    fw.op("act", lambda e: e.activation(sq_t[:, :, 0:n], y_t[:, :, 0:n], AF.Square), reads=[y_d], writes=[sq_d])
    pt, pd = fw.ps()
    for kc in range(8):
        fw.mm(pt[:, 0:n], ones_bf[:], sq_t[:, kc, 0:n], kc == 0, kc == 7, [ones_d, sq_d], pd)
    fw.op("act", lambda e: e.activation(rb_t[:, 0:n], pt[:, 0:n], AF.Sqrt, bias=float(D * EPS)),
          reads=[pd], writes=[rb_d])
    fw.op("dve", lambda e: e.reciprocal(rb_t[:, 0:n], rb_t[:, 0:n]), reads=[rb_d], writes=[rb_d])
    for kc in range(8):
        fw.op("dve", lambda e: e.tensor_tensor(tmp_t[:, kc, 0:n], y_t[:, kc, 0:n], rb_t[:, 0:n], ALU.mult),
              reads=[y_d, rb_d], writes=[tmp_d[kc]])
        fw.op("dve", lambda e: e.scalar_tensor_tensor(x_t[:, kc, 0:n], tmp_t[:, kc, 0:n], cvec[:, kc:kc + 1],
                                                       x_t[:, kc, 0:n], ALU.mult, ALU.add),
              reads=[tmp_d[kc], c_d, x_d], writes=[x_d])


def build_C1():
    fw = FW(); nc = fw.nc
    TN = 256
    xT = fw.dram("xT", [128, 8, NT])
    gT = fw.dram("gT", [4096, NT])
    names = ["yna", "ymla", "gof", "gob", "gz", "s5f", "s5b", "s5u"]
    yin = {k: fw.dram(k, [256, NT]) for k in names}
    vecs = fw.dram("vecs", [128, 8, 4])
    v2 = fw.dram("v2", [128, 2, 4])
    w_branch = fw.dram("w_branch", [4, 256, D])
    w_out = fw.dram("w_out", [D, D])
    glu_w = fw.dram("glu_w", [256, 256])
    x1T = fw.dram("x1T", [128, 8, NT], kind="ExternalOutput")
    fw.psum_banks(8)
    WB, WB_d = fw.sb("WB", [128, 8, D], BF16)
    WO, WO_d = fw.sb("WO", [128, 8, D], BF16)
    WG, WG_d = fw.sb("WG", [128, 2, 256], BF16)
    vec_t, v_d = fw.sb("vec", [128, 8, 4])
    v2_t, v2_d = fw.sb("v2", [128, 2, 4])
    c_l, c_ld = fw.sb("c_l", [128, 8]); c_c, c_cd = fw.sb("c_c", [128, 8])
    gn8, gn8_d = fw.sb("gn8", [128, 2])
    ones_bf, ones_d = fw.sb("ones", [128, 128], BF16)
    bd_bf, bd_d = fw.sb("bd", [128, 128], BF16)
    x_t, x_d = fw.sb("x", [128, 8, TN])
    yb = {k: fw.sb("b_" + k, [128, 2, TN], BF16 if k in ("yna", "ymla") else F32) for k in names}
    t1, t1_d = fw.sb("t1", [128, 2, TN]); t2, t2_d = fw.sb("t2", [128, 2, TN]); t3, t3_d = fw.sb("t3", [128, 2, TN])
    sqs, sqs_d = fw.sb("sqs", [128, 2, TN], BF16)
    ygdn, ygdn_d = fw.sb("ygdn", [128, 2, TN], BF16)
    yg, yg_d = fw.sb("yg", [128, 2, TN]); ygb, ygb_d = fw.sb("ygb", [128, 2, TN], BF16)
    ys5, ys5_d = fw.sb("ys5", [128, 2, TN], BF16)
    gbuf = [fw.sb(f"g{i}", [128, TN]) for i in range(8)]
    acc, acc_d = fw.sb("acc", [128, TN]); tmpm, tmpm_d = fw.sb("tmpm", [128, TN])
    mrg, _ = fw.sb("mrg", [128, 8, TN], BF16); mrg_d = [Dep() for _ in range(8)]
    y_t, y_d = fw.sb("y", [128, 8, TN])
    sq_t, sq_d = fw.sb("sq", [128, 8, TN], BF16)
    rb_t, rb_d = fw.sb("rb", [128, TN])
    tmp_t, _ = fw.sb("tmp", [128, 8, TN]); tmp_d = [Dep() for _ in range(8)]

    fw.dma("sp", vec_t[:], vecs, writes=[v_d])
    fw.dma("sp", v2_t[:], v2, writes=[v2_d])
    for i in range(4):
        for kc in range(2):
            fw.dma("pool", WB[:, i * 2 + kc, :], w_branch[i, kc * 128:(kc + 1) * 128, :], writes=[WB_d])
    for kc in range(8):
        fw.dma("pool", WO[:, kc, :], w_out[kc * 128:(kc + 1) * 128, :], writes=[WO_d])
    for kc in range(2):
        fw.dma("pool", WG[:, kc, :], glu_w[kc * 128:(kc + 1) * 128, :], writes=[WG_d])
    fw.op("dve", lambda e: e.memset(ones_bf[:], 1.0), writes=[ones_d])
    fw.op("dve", lambda e: e.memset(bd_bf[:], 0.0), writes=[bd_d])
    fw.op("dve", lambda e: e.memset(bd_bf[0:64, 0:64], 1.0), writes=[bd_d])
    fw.op("dve", lambda e: e.memset(bd_bf[64:128, 64:128], 1.0), writes=[bd_d])
    for (cv, cd, gi) in ((c_l, c_ld, 1), (c_c, c_cd, 2)):
        fw.op("dve", lambda e: e.scalar_tensor_tensor(cv[:], vec_t[:, :, gi], 32.0, vec_t[:, :, 0], ALU.mult, ALU.mult),
              reads=[v_d], writes=[cd])
    fw.op("dve", lambda e: e.tensor_scalar(gn8[:], v2_t[:, :, 0], 8.0, None, ALU.mult), reads=[v2_d], writes=[gn8_d])
    C_TANH = 0.7978845608028654
    for (t0, n, isctx) in _ctiles():
        fw.dma("sp", x_t[:, :, 0:n], xT[:, :, t0:t0 + n], writes=[x_d])
        for k in names:
            bt, bdp = yb[k]
            src = yin[k].rearrange("(c p) t -> p c t", p=128)[:, :, t0:t0 + n]
            fw.dma("pool" if k in ("yna", "ymla") else "sp", bt[:, :, 0:n], src, writes=[bdp])
        fw.op("dve", lambda e: e.tensor_tensor(t1[:, :, 0:n], yb["gof"][0][:, :, 0:n], yb["gob"][0][:, :, 0:n], ALU.add),
              reads=[yb["gof"][1], yb["gob"][1]], writes=[t1_d])
        fw.op("act", lambda e: e.activation(sqs[:, :, 0:n], t1[:, :, 0:n], AF.Square), reads=[t1_d], writes=[sqs_d])
        fw.op("act", lambda e: e.activation(t3[:, :, 0:n], yb["gz"][0][:, :, 0:n], AF.Silu), reads=[yb["gz"][1]], writes=[t3_d])
        for c in range(2):
            pt, pd = fw.ps()
            fw.mm(pt[:, 0:n], bd_bf[:], sqs[:, c, 0:n], True, True, [bd_d, sqs_d], pd)
            fw.op("act", lambda e: e.activation(t2[:, c, 0:n], pt[:, 0:n], AF.Sqrt, bias=float(64 * EPS)),
                  reads=[pd], writes=[t2_d])
        fw.op("dve", lambda e: e.reciprocal(t2[:, :, 0:n], t2[:, :, 0:n]), reads=[t2_d], writes=[t2_d])
        fw.op("dve", lambda e: e.tensor_tensor(t1[:, :, 0:n], t1[:, :, 0:n], t2[:, :, 0:n], ALU.mult),
              reads=[t1_d, t2_d], writes=[t1_d])
        for c in range(2):
            fw.op("dve", lambda e: e.scalar_tensor_tensor(ygdn[:, c, 0:n], t1[:, c, 0:n], gn8[:, c:c + 1], t3[:, c, 0:n],
                                                           ALU.mult, ALU.mult),
                  reads=[t1_d, gn8_d, t3_d], writes=[ygdn_d])
        for c in range(2):
            fw.op("dve", lambda e: e.scalar_tensor_tensor(t1[:, c, 0:n], yb["s5u"][0][:, c, 0:n], v2_t[:, c, 1:2],
                                                           yb["s5f"][0][:, c, 0:n], ALU.mult, ALU.add),
                  reads=[yb["s5u"][1], yb["s5f"][1], v2_d], writes=[t1_d])
        fw.op("dve", lambda e: e.tensor_tensor(t1[:, :, 0:n], t1[:, :, 0:n], yb["s5b"][0][:, :, 0:n], ALU.add),
              reads=[t1_d, yb["s5b"][1]], writes=[t1_d])
        fw.op("dve", lambda e: e.tensor_tensor(t2[:, :, 0:n], t1[:, :, 0:n], t1[:, :, 0:n], ALU.mult), reads=[t1_d], writes=[t2_d])
        fw.op("dve", lambda e: e.tensor_scalar(t2[:, :, 0:n], t2[:, :, 0:n], 0.044715, 1.0, ALU.mult, ALU.add),
              reads=[t2_d], writes=[t2_d])
        fw.op("dve", lambda e: e.tensor_tensor(t2[:, :, 0:n], t2[:, :, 0:n], t1[:, :, 0:n], ALU.mult),
              reads=[t1_d, t2_d], writes=[t2_d])
        fw.op("act", lambda e: e.activation(t3[:, :, 0:n], t2[:, :, 0:n], AF.Tanh, scale=C_TANH), reads=[t2_d], writes=[t3_d])
        fw.op("dve", lambda e: e.tensor_scalar(t3[:, :, 0:n], t3[:, :, 0:n], 1.0, 0.5, ALU.add, ALU.mult),
              reads=[t3_d], writes=[t3_d])
        fw.op("dve", lambda e: e.tensor_tensor(yg[:, :, 0:n], t3[:, :, 0:n], t1[:, :, 0:n], ALU.mult),
              reads=[t3_d, t1_d], writes=[yg_d])
        fw.op("act", lambda e: e.activation(ygb[:, :, 0:n], yg[:, :, 0:n], AF.Copy), reads=[yg_d], writes=[ygb_d])
        for m in range(2):
            pt, pd = fw.ps()
            for kc in range(2):
                fw.mm(pt[:, 0:n], WG[:, kc, m * 128:(m + 1) * 128], ygb[:, kc, 0:n], kc == 0, kc == 1, [WG_d, ygb_d], pd)
            fw.op("act", lambda e: e.activation(t2[:, m, 0:n], pt[:, 0:n], AF.Sigmoid, bias=v2_t[:, m, 2:3]),
                  reads=[pd, v2_d], writes=[t2_d])
        fw.op("dve", lambda e: e.tensor_tensor(ys5[:, :, 0:n], yg[:, :, 0:n], t2[:, :, 0:n], ALU.mult),
              reads=[yg_d, t2_d], writes=[ys5_d])
        br = [(yb["yna"][0], yb["yna"][1]), (yb["ymla"][0], yb["ymla"][1]), (ygdn, ygdn_d), (ys5, ys5_d)]
        gi = 0
        for m in range(8):
            for i in range(4):
                g_t, g_d = gbuf[gi % 8]; gi += 1
                fw.dma("sp", g_t[:, 0:n], gT[i * 1024 + m * 128:i * 1024 + (m + 1) * 128, t0:t0 + n], writes=[g_d])
                pt, pd = fw.ps()
                for kc in range(2):
                    fw.mm(pt[:, 0:n], WB[:, i * 2 + kc, m * 128:(m + 1) * 128], br[i][0][:, kc, 0:n], kc == 0, kc == 1,
                          [WB_d, br[i][1]], pd)
                if i == 0:
                    fw.op("dve", lambda e: e.tensor_tensor(acc[:, 0:n], pt[:, 0:n], g_t[:, 0:n], ALU.mult),
                          reads=[pd, g_d], writes=[acc_d])
                else:
                    fw.op("dve", lambda e: e.tensor_tensor(tmpm[:, 0:n], pt[:, 0:n], g_t[:, 0:n], ALU.mult),
                          reads=[pd, g_d], writes=[tmpm_d])
                    if i < 3:
                        fw.op("dve", lambda e: e.tensor_tensor(acc[:, 0:n], acc[:, 0:n], tmpm[:, 0:n], ALU.add),
                              reads=[acc_d, tmpm_d], writes=[acc_d])
                    else:
                        fw.op("dve", lambda e: e.tensor_tensor(mrg[:, m, 0:n], acc[:, 0:n], tmpm[:, 0:n], ALU.add),
                              reads=[acc_d, tmpm_d], writes=[mrg_d[m]])
        for m in range(8):
            pt, pd = fw.ps()
            for kc in range(8):
                fw.mm(pt[:, 0:n], WO[:, kc, m * 128:(m + 1) * 128], mrg[:, kc, 0:n], kc == 0, kc == 7, [WO_d, mrg_d[kc]], pd)
            fw.op("act", lambda e: e.activation(y_t[:, m, 0:n], pt[:, 0:n], AF.Copy), reads=[pd], writes=[y_d])
        cv, cd = (c_c, c_cd) if isctx else (c_l, c_ld)
        _rms_resid(fw, y_t, y_d, x_t, x_d, n, ones_bf, ones_d, cv, cd, sq_t, sq_d, rb_t, rb_d, tmp_t, tmp_d)
        fw.dma("sp", x1T[:, :, t0:t0 + n], x_t[:, :, 0:n], reads=[x_d])
    return fw.finish()


def build_C2():
    fw = FW(); nc = fw.nc
    TN = 256
    xT = fw.dram("xT", [128, 8, NT])
    vecs = fw.dram("vecs", [128, 8, 8])
    w1 = fw.dram("w1", [D, 4096])
    w2 = fw.dram("w2", [4096, D])
    x2T = fw.dram("x2T", [128, 8, NT], kind="ExternalOutput")
    fw.psum_banks(8)
    W1, _ = fw.sb("W1", [128, 8, 4096], BF16); W1d = [Dep() for _ in range(8)]
    W2, _ = fw.sb("W2", [128, 32, D], BF16); W2d = [Dep() for _ in range(32)]
    vec_t, v_d = fw.sb("vec", [128, 8, 8])
    ab_l, ab_ld = fw.sb("ab_l", [128, 8, 2]); ab_c, ab_cd = fw.sb("ab_c", [128, 8, 2])
    c_l, c_ld = fw.sb("c_l", [128, 8]); c_c, c_cd = fw.sb("c_c", [128, 8])
    ones_bf, ones_d = fw.sb("ones", [128, 128], BF16)
    x_t, x_d = fw.sb("x", [128, 8, TN])
    sq_t, sq_d = fw.sb("sq", [128, 8, TN], BF16)
    rb_t, rb_d = fw.sb("rb", [128, TN])
    tmp_t, _ = fw.sb("tmp", [128, 8, TN]); tmp_d = [Dep() for _ in range(8)]
    h_t, _ = fw.sb("h", [128, 8, TN], BF16); h_d = [Dep() for _ in range(8)]
    hid, _ = fw.sb("hid", [128, 32, TN], BF16); hid_d = [Dep() for _ in range(32)]
    rbuf = [fw.sb(f"r{i}", [128, TN]) for i in range(2)]
    o_t, o_d = fw.sb("o", [128, 8, TN])

    fw.dma("sp", vec_t[:], vecs, writes=[v_d])
    w1_v = w1.rearrange("(kc p) c -> p kc c", p=128); w2_v = w2.rearrange("(kc p) c -> p kc c", p=128)
    W1d = [Dep() for _ in range(8)]; W2d = [Dep() for _ in range(8)]
    for bb in range(8):
        fw.dma("pool", W1[:, :, bb * 512:(bb + 1) * 512], w1_v[:, :, bb * 512:(bb + 1) * 512], writes=[W1d[bb]])
    for bb in range(8):
        fw.dma("pool", W2[:, :, bb * 128:(bb + 1) * 128], w2_v[:, :, bb * 128:(bb + 1) * 128], writes=[W2d[bb]])
    fw.op("dve", lambda e: e.memset(ones_bf[:], 1.0), writes=[ones_d])
    _prep_ab(fw, vec_t, v_d, 0, 1, 2, ab_l, ab_ld)
    _prep_ab(fw, vec_t, v_d, 0, 3, 4, ab_c, ab_cd)
    for (cv, cd, gi) in ((c_l, c_ld, 6), (c_c, c_cd, 7)):
        fw.op("dve", lambda e: e.scalar_tensor_tensor(cv[:], vec_t[:, :, gi], 32.0, vec_t[:, :, 5], ALU.mult, ALU.mult),
              reads=[v_d], writes=[cd])
    for (t0, n, isctx) in _ctiles():
        fw.dma("sp", x_t[:, :, 0:n], xT[:, :, t0:t0 + n], writes=[x_d])
        ab_t, ab_d = (ab_c, ab_cd) if isctx else (ab_l, ab_ld)
        _norm_mod(fw, x_t, x_d, n, ones_bf, ones_d, ab_t[:, :, 0], ab_t[:, :, 1], ab_d, h_t, h_d,
                  sq_t, sq_d, rb_t, rb_d, tmp_t, tmp_d)
        for m in range(32):
            pt, pd = fw.ps()
            for kc in range(8):
                fw.mm(pt[:, 0:n], W1[:, kc, m * 128:(m + 1) * 128], h_t[:, kc, 0:n], kc == 0, kc == 7, [W1d[m // 4], h_d[kc]], pd)
            r_t, r_d = rbuf[m % 2]
            fw.op("act", lambda e: e.activation(r_t[:, 0:n], pt[:, 0:n], AF.Relu), reads=[pd], writes=[r_d])
            fw.op("dve", lambda e: e.tensor_tensor(hid[:, m, 0:n], r_t[:, 0:n], r_t[:, 0:n], ALU.mult),
                  reads=[r_d], writes=[hid_d[m]])
        for m in range(8):
            pt, pd = fw.ps()
            for kc in range(32):
                fw.mm(pt[:, 0:n], W2[:, kc, m * 128:(m + 1) * 128], hid[:, kc, 0:n], kc == 0, kc == 31, [W2d[m], hid_d[kc]], pd)
            fw.op("act", lambda e: e.activation(o_t[:, m, 0:n], pt[:, 0:n], AF.Copy), reads=[pd], writes=[o_d])
        cv, cd = (c_c, c_cd) if isctx else (c_l, c_ld)
        _rms_resid(fw, o_t, o_d, x_t, x_d, n, ones_bf, ones_d, cv, cd, sq_t, sq_d, rb_t, rb_d, tmp_t, tmp_d)
        fw.dma("sp", x2T[:, :, t0:t0 + n], x_t[:, :, 0:n], reads=[x_d])
    return fw.finish()


TS = CTX + SEQ
NEG = -30000.0


def _attend(fw, QT, q_d, q0, nq, kd, KT, k_d, blocks, VA, v_d, scale, ident, id_d, ones_f, on_d,
            po_pair, pbufs, osb, osb_d, rec, rec_d, yo, yo_d, out_ap, cnt):
    po, po_d = po_pair
    nb = len(blocks)
    LA = 2
    inflight = {}
    for step in range(nb + LA):
        if step < nb:
            off, bias_ap, b_d = blocks[step]
            pt, pd = fw.ps()
            fw.mm(pt[:, 0:nq], KT[0:kd, off:off + 128], QT[0:kd, q0:q0 + nq], True, bias_ap is None, [k_d, q_d], pd)
            if bias_ap is not None:
                fw.mm(pt[:, 0:nq], ident[:], bias_ap, False, True, [id_d, b_d], pd)
            inflight[step] = (pt, pd)
        bi = step - LA
        if bi >= 0:
            off = blocks[bi][0]
            pt, pd = inflight.pop(bi)
            p_t, p_d = pbufs[cnt[0] % len(pbufs)]; cnt[0] += 1
            fw.op("act", lambda e: e.activation(p_t[:, 0:nq], pt[:, 0:nq], AF.Exp, scale=float(scale)), reads=[pd], writes=[p_d])
            fw.mm(po[0:65, 0:nq], VA[:, off // 128, 0:65], p_t[:, 0:nq], bi == 0, bi == nb - 1, [v_d, p_d], po_d)
    fw.op("act", lambda e: e.activation(osb[0:65, 0:nq], po[0:65, 0:nq], AF.Copy), reads=[po_d], writes=[osb_d])
    pt, pd = fw.ps()
    fw.mm(pt[0:64, 0:nq], ones_f[64:65, 0:64], osb[64:65, 0:nq], True, True, [on_d, osb_d], pd)
    fw.op("dve", lambda e: e.reciprocal(rec[0:64, 0:nq], pt[0:64, 0:nq]), reads=[pd], writes=[rec_d])
    fw.op("dve", lambda e: e.tensor_tensor(yo[0:64, 0:nq], osb[0:64, 0:nq], rec[0:64, 0:nq], ALU.mult),
          reads=[osb_d, rec_d], writes=[yo_d])
    fw.dma("sp", out_ap, yo[0:64, 0:nq], reads=[yo_d])


def build_BA():
    fw = FW(); nc = fw.nc
    na_q = fw.dram("na_q", [64, TS]); na_k = fw.dram("na_k", [64, TS]); na_v = fw.dram("na_v", [TS, 64])
    nbias = fw.dram("nbias", [24, 128, 512])
    cq = fw.dram("cq", [256, TS]); ckv = fw.dram("ckv", [128, TS])
    kr = fw.dram("kr", [96, TS]); krs = fw.dram("krs", [96, TS])
    cosF = fw.dram("cosF", [96, TS]); sinF = fw.dram("sinF", [96, TS])
    wuq = fw.dram("wuq", [256, 96]); wuqs = fw.dram("wuqs", [256, 96])
    wuk = fw.dram("wuk", [128, 64]); wuv = fw.dram("wuv", [128, 64])
    nrm = fw.dram("nrm", [128, 4])
    identd = fw.dram("ident", [128, 128])
    ynaT = fw.dram("ynaT", [64, TS], kind="ExternalOutput")
    ymlaT = fw.dram("ymlaT", [64, TS], kind="ExternalOutput")
    fw.psum_banks(6)
    po_pairs = [(nc.alloc_psum_tensor(f"po{i}", [128, 512], F32), Dep()) for i in range(2)]

    nqT, nq_d = fw.sb("nqT", [64, TS], BF16); nkT, nk_d = fw.sb("nkT", [64, TS], BF16)
    nVA, nv_d = fw.sb("nVA", [128, 66, 65], BF16)
    nB, nB_d = fw.sb("nB", [128, 24, 512], BF16)
    mQT, _ = fw.sb("mQT", [96, TS], BF16); mKT, _ = fw.sb("mKT", [96, TS], BF16)
    mVA, _ = fw.sb("mVA", [128, 66, 65], BF16)
    ident, id_d = fw.sb("identb", [128, 128], BF16)
    ones_f, on_d = fw.sb("ones_f", [128, 64])
    ones_bf, ones_d = fw.sb("ones", [128, 128], BF16)
    Wq, wq_d = fw.sb("Wq", [128, 2, 96], BF16); Wqs, wqs_d = fw.sb("Wqs", [128, 2, 96], BF16)
    Wk, wk_d = fw.sb("Wk", [128, 64], BF16); Wv, wv_d = fw.sb("Wv", [128, 64], BF16)
    nrm_t, nrm_d = fw.sb("nrm", [128, 4]); nrm2, nrm2_d = fw.sb("nrm2", [128, 4])
    stg, stg_d = fw.sb("stg", [64, 2112])
    cq_t, cq_d = fw.sb("cq", [128, 2, 512]); ck_t, ck_d = fw.sb("ck", [128, 512])
    kr_t, kr_d = fw.sb("kr", [96, 512]); krs_t, krs_d = fw.sb("krs", [96, 512])
    cs_t, cs_d = fw.sb("cs", [96, 512]); sn_t, sn_d = fw.sb("sn", [96, 512])
    sq_t, sq_d = fw.sb("sq", [128, 2, 512], BF16); sqk, sqk_d = fw.sb("sqk", [128, 512], BF16)
    rq, rq_d = fw.sb("rq", [128, 512]); rk, rk_d = fw.sb("rk", [128, 512])
    tq, tq_d = fw.sb("tq", [128, 2, 512]); tk, tk_d = fw.sb("tk", [128, 512])
    cqn, cqn_d = fw.sb("cqn", [128, 2, 512], BF16); ckn, ckn_d = fw.sb("ckn", [128, 512], BF16)
    ta, ta_d = fw.sb("ta", [96, 512]); tb, tb_d = fw.sb("tb", [96, 512])
    pbufs = [fw.sb(f"p{i}", [128, 512], BF16) for i in range(3)]
    osb, osb_d = fw.sb("osb", [128, 512]); rec, rec_d = fw.sb("rec", [64, 512]); yo, yo_d = fw.sb("yo", [64, 512])

    fw.dma("pool", ident[:], identd, writes=[id_d])
    fw.op("dve", lambda e: e.memset(ones_f[:], 1.0), writes=[on_d])
    fw.op("dve", lambda e: e.memset(ones_bf[:], 1.0), writes=[ones_d])
    fw.dma("sp", nrm_t[:], nrm, writes=[nrm_d])
    fw.op("dve", lambda e: e.tensor_scalar(nrm2[:, 0:2], nrm_t[:, 0:2], 16.0, None, ALU.mult), reads=[nrm_d], writes=[nrm2_d])
    fw.op("dve", lambda e: e.tensor_scalar(nrm2[:, 2:3], nrm_t[:, 2:3], float(np.sqrt(128.0)), None, ALU.mult),
          reads=[nrm_d, nrm2_d], writes=[nrm2_d])
    for kc in range(2):
        fw.dma("pool", Wq[:, kc, :], wuq[kc * 128:(kc + 1) * 128, :], writes=[wq_d])
        fw.dma("pool", Wqs[:, kc, :], wuqs[kc * 128:(kc + 1) * 128, :], writes=[wqs_d])
    fw.dma("pool", Wk[:], wuk, writes=[wk_d]); fw.dma("pool", Wv[:], wuv, writes=[wv_d])
    fw.dma("pool", nkT[:], na_k, writes=[nk_d])
    fw.dma("pool", nVA[:, :, 0:64], na_v.rearrange("(j p) d -> p j d", p=128), writes=[nv_d])
    fw.op("dve", lambda e: e.memset(nVA[:, :, 64:65], 1.0), reads=[], writes=[nv_d])
    mv_d = Dep()
    fw.op("dve", lambda e: e.memset(mVA[:, :, 64:65], 1.0), writes=[mv_d])
    for j in range(6):
        fw.dma("pool", nB[:, j * 4:(j + 1) * 4, :], nbias[j * 4:(j + 1) * 4].rearrange("s p q -> p s q"), writes=[nB_d])
    for j in range(4):
        fw.dma("sp", stg[:], na_q[:, j * 2112:(j + 1) * 2112], writes=[stg_d])
        fw.op("act", lambda e: e.activation(nqT[:, j * 2112:(j + 1) * 2112], stg[:], AF.Copy, scale=0.125),
              reads=[stg_d], writes=[nq_d])
    mq_d = Dep(); mk_d = Dep()
    tiles = [(0, 256)] + [(256 + 512 * i, 512) for i in range(16)]
    for (t0, n) in tiles:
        fw.dma("sp", cq_t[:, :, 0:n], cq.rearrange("(c p) t -> p c t", p=128)[:, :, t0:t0 + n], writes=[cq_d])
        fw.dma("sp", ck_t[:, 0:n], ckv[:, t0:t0 + n], writes=[ck_d])
        fw.dma("sp", kr_t[:, 0:n], kr[:, t0:t0 + n], writes=[kr_d])
        fw.dma("sp", krs_t[:, 0:n], krs[:, t0:t0 + n], writes=[krs_d])
        fw.dma("sp", cs_t[:, 0:n], cosF[:, t0:t0 + n], writes=[cs_d])
        fw.dma("sp", sn_t[:, 0:n], sinF[:, t0:t0 + n], writes=[sn_d])
        fw.op("act", lambda e: e.activation(sq_t[:, :, 0:n], cq_t[:, :, 0:n], AF.Square), reads=[cq_d], writes=[sq_d])
        pt, pd = fw.ps()
        for kc in range(2):
            fw.mm(pt[:, 0:n], ones_bf[:], sq_t[:, kc, 0:n], kc == 0, kc == 1, [ones_d, sq_d], pd)
        fw.op("act", lambda e: e.activation(rq[:, 0:n], pt[:, 0:n], AF.Sqrt, bias=float(256 * EPS)), reads=[pd], writes=[rq_d])
        fw.op("dve", lambda e: e.reciprocal(rq[:, 0:n], rq[:, 0:n]), reads=[rq_d], writes=[rq_d])
        for kc in range(2):
            fw.op("dve", lambda e: e.tensor_tensor(tq[:, kc, 0:n], cq_t[:, kc, 0:n], rq[:, 0:n], ALU.mult),
                  reads=[cq_d, rq_d], writes=[tq_d])
            fw.op("act", lambda e: e.activation(cqn[:, kc, 0:n], tq[:, kc, 0:n], AF.Copy, scale=nrm2[:, kc:kc + 1]),
                  reads=[tq_d, nrm2_d], writes=[cqn_d])
        p1, p1d = fw.ps(); p2, p2d = fw.ps()
        for kc in range(2):
            fw.mm(p1[0:96, 0:n], Wq[:, kc, :], cqn[:, kc, 0:n], kc == 0, kc == 1, [wq_d, cqn_d], p1d)
        for kc in range(2):
            fw.mm(p2[0:96, 0:n], Wqs[:, kc, :], cqn[:, kc, 0:n], kc == 0, kc == 1, [wqs_d, cqn_d], p2d)
        fw.op("dve", lambda e: e.tensor_tensor(ta[:, 0:n], p1[0:96, 0:n], cs_t[:, 0:n], ALU.mult), reads=[p1d, cs_d], writes=[ta_d])
        fw.op("dve", lambda e: e.tensor_tensor(tb[:, 0:n], p2[0:96, 0:n], sn_t[:, 0:n], ALU.mult), reads=[p2d, sn_d], writes=[tb_d])
        fw.op("dve", lambda e: e.tensor_tensor(mQT[:, t0:t0 + n], ta[:, 0:n], tb[:, 0:n], ALU.add), reads=[ta_d, tb_d], writes=[mq_d])
        fw.op("act", lambda e: e.activation(sqk[:, 0:n], ck_t[:, 0:n], AF.Square), reads=[ck_d], writes=[sqk_d])
        pt, pd = fw.ps()
        fw.mm(pt[:, 0:n], ones_bf[:], sqk[:, 0:n], True, True, [ones_d, sqk_d], pd)
        fw.op("act", lambda e: e.activation(rk[:, 0:n], pt[:, 0:n], AF.Sqrt, bias=float(128 * EPS)), reads=[pd], writes=[rk_d])
        fw.op("dve", lambda e: e.reciprocal(rk[:, 0:n], rk[:, 0:n]), reads=[rk_d], writes=[rk_d])
        fw.op("dve", lambda e: e.tensor_tensor(tk[:, 0:n], ck_t[:, 0:n], rk[:, 0:n], ALU.mult), reads=[ck_d, rk_d], writes=[tk_d])
        fw.op("act", lambda e: e.activation(ckn[:, 0:n], tk[:, 0:n], AF.Copy, scale=nrm2[:, 2:3]), reads=[tk_d, nrm2_d], writes=[ckn_d])
        pt, pd = fw.ps()
        fw.mm(pt[0:64, 0:n], Wk[:], ckn[:, 0:n], True, True, [wk_d, ckn_d], pd)
        fw.op("act", lambda e: e.activation(mKT[0:64, t0:t0 + n], pt[0:64, 0:n], AF.Copy), reads=[pd], writes=[mk_d])
        fw.op("dve", lambda e: e.tensor_tensor(ta[64:96, 0:n], kr_t[64:96, 0:n], cs_t[64:96, 0:n], ALU.mult),
              reads=[kr_d, cs_d, ta_d], writes=[ta_d])
        fw.op("dve", lambda e: e.tensor_tensor(tb[64:96, 0:n], krs_t[64:96, 0:n], sn_t[64:96, 0:n], ALU.mult),
              reads=[krs_d, sn_d, tb_d], writes=[tb_d])
        fw.op("dve", lambda e: e.tensor_tensor(mKT[64:96, t0:t0 + n], ta[64:96, 0:n], tb[64:96, 0:n], ALU.add),
              reads=[ta_d, tb_d], writes=[mk_d])
        for j in range(n // 128):
            pt, pd = fw.ps()
            fw.mm(pt[:, 0:64], ckn[:, j * 128:(j + 1) * 128], Wv[:], True, True, [ckn_d, wv_d], pd)
            fw.op("dve", lambda e: e.tensor_copy(mVA[:, t0 // 128 + j, 0:64], pt[:, 0:64]), reads=[pd], writes=[mv_d])
    cnt = [0]; ai = 0
    common = dict(ident=ident, id_d=id_d, ones_f=ones_f, on_d=on_d, pbufs=pbufs, osb=osb, osb_d=osb_d,
                  rec=rec, rec_d=rec_d, yo=yo, yo_d=yo_d, cnt=cnt)
    ctxb = [(0, None, None), (128, None, None)]
    _attend(fw, nqT, nq_d, 0, 256, 64, nkT, nk_d, ctxb, nVA, nv_d, 1.0, po_pair=po_pairs[0], out_ap=ynaT[:, 0:256], **common)
    for i in range(16):
        cls = 0 if i == 0 else (2 if i == 15 else 1)
        kr0 = 0 if i == 0 else (112 if i == 15 else 8 * i - 4)
        blocks = list(ctxb) + [(256 + kr0 * 64 + 128 * j, nB[:, cls * 8 + j, :], nB_d) for j in range(8)]
        q0 = 256 + 512 * i
        _attend(fw, nqT, nq_d, q0, 512, 64, nkT, nk_d, blocks, nVA, nv_d, 1.0, po_pair=po_pairs[(i + 1) % 2],
                out_ap=ynaT[:, q0:q0 + 512], **common)
    msc = 96.0 ** -0.5
    _attend(fw, mQT, mq_d, 0, 256, 96, mKT, mk_d, ctxb, mVA, mv_d, msc, po_pair=po_pairs[0], out_ap=ymlaT[:, 0:256], **common)
    allb = [(128 * j, None, None) for j in range(66)]
    for i in range(16):
        q0 = 256 + 512 * i
        _attend(fw, mQT, mq_d, q0, 512, 96, mKT, mk_d, allb, mVA, mv_d, msc, po_pair=po_pairs[(i + 1) % 2],
                out_ap=ymlaT[:, q0:q0 + 512], **common)
    return fw.finish()


S5L = 32
S5NC = TS // S5L
_KCOLS = np.array(list(range(0, 33)) + list(range(31, -1, -1)) + [32 * (2 ** j) for j in range(9)], np.float32)


def _bc(ap2, axis, count):
    P, n = ap2.shape
    shape = [P, n, count] if axis == 2 else [P, count, n]
    return ap2.unsqueeze(axis).broadcast_to(shape)


def build_BS():
    fw = FW(); nc = fw.nc
    PI = float(np.pi)
    spar = fw.dram("spar", [64, 8, 4])
    bpar = fw.dram("bpar", [64, 4, 2, 16])
    cpar = fw.dram("cpar", [64, 8, 2, 16])
    kvd = fw.dram("kv", [64, 74])
    maskd = fw.dram("mask", [4, 128, 512])
    i64d = fw.dram("i64", [64, 64])
    ud = fw.dram("u", [8, 512, S5NC])
    yd = fw.dram("y", [8, 512, S5NC], kind="ExternalOutput")
    fw.psum_banks(8)
    sp_t, sp_d = fw.sb("sp", [64, 8, 4]); bp_t, bp_d = fw.sb("bp", [64, 4, 2, 16]); cp_t, cp_d = fw.sb("cp", [64, 8, 2, 16])
    kv, kv_d = fw.sb("kv", [64, 74]); msk, msk_d = fw.sb("msk", [128, 4, 512]); i64, i64_d = fw.sb("i64", [64, 64])
    sc, sc_d = fw.sb("sc", [64, 8, 16])
    ang2, ang_d = fw.sb("ang2", [64, 2, 74]); r1, r1_d = fw.sb("r1", [64, 2, 74]); r2, r2_d = fw.sb("r2", [64, 2, 74])
    qi, qi_d = fw.sb("qi", [64, 2, 74], mybir.dt.int32)
    sc2, sn_d = fw.sb("sc2", [64, 2, 74]); cs_d = Dep(); sn = sc2[:, 0, :]; cs = sc2[:, 1, :]
    mP, mP_d = fw.sb("mP", [64, 74]); mN, mN_d = fw.sb("mN", [64, 32])
    A, A_d = fw.sb("A", [64, 74]); Bt, Bt_d = fw.sb("Bt", [64, 74]); Btn, Btn_d = fw.sb("Btn", [64, 74])
    aN, aN_d = fw.sb("aN", [64, 32]); bN, bN_d = fw.sb("bN", [64, 32])
    bb, bb_d = fw.sb("bb", [64, 2, 16])
    T1, T1_d = fw.sb("T1", [64, 512]); T2, T2_d = fw.sb("T2", [64, 512])
    Lk_re, Lk_re_d = fw.sb("Lk_re", [64, 512]); Lk_im, Lk_im_d = fw.sb("Lk_im", [64, 512])
    Lq_re, Lq_re_d = fw.sb("Lq_re", [64, 512]); Lq_in, Lq_in_d = fw.sb("Lq_in", [64, 512])
    Lg_re, Lg_re_d = fw.sb("Lg_re", [64, 512]); Lg_im, Lg_im_d = fw.sb("Lg_im", [64, 512])
    Mo_re, Mo_re_d = fw.sb("Mo_re", [64, 512], BF16); Mo_in, Mo_in_d = fw.sb("Mo_in", [64, 512], BF16)
    MI, MI_d = fw.sb("MI", [128, 4, 512], BF16); MG, MG_d = fw.sb("MG", [128, 4, 128], BF16)
    U, U_d = fw.sb("U", [128, 4, S5NC], BF16)
    X = [[fw.sb(f"X{i}{j}", [64, S5NC]) for j in range(2)] for i in range(2)]
    Xp, Xp_d = fw.sb("Xp", [64, 2, S5NC], BF16)
    Ysb, Ysb_d = fw.sb("Ysb", [128, 4, S5NC])

    for (t, d, src) in ((sp_t, sp_d, spar), (bp_t, bp_d, bpar), (cp_t, cp_d, cpar), (kv, kv_d, kvd), (i64, i64_d, i64d)):
        fw.dma("sp", t[:], src, writes=[d])
    fw.dma("sp", msk[:], maskd.rearrange("k p q -> p k q"), writes=[msk_d])
    V = lambda e, fn, r, w: fw.op(e, fn, reads=r, writes=w)
    S = lambda c: sc[:, :, c]
    V("dve", lambda e: e.tensor_scalar(S(0), sp_t[:, :, 0], -1e-4, None, ALU.min), [sp_d], [sc_d])
    V("act", lambda e: e.activation(S(1), sp_t[:, :, 2], AF.Exp), [sp_d, sc_d], [sc_d])
    V("dve", lambda e: e.tensor_tensor(S(2), S(0), S(1), ALU.mult), [sc_d], [sc_d])
    V("dve", lambda e: e.tensor_tensor(S(3), sp_t[:, :, 1], S(1), ALU.mult), [sp_d, sc_d], [sc_d])
    V("dve", lambda e: e.tensor_tensor(S(4), S(0), S(0), ALU.mult), [sc_d], [sc_d])
    V("dve", lambda e: e.tensor_tensor(S(10), sp_t[:, :, 1], sp_t[:, :, 1], ALU.mult), [sp_d, sc_d], [sc_d])
    V("dve", lambda e: e.tensor_tensor(S(4), S(4), S(10), ALU.add), [sc_d], [sc_d])
    V("dve", lambda e: e.reciprocal(S(4), S(4)), [sc_d], [sc_d])
    V("dve", lambda e: e.tensor_scalar(S(5), S(2), -1.0, None, ALU.mult), [sc_d], [sc_d])

    for gd in range(8):
        gl = gd % 4
        th = sc[:, gd, 3:4]; lrdt = sc[:, gd, 2:3]; nlrdt = sc[:, gd, 5:6]
        V("dve", lambda e: e.tensor_scalar(ang2[:, 0, :], kv[:], th, None, ALU.mult), [kv_d, sc_d], [ang_d])
        V("dve", lambda e: e.tensor_scalar(ang2[:, 1, :], ang2[:, 0, :], 0.5 * PI, None, ALU.add), [ang_d], [ang_d])
        V("dve", lambda e: e.tensor_scalar(r1[:], ang2[:], 1.0 / (2 * PI), None, ALU.mult), [ang_d], [r1_d])
        V("dve", lambda e: e.tensor_copy(qi[:], r1[:]), [r1_d], [qi_d])
        V("dve", lambda e: e.tensor_copy(r1[:], qi[:]), [qi_d], [r1_d])
        V("dve", lambda e: e.scalar_tensor_tensor(r2[:], r1[:], -2 * PI, ang2[:], ALU.mult, ALU.add), [r1_d, ang_d], [r2_d])
        V("dve", lambda e: e.tensor_scalar(r1[:], r2[:], PI, None, ALU.is_gt), [r2_d], [r1_d])
        V("dve", lambda e: e.scalar_tensor_tensor(r2[:], r1[:], -2 * PI, r2[:], ALU.mult, ALU.add), [r1_d, r2_d], [r2_d])
        V("dve", lambda e: e.tensor_scalar(r1[:], r2[:], -PI, None, ALU.is_lt), [r2_d], [r1_d])
        V("dve", lambda e: e.scalar_tensor_tensor(r2[:], r1[:], 2 * PI, r2[:], ALU.mult, ALU.add), [r1_d, r2_d], [r2_d])
        V("act", lambda e: e.activation(sc2[:], r2[:], AF.Sin), [r2_d], [sn_d, cs_d])
        V("act", lambda e: e.activation(mP[:], kv[:], AF.Exp, scale=lrdt), [kv_d, sc_d], [mP_d])
        V("act", lambda e: e.activation(mN[:], kv[:, 0:32], AF.Exp, scale=nlrdt), [kv_d, sc_d], [mN_d])
        V("dve", lambda e: e.tensor_tensor(A[:], mP[:], cs, ALU.mult), [mP_d, cs_d], [A_d])
        V("dve", lambda e: e.tensor_tensor(Bt[:], mP[:], sn, ALU.mult), [mP_d, sn_d], [Bt_d])
        V("dve", lambda e: e.tensor_scalar(Btn[:], Bt[:], -1.0, None, ALU.mult), [Bt_d], [Btn_d])
        V("dve", lambda e: e.tensor_tensor(aN[:], mN[:], sc2[:, 1, 0:32], ALU.mult), [mN_d, cs_d], [aN_d])
        V("dve", lambda e: e.scalar_tensor_tensor(bN[:], mN[:], -1.0, sc2[:, 0, 0:32], ALU.mult, ALU.mult), [mN_d, sn_d], [bN_d])
        lr = sc[:, gd, 0:1]; li = sp_t[:, gd, 1:2]; rden = sc[:, gd, 4:5]
        c6 = sc[:, gd, 6:7]; c7 = sc[:, gd, 7:8]; c8 = sc[:, gd, 8:9]; c9 = sc[:, gd, 9:10]
        c10 = sc[:, gd, 10:11]; c11 = sc[:, gd, 11:12]; c12 = sc[:, gd, 12:13]
        V("dve", lambda e: e.tensor_scalar(c6, A[:, 1:2], -1.0, None, ALU.add), [A_d, sc_d], [sc_d])
        V("dve", lambda e: e.tensor_copy(c7, Bt[:, 1:2]), [Bt_d, sc_d], [sc_d])
        V("dve", lambda e: e.tensor_tensor(c10, c6, lr, ALU.mult), [sc_d], [sc_d])
        V("dve", lambda e: e.tensor_tensor(c11, c7, li, ALU.mult), [sc_d, sp_d], [sc_d])
        V("dve", lambda e: e.tensor_tensor(c10, c10, c11, ALU.add), [sc_d], [sc_d])
        V("dve", lambda e: e.tensor_tensor(c8, c10, rden, ALU.mult), [sc_d], [sc_d])
        V("dve", lambda e: e.tensor_tensor(c10, c7, lr, ALU.mult), [sc_d], [sc_d])
        V("dve", lambda e: e.tensor_tensor(c11, c6, li, ALU.mult), [sc_d, sp_d], [sc_d])
        V("dve", lambda e: e.tensor_tensor(c10, c10, c11, ALU.subtract), [sc_d], [sc_d])
        V("dve", lambda e: e.tensor_tensor(c9, c10, rden, ALU.mult), [sc_d], [sc_d])
        V("dve", lambda e: e.tensor_scalar(c12, c9, -1.0, None, ALU.mult), [sc_d], [sc_d])
        bre = bp_t[:, gl, 0, :]; bim = bp_t[:, gl, 1, :]
        V("dve", lambda e: e.tensor_scalar(bb[:, 0, :], bre, c8, None, ALU.mult), [bp_d, sc_d], [bb_d])
        V("dve", lambda e: e.scalar_tensor_tensor(bb[:, 0, :], bim, c12, bb[:, 0, :], ALU.mult, ALU.add), [bp_d, sc_d, bb_d], [bb_d])
        V("dve", lambda e: e.tensor_scalar(bb[:, 1, :], bim, c8, None, ALU.mult), [bp_d, sc_d, bb_d], [bb_d])
        V("dve", lambda e: e.scalar_tensor_tensor(bb[:, 1, :], bre, c9, bb[:, 1, :], ALU.mult, ALU.add), [bp_d, sc_d, bb_d], [bb_d])
        cre = cp_t[:, gd, 0, :]; cim = cp_t[:, gd, 1, :]

        def outer(dst, dst_d, a1, a1d, v1, v1d, a2, a2d, v2, v2d, sub, negate=False):
            d3 = dst[:].rearrange("p (s c) -> p s c", c=16)
            t1 = T1[:].rearrange("p (s c) -> p s c", c=16); t2 = T2[:].rearrange("p (s c) -> p s c", c=16)
            V("dve", lambda e: e.tensor_tensor(t1, _bc(a1, 2, 16), _bc(v1, 1, 32), ALU.mult), [a1d, v1d], [T1_d])
            V("dve", lambda e: e.tensor_tensor(t2, _bc(a2, 2, 16), _bc(v2, 1, 32), ALU.mult), [a2d, v2d], [T2_d])
            if negate:
                V("dve", lambda e: e.scalar_tensor_tensor(d3, t1, -1.0, t2, ALU.mult, ALU.subtract), [T1_d, T2_d], [dst_d])
            else:
                V("dve", lambda e: e.tensor_tensor(d3, t1, t2, ALU.subtract if sub else ALU.add), [T1_d, T2_d], [dst_d])

        outer(Lk_re, Lk_re_d, aN[:], aN_d, bb[:, 0, :], bb_d, bN[:], bN_d, bb[:, 1, :], bb_d, True)
        outer(Lk_im, Lk_im_d, aN[:], aN_d, bb[:, 1, :], bb_d, bN[:], bN_d, bb[:, 0, :], bb_d, False)
        outer(Lq_re, Lq_re_d, A[:, 0:32], A_d, cre, cp_d, Bt[:, 0:32], Bt_d, cim, cp_d, True)
        outer(Lq_in, Lq_in_d, A[:, 0:32], A_d, cim, cp_d, Bt[:, 0:32], Bt_d, cre, cp_d, False, negate=True)
        outer(Lg_re, Lg_re_d, A[:, 33:65], A_d, bb[:, 0, :], bb_d, Bt[:, 33:65], Bt_d, bb[:, 1, :], bb_d, True)
        outer(Lg_im, Lg_im_d, A[:, 33:65], A_d, bb[:, 1, :], bb_d, Bt[:, 33:65], Bt_d, bb[:, 0, :], bb_d, False)
        outer(Mo_re, Mo_re_d, A[:, 1:33], A_d, cre, cp_d, Bt[:, 1:33], Bt_d, cim, cp_d, True)
        outer(Mo_in, Mo_in_d, A[:, 1:33], A_d, cim, cp_d, Bt[:, 1:33], Bt_d, cre, cp_d, False, negate=True)
        for ki in range(4):
            pt, pd = fw.ps()
            fw.mm(pt[:, :], Lk_re[:, ki * 128:(ki + 1) * 128], Lq_re[:], True, False, [Lk_re_d, Lq_re_d], pd)
            fw.mm(pt[:, :], Lk_im[:, ki * 128:(ki + 1) * 128], Lq_in[:], False, True, [Lk_im_d, Lq_in_d], pd)
            V("dve", lambda e: e.tensor_tensor(MI[:, ki, :], pt[:, :], msk[:, ki, :], ALU.mult), [pd, msk_d], [MI_d])
            pt, pd = fw.ps()
            fw.mm(pt[:, 0:64], Lg_re[:, ki * 128:(ki + 1) * 128], i64[:], True, True, [Lg_re_d, i64_d], pd)
            fw.mm(pt[:, 64:128], Lg_im[:, ki * 128:(ki + 1) * 128], i64[:], True, True, [Lg_im_d, i64_d], pd)
            V("act", lambda e: e.activation(MG[:, ki, :], pt[:, 0:128], AF.Copy), [pd], [MG_d])
        fw.dma("pool", U[:], ud[gd].rearrange("(k p) n -> p k n", p=128), writes=[U_d])
        (x0r, x0r_d), (x0i, x0i_d) = X[0]
        for ri, (xt, xd) in enumerate(X[0]):
            pt, pd = fw.ps()
            for ki in range(4):
                fw.mm(pt[0:64, 0:S5NC], MG[:, ki, ri * 64:(ri + 1) * 64], U[:, ki, :], ki == 0, ki == 3, [MG_d, U_d], pd)
            V("act", lambda e: e.activation(xt[:], pt[0:64, 0:S5NC], AF.Copy), [pd], [xd])
        cur = 0
        for j in range(9):
            sft = 2 ** j
            if sft >= S5NC:
                break
            lre = A[:, 65 + j:66 + j]; lim = Bt[:, 65 + j:66 + j]; limn = Btn[:, 65 + j:66 + j]
            (or_, or_d), (oi, oi_d) = X[cur]
            (nr, nr_d), (ni, ni_d) = X[1 - cur]
            V("dve", lambda e: e.tensor_copy(nr[:, 0:sft], or_[:, 0:sft]), [or_d], [nr_d])
            V("dve", lambda e: e.tensor_copy(ni[:, 0:sft], oi[:, 0:sft]), [oi_d], [ni_d])
            V("dve", lambda e: e.scalar_tensor_tensor(nr[:, sft:], or_[:, 0:S5NC - sft], lre, or_[:, sft:], ALU.mult, ALU.add),
              [or_d, A_d, nr_d], [nr_d])
            V("dve", lambda e: e.scalar_tensor_tensor(nr[:, sft:], oi[:, 0:S5NC - sft], limn, nr[:, sft:], ALU.mult, ALU.add),
              [oi_d, Btn_d, nr_d], [nr_d])
            V("dve", lambda e: e.scalar_tensor_tensor(ni[:, sft:], oi[:, 0:S5NC - sft], lre, oi[:, sft:], ALU.mult, ALU.add),
              [oi_d, A_d, ni_d], [ni_d])
            V("dve", lambda e: e.scalar_tensor_tensor(ni[:, sft:], or_[:, 0:S5NC - sft], lim, ni[:, sft:], ALU.mult, ALU.add),
              [or_d, Bt_d, ni_d], [ni_d])
            cur = 1 - cur
        (fr, fr_d), (fi, fi_d) = X[cur]
        V("dve", lambda e: e.memset(Xp[:, :, 0:1], 0.0), [], [Xp_d])
        V("dve", lambda e: e.tensor_copy(Xp[:, 0, 1:], fr[:, 0:S5NC - 1]), [fr_d, Xp_d], [Xp_d])
        V("dve", lambda e: e.tensor_copy(Xp[:, 1, 1:], fi[:, 0:S5NC - 1]), [fi_d, Xp_d], [Xp_d])
        for mo in range(4):
            pt, pd = fw.ps()
            for ki in range(mo + 1):
                fw.mm(pt[:, 0:S5NC], MI[:, ki, mo * 128:(mo + 1) * 128], U[:, ki, :], ki == 0, False, [MI_d, U_d], pd)
            fw.mm(pt[:, 0:S5NC], Mo_re[:, mo * 128:(mo + 1) * 128], Xp[:, 0, :], False, False, [Mo_re_d, Xp_d], pd)
            fw.mm(pt[:, 0:S5NC], Mo_in[:, mo * 128:(mo + 1) * 128], Xp[:, 1, :], False, True, [Mo_in_d, Xp_d], pd)
            V("act", lambda e: e.activation(Ysb[:, mo, :], pt[:, 0:S5NC], AF.Copy), [pd], [Ysb_d])
        fw.dma("sp", yd[gd].rearrange("(k p) n -> p k n", p=128), Ysb[:], reads=[Ysb_d])
    return fw.finish()


def _s5_mask():
    i = np.arange(512)
    s_ = i // 16
    return (s_[:, None] <= s_[None, :]).astype(np.float32).reshape(4, 128, 512)


def prep_BS(zs, l, inputs):
    maps = []
    mask = _s5_mask()
    kvt = np.ascontiguousarray(np.broadcast_to(_KCOLS[None, :], (64, 74)))
    for core in range(NCORE):
        b, h = core // 4, core % 4
        u = zs[b][:, 2224:2480]
        spar = np.zeros((64, 8, 4), np.float32); cpar = np.zeros((64, 8, 2, 16), np.float32)
        bpar = np.zeros((64, 4, 2, 16), np.float32)
        U = np.zeros((8, 512, S5NC), np.float32)
        for gl in range(4):
            g = 4 * h + gl
            bpar[:, gl, 0] = inputs['s5_b_re'][l][g]; bpar[:, gl, 1] = inputs['s5_b_im'][l][g]
            ug = u[:, g * 16:(g + 1) * 16]
            for d in range(2):
                gd = d * 4 + gl
                spar[:, gd, 0] = inputs['s5_a_re'][l][d, g]; spar[:, gd, 1] = inputs['s5_a_im'][l][d, g]
                spar[:, gd, 2] = inputs['s5_log_dt'][l][d, g]
                cpar[:, gd, 0] = inputs['s5_c_re'][l][d, g].T; cpar[:, gd, 1] = inputs['s5_c_im'][l][d, g].T
                useq = ug if d == 0 else np.concatenate([ug[:CTX][::-1], ug[CTX:][::-1]], 0)
                U[gd] = useq.reshape(S5NC, 512).T
        maps.append({"spar": spar, "bpar": bpar, "cpar": cpar, "kv": kvt, "mask": mask,
                     "i64": np.eye(64, dtype=np.float32), "u": U})
    return maps


def post_BS(res):
    out = []
    for b in range(2):
        yf = np.zeros((TS, 256), np.float32); yb = np.zeros((TS, 256), np.float32)
        for h in range(4):
            y = res[b * 4 + h]["y"]
            for gl in range(4):
                g = 4 * h + gl
                yf[:, g * 16:(g + 1) * 16] = y[gl].T.reshape(TS, 16)
                r = y[4 + gl].T.reshape(TS, 16)
                yb[:, g * 16:(g + 1) * 16] = np.concatenate([r[:CTX][::-1], r[CTX:][::-1]], 0)
        out.append((yf, yb))
    return out


GPAD = TS + 8
GNC = TS // 64


def _tokp(c):
    return 2 + 64 * c if c < 4 else 262 + 64 * (c - 4)


def build_BG():
    fw = FW(); nc = fw.nc
    rawd = fw.dram("raw", [2, 3, 64, GPAD])
    convd = fw.dram("convw", [64, 3, 4])
    abd = fw.dram("ab", [2, 2, 64, GNC])
    gpd = fw.dram("gpar", [64, 2, 2])
    trid = fw.dram("tri", [64, 64]); mud = fw.dram("maskU", [64, 64]); smd = fw.dram("smask", [64, 64]); i64d = fw.dram("i64", [64, 64])
    od = fw.dram("o", [2, TS, 64], kind="ExternalOutput")
    fw.psum_banks(6)
    po_pairs = [(nc.alloc_psum_tensor(f"po{i}", [128, 512], F32), Dep()) for i in range(2)]
    V = lambda e, fn, r, w: fw.op(e, fn, reads=r, writes=w)
    DD = (0, 1)

    cw, cw_d = fw.sb("cw", [64, 3, 4]); gp, gp_d = fw.sb("gp", [64, 2, 2]); gp2, gp2_d = fw.sb("gp2", [64, 2, 2])
    tri, tri_d = fw.sb("tri", [64, 64]); mU, mU_d = fw.sb("mU", [64, 64]); sm, sm_d = fw.sb("sm", [64, 64]); i64, i64_d = fw.sb("i64", [64, 64])
    ones, ones_d = fw.sb("ones", [64, 64])
    W = 512

    def g3(name):
        t, d = fw.sb(name, [64, W])
        return t, d, (lambda n=8, t=t: t[:, 0:n * 64].rearrange("p (c i) -> p c i", i=64))

    def per_dir(maker):
        return [maker(d) for d in DD]

    col = lambda nm: per_dir(lambda d: fw.sb(f"{nm}{d}", [64, GNC]))
    a_t = col("a_t"); b_t = col("b_t"); g_t = col("g_t"); ng_t = col("ng_t"); bet = col("bet"); nbet = col("nbet")
    gc = col("gc"); egc = col("egc"); etl = col("etl"); gl = col("gl")
    rawt = per_dir(lambda d: [fw.sb(f"rawt{d}{q}", [64, W + 4]) for q in range(3)])
    fq = per_dir(lambda d: [fw.sb(f"fq{d}{q}", [64, W]) for q in range(3)])
    sq = per_dir(lambda d: fw.sb(f"sq{d}", [64, W])); rn = per_dir(lambda d: fw.sb(f"rn{d}", [64, W]))
    G = lambda nm: per_dir(lambda d: g3(f"{nm}{d}"))
    Ktm = G("Ktm"); Vtm = G("Vtm"); Ke = G("Ke"); Kt = G("Kt"); gB = G("gB"); ngB = G("ngB")
    tE = G("tE"); DT = G("DT"); DTs = G("DTs"); eB = G("eB"); qd = G("qd"); QK = G("QK")
    P = per_dir(lambda d: [g3(f"P{d}{i}") for i in range(2)]); PT = per_dir(lambda d: [g3(f"PT{d}{i}") for i in range(2)])
    R = per_dir(lambda d: [g3(f"R{d}{i}") for i in range(2)])
    ub = G("ub"); kcT = G("kcT"); Osb = G("Osb")
    vn = per_dir(lambda d: [fw.sb(f"vn{d}{i}", [64, 64]) for i in range(2)])
    Sb = per_dir(lambda d: [fw.sb(f"S{d}{i}", [64, 64]) for i in range(2)])

    for (t, d, src) in ((cw, cw_d, convd), (gp, gp_d, gpd), (tri, tri_d, trid), (mU, mU_d, mud), (sm, sm_d, smd), (i64, i64_d, i64d)):
        fw.dma("sp", t[:], src, writes=[d])
    V("dve", lambda e: e.memset(ones[:], 1.0), [], [ones_d])
    V("act", lambda e: e.activation(gp2[:, :, 0], gp[:, :, 0], AF.Exp), [gp_d], [gp2_d])
    V("dve", lambda e: e.tensor_scalar(gp2[:, :, 0], gp2[:, :, 0], -1.0, None, ALU.mult), [gp2_d], [gp2_d])
    V("dve", lambda e: e.tensor_copy(gp2[:, :, 1], gp[:, :, 1]), [gp_d, gp2_d], [gp2_d])

    for d in DD:
        fw.dma("sp", a_t[d][0][:], abd[d, 0], writes=[a_t[d][1]]); fw.dma("sp", b_t[d][0][:], abd[d, 1], writes=[b_t[d][1]])
        V("act", lambda e: e.activation(g_t[d][0][:], a_t[d][0][:], AF.Exp, bias=gp2[:, d, 1:2]), [a_t[d][1], gp2_d], [g_t[d][1]])
        V("act", lambda e: e.activation(g_t[d][0][:], g_t[d][0][:], AF.Ln, bias=1.0), [g_t[d][1]], [g_t[d][1]])
        V("dve", lambda e: e.tensor_scalar(g_t[d][0][:], g_t[d][0][:], gp2[:, d, 0:1], None, ALU.mult), [g_t[d][1], gp2_d], [g_t[d][1]])
        V("dve", lambda e: e.tensor_scalar(ng_t[d][0][:], g_t[d][0][:], -1.0, None, ALU.mult), [g_t[d][1]], [ng_t[d][1]])
        V("act", lambda e: e.activation(bet[d][0][:], b_t[d][0][:], AF.Sigmoid), [b_t[d][1]], [bet[d][1]])
        V("dve", lambda e: e.tensor_scalar(nbet[d][0][:], bet[d][0][:], -1.0, None, ALU.mult), [bet[d][1]], [nbet[d][1]])
        pt, pd = fw.ps()
        fw.mm(pt[0:64, 0:GNC], tri[:], g_t[d][0][:], True, True, [tri_d, g_t[d][1]], pd)
        V("dve", lambda e: e.tensor_copy(gc[d][0][:], pt[0:64, 0:GNC]), [pd], [gc[d][1]])
        V("act", lambda e: e.activation(egc[d][0][:], pt[0:64, 0:GNC], AF.Exp), [pd], [egc[d][1]])
        pt, pd = fw.ps()
        fw.mm(pt[0:64, 0:GNC], ones[:], g_t[d][0][:], True, True, [ones_d, g_t[d][1]], pd)
        V("act", lambda e: e.activation(gl[d][0][:], pt[0:64, 0:GNC], AF.Exp), [pd], [gl[d][1]])
        V("dve", lambda e: e.tensor_tensor(etl[d][0][:], pt[0:64, 0:GNC], gc[d][0][:], ALU.subtract), [pd, gc[d][1]], [etl[d][1]])
        V("act", lambda e: e.activation(etl[d][0][:], etl[d][0][:], AF.Exp), [etl[d][1]], [etl[d][1]])
        V("dve", lambda e: e.memset(Sb[d][0][0][:], 0.0), [], [Sb[d][0][1]])

    s_cur = [0, 0]
    groups = [(0, 4)] + [(4 + 8 * i, 8) for i in range(16)]
    for gi, (c0, ncg) in enumerate(groups):
        p0 = _tokp(c0); w = ncg * 64
        cs = slice(c0, c0 + ncg)
        bc2 = lambda ap: ap.unsqueeze(2).broadcast_to([64, ncg, 64])
        bc1 = lambda ap: ap.unsqueeze(1).broadcast_to([64, ncg, 64])
        CH = [slice(c * 64, (c + 1) * 64) for c in range(ncg)]
        for d in DD:
            offs = [j - 2 for j in range(4)] if d == 0 else [2 - j for j in range(4)]
            for qi in range(3):
                r_t, r_d = rawt[d][qi]; f_t, f_d = fq[d][qi]
                fw.dma("sp", r_t[:, 0:w + 4], rawd[d, qi, :, p0 - 2:p0 + w + 2], writes=[r_d])
                eng = "dve"
                V(eng, lambda e: e.tensor_scalar(f_t[:, 0:w], r_t[:, 2 + offs[0]:2 + offs[0] + w], cw[:, qi, 0:1], None, ALU.mult),
                  [r_d, cw_d], [f_d])
                for j in range(1, 4):
                    V(eng, lambda e: e.scalar_tensor_tensor(f_t[:, 0:w], r_t[:, 2 + offs[j]:2 + offs[j] + w], cw[:, qi, j:j + 1],
                                                            f_t[:, 0:w], ALU.mult, ALU.add), [r_d, cw_d, f_d], [f_d])
                V("act", lambda e: e.activation(f_t[:, 0:w], f_t[:, 0:w], AF.Silu), [f_d], [f_d])
        for d in DD:
            for qi in range(2):
                f_t, f_d = fq[d][qi]; sq_t, sq_d = sq[d]; rn_t, rn_d = rn[d]
                V("act", lambda e: e.activation(sq_t[:, 0:w], f_t[:, 0:w], AF.Square), [f_d], [sq_d])
                pt, pd = fw.ps()
                fw.mm(pt[0:64, 0:w], ones[:], sq_t[:, 0:w], True, True, [ones_d, sq_d], pd)
                V("act", lambda e: e.activation(rn_t[:, 0:w], pt[0:64, 0:w], AF.Sqrt, bias=float(EPS)), [pd], [rn_d])
                V("dve", lambda e: e.reciprocal(rn_t[:, 0:w], rn_t[:, 0:w]), [rn_d], [rn_d])
                if qi == 0:
                    V("dve", lambda e: e.scalar_tensor_tensor(f_t[:, 0:w], f_t[:, 0:w], 0.125, rn_t[:, 0:w], ALU.mult, ALU.mult),
                      [f_d, rn_d], [f_d])
                else:
                    V("dve", lambda e: e.tensor_tensor(f_t[:, 0:w], f_t[:, 0:w], rn_t[:, 0:w], ALU.mult), [f_d, rn_d], [f_d])
        for d in DD:
            (qf, qf_d), (kf, kf_d), (vf, vf_d) = fq[d]
            for (src, src_d, (dst, dst_d, _)) in ((kf, kf_d, Ktm[d]), (vf, vf_d, Vtm[d])):
                pt, pd = fw.ps()
                for sl in CH:
                    fw.op("pe", lambda e: e.transpose(pt[0:64, sl], src[:, sl], i64[:]), reads=[src_d, i64_d], writes=[pd])
                V("act", lambda e: e.activation(dst[:, 0:w], pt[0:64, 0:w], AF.Copy), [pd], [dst_d])
        for d in DD:
            V("dve", lambda e: e.tensor_tensor(Ke[d][2](ncg), Ktm[d][2](ncg), bc2(egc[d][0][:, cs]), ALU.mult), [Ktm[d][1], egc[d][1]], [Ke[d][1]])
            V("dve", lambda e: e.tensor_tensor(Kt[d][2](ncg), Ktm[d][2](ncg), bc2(etl[d][0][:, cs]), ALU.mult), [Ktm[d][1], etl[d][1]], [Kt[d][1]])
            V("dve", lambda e: e.tensor_copy(gB[d][2](ncg), bc2(g_t[d][0][:, cs])), [g_t[d][1]], [gB[d][1]])
            V("dve", lambda e: e.tensor_copy(ngB[d][2](ncg), bc2(ng_t[d][0][:, cs])), [ng_t[d][1]], [ngB[d][1]])
        for d in DD:
            (qf, qf_d), (kf, kf_d), (vf, vf_d) = fq[d]
            pE, pEd = fw.ps(); pG, pGd = fw.ps()
            gB_t, gB_d, _ = gB[d]; ngB_t, ngB_d, _ = ngB[d]
            for sl in CH:
                fw.mm(pE[0:64, sl], gB_t[:, sl], tri[:], True, False, [gB_d, tri_d], pEd)
                fw.mm(pE[0:64, sl], tri[:], ngB_t[:, sl], False, True, [ngB_d, tri_d], pEd)
                fw.mm(pG[0:64, sl], gB_t[:, sl], tri[:], True, True, [gB_d, tri_d], pGd)
            pE3 = pE[0:64, 0:w].rearrange("p (c i) -> p c i", i=64)
            V("dve", lambda e: e.scalar_tensor_tensor(tE[d][2](ncg), pE3, 0.0, bc1(mU[:]), ALU.min, ALU.add), [pEd, mU_d], [tE[d][1]])
            V("act", lambda e: e.activation(DT[d][0][:, 0:w], tE[d][0][:, 0:w], AF.Exp), [tE[d][1]], [DT[d][1]])
            V("dve", lambda e: e.tensor_tensor(DTs[d][2](ncg), DT[d][2](ncg), bc1(sm[:]), ALU.mult), [DT[d][1], sm_d], [DTs[d][1]])
            V("act", lambda e: e.activation(eB[d][0][:, 0:w], pG[0:64, 0:w], AF.Exp), [pGd], [eB[d][1]])
            V("dve", lambda e: e.tensor_tensor(qd[d][0][:, 0:w], qf[:, 0:w], eB[d][0][:, 0:w], ALU.mult), [qf_d, eB[d][1]], [qd[d][1]])
        for d in DD:
            (qf, qf_d), (kf, kf_d), (vf, vf_d) = fq[d]
            pK, pKd = fw.ps(); pQ, pQd = fw.ps()
            for sl in CH:
                fw.mm(pK[0:64, sl], kf[:, sl], kf[:, sl], True, True, [kf_d], pKd)
                fw.mm(pQ[0:64, sl], kf[:, sl], qf[:, sl], True, True, [kf_d, qf_d], pQd)
            X_t, X_d, X3 = P[d][0]
            V("dve", lambda e: e.tensor_tensor(X_t[:, 0:w], pK[0:64, 0:w], DTs[d][0][:, 0:w], ALU.mult), [pKd, DTs[d][1]], [X_d])
            V("dve", lambda e: e.tensor_tensor(X3(ncg), X3(ncg), bc2(bet[d][0][:, cs]), ALU.mult), [X_d, bet[d][1]], [X_d])
            V("dve", lambda e: e.tensor_tensor(QK[d][0][:, 0:w], pQ[0:64, 0:w], DT[d][0][:, 0:w], ALU.mult), [pQd, DT[d][1]], [QK[d][1]])
        for d in DD:
            X_t, X_d, X3 = P[d][0]; XT_t, XT_d, _ = PT[d][0]; R_t, R_d, R3 = R[d][0]
            pt, pd = fw.ps()
            for sl in CH:
                fw.op("pe", lambda e: e.transpose(pt[0:64, sl], X_t[:, sl], i64[:]), reads=[X_d, i64_d], writes=[pd])
            V("act", lambda e: e.activation(XT_t[:, 0:w], pt[0:64, 0:w], AF.Copy), [pd], [XT_d])
            V("dve", lambda e: e.scalar_tensor_tensor(R3(ncg), X3(ncg), -1.0, bc1(i64[:]), ALU.mult, ALU.add), [X_d, i64_d], [R_d])
        cp = 0; cr = 0
        for k in range(1, 6):
            for d in DD:
                Pc, Pc_d, _ = P[d][cp]; PTc, PTc_d, _ = PT[d][cp]
                Pn, Pn_d, _ = P[d][1 - cp]; PTn, PTn_d, _ = PT[d][1 - cp]
                pa, pad_ = fw.ps()
                for sl in CH:
                    fw.mm(pa[0:64, sl], Pc[:, sl], PTc[:, sl], True, True, [Pc_d, PTc_d], pad_)
                V("act", lambda e: e.activation(PTn[:, 0:w], pa[0:64, 0:w], AF.Copy), [pad_], [PTn_d])
                if k < 5:
                    pb, pbd = fw.ps()
                    for sl in CH:
                        fw.mm(pb[0:64, sl], PTc[:, sl], Pc[:, sl], True, True, [Pc_d, PTc_d], pbd)
                    V("act", lambda e: e.activation(Pn[:, 0:w], pb[0:64, 0:w], AF.Copy), [pbd], [Pn_d])
            for d in DD:
                PTn, PTn_d, _ = PT[d][1 - cp]
                Rc, Rc_d, _ = R[d][cr]; Rn, Rn_d, _ = R[d][1 - cr]
                pc_, pcd = fw.ps()
                for sl in CH:
                    fw.mm(pc_[0:64, sl], PTn[:, sl], Rc[:, sl], True, True, [PTn_d, Rc_d], pcd)
                V("dve", lambda e: e.tensor_tensor(Rn[:, 0:w], Rc[:, 0:w], pc_[0:64, 0:w], ALU.add), [Rc_d, pcd], [Rn_d])
            cp = 1 - cp; cr = 1 - cr
        for d in DD:
            ZT, ZT_d, _ = R[d][cr]
            pU, pUd = fw.ps(); pC, pCd = fw.ps()
            for sl in CH:
                fw.mm(pU[0:64, sl], ZT[:, sl], Vtm[d][0][:, sl], True, True, [ZT_d, Vtm[d][1]], pUd)
                fw.mm(pC[0:64, sl], Ke[d][0][:, sl], ZT[:, sl], True, True, [ZT_d, Ke[d][1]], pCd)
            pU3 = pU[0:64, 0:w].rearrange("p (c i) -> p c i", i=64)
            V("dve", lambda e: e.tensor_tensor(ub[d][2](ncg), pU3, bc2(bet[d][0][:, cs]), ALU.mult), [pUd, bet[d][1]], [ub[d][1]])
            V("act", lambda e: e.activation(kcT[d][0][:, 0:w], pC[0:64, 0:w], AF.Copy), [pCd], [kcT[d][1]])
        for c in range(ncg):
            sl = CH[c]; cc = c0 + c
            pas = []; pbs = []
            for d in DD:
                S_t, S_d = Sb[d][s_cur[d]]
                pa, pad_ = fw.ps(); pas.append((pa, pad_))
                fw.mm(pa[0:64, 0:64], kcT[d][0][:, sl], S_t[:], True, True, [kcT[d][1], S_d], pad_)
            for d in DD:
                vn_t, vn_d = vn[d][c % 2]; pa, pad_ = pas[d]
                V("dve", lambda e: e.scalar_tensor_tensor(vn_t[:], pa[0:64, 0:64], nbet[d][0][:, cc:cc + 1], ub[d][0][:, sl], ALU.mult, ALU.add),
                  [pad_, nbet[d][1], ub[d][1]], [vn_d])
            for d in DD:
                S_t, S_d = Sb[d][s_cur[d]]; vn_t, vn_d = vn[d][c % 2]; po, po_d = po_pairs[d]
                pb, pbd = fw.ps(); pbs.append((pb, pbd))
                fw.mm(pb[0:64, 0:64], Kt[d][0][:, sl], vn_t[:], True, True, [Kt[d][1], vn_d], pbd)
                fw.mm(po[0:64, sl], qd[d][0][:, sl], S_t[:], True, False, [qd[d][1], S_d], po_d)
                fw.mm(po[0:64, sl], QK[d][0][:, sl], vn_t[:], False, True, [QK[d][1], vn_d], po_d)
            for d in DD:
                S_t, S_d = Sb[d][s_cur[d]]; Sn_t, Sn_d = Sb[d][1 - s_cur[d]]; pb, pbd = pbs[d]
                V("dve", lambda e: e.scalar_tensor_tensor(Sn_t[:], S_t[:], gl[d][0][:, cc:cc + 1], pb[0:64, 0:64], ALU.mult, ALU.add),
                  [S_d, gl[d][1], pbd], [Sn_d])
                s_cur[d] = 1 - s_cur[d]
        for d in DD:
            po, po_d = po_pairs[d]
            V("act", lambda e: e.activation(Osb[d][0][:, 0:w], po[0:64, 0:w], AF.Copy), [po_d], [Osb[d][1]])
            fw.dma("sp", od[d, c0 * 64:(c0 + ncg) * 64, :].rearrange("(c i) e -> i c e", i=64), Osb[d][2](ncg), reads=[Osb[d][1]])
    return fw.finish()


def prep_BG(zs, l, inputs):
    maps = []
    ii = np.arange(64)
    tri = (ii[:, None] <= ii[None, :]).astype(np.float32)
    maskU = np.where(ii[None, :] >= ii[:, None], 0.0, NEG).astype(np.float32)
    smask = (ii[None, :] > ii[:, None]).astype(np.float32)
    for core in range(NCORE):
        b, h = core // 4, core % 4
        z = zs[b]
        raw = np.zeros((2, 3, 64, GPAD), np.float32)
        ab = np.zeros((2, 2, 64, GNC), np.float32)
        convw = np.zeros((64, 3, 4), np.float32)
        gpar = np.zeros((64, 2, 2), np.float32)
        for qi in range(3):
            cols = slice(1184 + qi * 256 + h * 64, 1184 + qi * 256 + (h + 1) * 64)
            x = z[:, cols]
            convw[:, qi, :] = inputs['gdn_conv'][l][:, qi * 256 + h * 64:qi * 256 + (h + 1) * 64].T
            for d in range(2):
                xc, xl = x[:CTX], x[CTX:]
                if d == 1:
                    xc, xl = xc[::-1], xl[::-1]
                raw[d, qi, :, 2:258] = xc.T; raw[d, qi, :, 262:262 + SEQ] = xl.T
        for d in range(2):
            for wi, c0 in enumerate((2208, 2216)):
                v = z[:, c0 + d * 4 + h]
                if d == 1:
                    v = np.concatenate([v[:CTX][::-1], v[CTX:][::-1]])
                ab[d, wi] = v.reshape(GNC, 64).T
            gpar[:, d, 0] = inputs['gdn_a_log'][l][d, h]; gpar[:, d, 1] = inputs['gdn_dt_bias'][l][d, h]
        maps.append({"raw": raw, "convw": convw, "ab": ab, "gpar": gpar, "tri": tri, "maskU": maskU, "smask": smask,
                     "i64": np.eye(64, dtype=np.float32)})
    return maps


def post_BG(res):
    out = []
    for b in range(2):
        of = np.zeros((TS, 256), np.float32); ob = np.zeros((TS, 256), np.float32)
        for h in range(4):
            o = res[b * 4 + h]["o"]
            of[:, h * 64:(h + 1) * 64] = o[0]
            ob[:, h * 64:(h + 1) * 64] = np.concatenate([o[1][:CTX][::-1], o[1][CTX:][::-1]], 0)
        out.append((of, ob))
    return out


_NC_CACHE = {}


def _get(name, builder):
    if name not in _NC_CACHE:
        _NC_CACHE[name] = builder()
    return _NC_CACHE[name]


def _run(name, builder, in_maps):
    nc = _get(name, builder)
    res = run_bass_kernel_spmd(nc, in_maps, core_ids=list(range(NCORE)))
    return res.results


def fm(v):
    v = np.asarray(v, np.float32)
    lead = v.shape[:-1]
    a = v.reshape(lead + (v.shape[-1] // 128, 128))
    a = np.moveaxis(a, -1, 0)
    a = np.moveaxis(a, -1, 1)
    return np.ascontiguousarray(a)


_ROPE_PERM = np.array(list(range(8, 16)) + list(range(0, 8)) + list(range(24, 32)) + list(range(16, 24)))


def _rope_tables():
    t = np.arange(SEQ)
    rows = (t // 64).astype(np.float32); cols = (t % 64).astype(np.float32)
    inv = (10000.0 ** (-np.arange(8, dtype=np.float32) / 8)).astype(np.float32)
    ar = rows[None, :] * inv[:, None]; ac = cols[None, :] * inv[:, None]
    cosF = np.ones((96, TS), np.float32); sinF = np.zeros((96, TS), np.float32)
    for base, a in ((64, ar), (80, ac)):
        cosF[base:base + 8, CTX:] = np.cos(a); cosF[base + 8:base + 16, CTX:] = np.cos(a)
        sinF[base:base + 8, CTX:] = -np.sin(a); sinF[base + 8:base + 16, CTX:] = np.sin(a)
    return cosF, sinF


def _na_bias(rpb_h):
    out = np.full((24, 128, 512), NEG, np.float32)
    kk = np.arange(128); kr_l = kk // 64; kc = kk % 64
    qq = np.arange(512); qr_l = qq // 64; qc = qq % 64
    for cls, (r0, kr0) in enumerate(((0, 0), (8, 4), (120, 112))):
        r = r0 + qr_l
        rs = np.clip(r - 4, 0, 120)
        cs = np.clip(qc - 8, 0, 48)
        for j in range(8):
            kr = kr0 + 2 * j + kr_l
            okr = (kr[:, None] >= rs[None, :]) & (kr[:, None] < rs[None, :] + 8)
            okc = (kc[:, None] >= cs[None, :]) & (kc[:, None] < cs[None, :] + 16)
            di = np.clip(kr[:, None] - r[None, :] + 7, 0, 14)
            dj = np.clip(kc[:, None] - qc[None, :] + 15, 0, 30)
            vals = rpb_h[di, dj]
            out[cls * 8 + j] = np.where(okr & okc, vals, np.float32(NEG))
    return out


def prep_BA(zs, l, inputs, consts):
    cosF, sinF = consts
    maps = []
    T = lambda a: np.ascontiguousarray(a.T)
    for core in range(NCORE):
        b, h = core // 4, core % 4
        z = zs[b]
        krope = z[:, 1152:1184]
        kr = np.zeros((96, TS), np.float32); krs = np.zeros((96, TS), np.float32)
        kr[64:96] = krope.T; krs[64:96] = krope[:, _ROPE_PERM].T
        wuq = np.ascontiguousarray(inputs['mla_w_uq'][l][:, h * 96:(h + 1) * 96])
        wuqs = wuq.copy(); wuqs[:, 64:96] = wuq[:, 64 + _ROPE_PERM]
        wkv = inputs['mla_w_ukv'][l]
        nrm = np.zeros((128, 4), np.float32)
        nrm[:, 0] = inputs['mla_q_norm'][l][0:128]; nrm[:, 1] = inputs['mla_q_norm'][l][128:256]
        nrm[:, 2] = inputs['mla_kv_norm'][l]
        maps.append({
            "na_q": T(z[:, h * 64:(h + 1) * 64]), "na_k": T(z[:, 256 + h * 64:256 + (h + 1) * 64]),
            "na_v": np.ascontiguousarray(z[:, 512 + h * 64:512 + (h + 1) * 64]),
            "nbias": _na_bias(inputs['na_rpb'][l][h]),
            "cq": T(z[:, 768:1024]), "ckv": T(z[:, 1024:1152]), "kr": kr, "krs": krs, "cosF": cosF, "sinF": sinF,
            "wuq": wuq, "wuqs": wuqs,
            "wuk": np.ascontiguousarray(wkv[:, h * 128:h * 128 + 64]),
            "wuv": np.ascontiguousarray(wkv[:, h * 128 + 64:(h + 1) * 128]),
            "nrm": nrm, "ident": np.eye(128, dtype=np.float32)})
    return maps


def run_M(c, c_ctx, ada_w, ada_b):
    cs = np.stack([c[0], c[1], c_ctx, c_ctx], axis=-1)
    cT = np.ascontiguousarray(cs.reshape(8, 128, 4).transpose(1, 0, 2))
    maps = []
    for core in range(NCORE):
        l, half = core // 2, core % 2
        sl = slice(half * 3072, (half + 1) * 3072)
        maps.append({"cT": cT, "aw": np.ascontiguousarray(ada_w[l][:, sl]),
                     "ab": np.ascontiguousarray(ada_b[l][sl].reshape(24, 128).T)})
    res = _run("M", build_M, maps)
    mod = np.zeros((DEPTH, 6144, 4), np.float32)
    for core in range(NCORE):
        l, half = core // 2, core % 2
        o = res[core]["modT"]
        mod[l, half * 3072:(half + 1) * 3072] = o.transpose(1, 0, 2).reshape(3072, 4)
    return mod[..., :3]


def _tok_shard_T(per_batch):
    out = []
    for core in range(NCORE):
        b, q = core // 4, core % 4
        a = per_batch[b]
        blk = np.concatenate([a[CTX + q * 2048:CTX + (q + 1) * 2048], a[q * 64:(q + 1) * 64]], 0)
        out.append(np.ascontiguousarray(blk.T))
    return out


def _fmv(v):
    F, n = v.shape
    return np.ascontiguousarray(v.reshape(F // 128, 128, n).transpose(1, 0, 2))


def kernel(**inputs):
    import time
    inputs = {k: np.asarray(v, np.float32) for k, v in inputs.items()}
    T0 = time.time()
    mod = run_M(inputs['c'], inputs['c_ctx'], inputs['ada_w'], inputs['ada_b'])
    consts = _rope_tables()
    xT = []
    for core in range(NCORE):
        b, q = core // 4, core % 4
        blk = np.concatenate([inputs['x'][b, q * 2048:(q + 1) * 2048], inputs['ctx'][b, q * 64:(q + 1) * 64]], 0)
        xT.append(np.ascontiguousarray(blk.T.reshape(8, 128, NT).transpose(1, 0, 2)))
    for l in range(DEPTH):
        ng = inputs['norm_gains'][l]; m = mod[l]
        seg = lambda i, col: m[i * 1024:(i + 1) * 1024, col]
        maps = []
        for core in range(NCORE):
            b = core // 4
            vec = np.zeros((1024, 8), np.float32)
            vec[:, 0] = ng[0]; vec[:, 1] = seg(1, b); vec[:, 2] = seg(0, b); vec[:, 3] = seg(1, 2); vec[:, 4] = seg(0, 2)
            maps.append({"xT": xT[core], "vecs": _fmv(vec), "w_in": inputs['w_in'][l]})
        rA = _run("A", build_A, maps)
        zs = []
        for b in range(2):
            lat = np.concatenate([rA[b * 4 + q]["zT"][:, :2048].T for q in range(4)], 0)
            cx = np.concatenate([rA[b * 4 + q]["zT"][:, 2048:].T for q in range(4)], 0)
            zs.append(np.concatenate([cx, lat], 0))
        rBA = _run("BA", build_BA, prep_BA(zs, l, inputs, consts))
        rBS = _run("BS", build_BS, prep_BS(zs, l, inputs))
        rBG = _run("BG", build_BG, prep_BG(zs, l, inputs))
        yna = [np.concatenate([rBA[b * 4 + h]["ynaT"].T for h in range(4)], 1) for b in range(2)]
        ymla = [np.concatenate([rBA[b * 4 + h]["ymlaT"].T for h in range(4)], 1) for b in range(2)]
        s5 = post_BS(rBS); gd = post_BG(rBG)
        sh = {"yna": _tok_shard_T(yna), "ymla": _tok_shard_T(ymla),
              "gof": _tok_shard_T([g[0] for g in gd]), "gob": _tok_shard_T([g[1] for g in gd]),
              "gz": _tok_shard_T([z[:, 1952:2208] for z in zs]),
              "s5f": _tok_shard_T([y[0] for y in s5]), "s5b": _tok_shard_T([y[1] for y in s5]),
              "s5u": _tok_shard_T([z[:, 2224:2480] for z in zs])}
        maps = []
        for core in range(NCORE):
            b = core // 4
            vec = np.zeros((1024, 4), np.float32); vec[:, 0] = ng[1]; vec[:, 1] = seg(2, b); vec[:, 2] = seg(2, 2)
            v2 = np.zeros((256, 4), np.float32)
            v2[:, 0] = np.tile(inputs['gdn_norm'][l], 4); v2[:, 1] = inputs['s5_d'][l]; v2[:, 2] = inputs['s5_glu_b'][l]
            d = {"xT": xT[core], "gT": rA[core]["gT"], "vecs": _fmv(vec), "v2": _fmv(v2),
                 "w_branch": inputs['w_branch'][l], "w_out": inputs['w_out'][l], "glu_w": inputs['s5_glu_w'][l]}
            for k in sh:
                d[k] = sh[k][core]
            maps.append(d)
        rC1 = _run("C1", build_C1, maps)
        maps = []
        for core in range(NCORE):
            b = core // 4
            vec = np.zeros((1024, 8), np.float32)
            vec[:, 0] = ng[2]; vec[:, 1] = seg(4, b); vec[:, 2] = seg(3, b); vec[:, 3] = seg(4, 2); vec[:, 4] = seg(3, 2)
            vec[:, 5] = ng[3]; vec[:, 6] = seg(5, b); vec[:, 7] = seg(5, 2)
            maps.append({"xT": rC1[core]["x1T"], "vecs": _fmv(vec), "w1": inputs['mlp_w1'][l], "w2": inputs['mlp_w2'][l]})
        rC2 = _run("C2", build_C2, maps)
        xT = [rC2[core]["x2T"] for core in range(NCORE)]
        print(f"[kernel] layer {l} done at {time.time() - T0:.0f}s", flush=True)
    out = np.zeros((2, SEQ, D), np.float32)
    for core in range(NCORE):
        b, q = core // 4, core % 4
        out[b, q * 2048:(q + 1) * 2048] = xT[core].transpose(2, 1, 0).reshape(NT, D)[:2048]
    return out
```
